# Optimizing a Trainium2 kernel written in Bass

```python
import math
import jax
import jax.numpy as jnp
from jax import lax
import numpy as np

D_MODEL = 1024
BATCH = 4
SEQ = 8192
DEPTH = 2

GRID_W = 64
CTX_LEN = 256
POOL_GROUPS = 4
POOL_GROUP_DIM = 128
POOL_WIDTH = POOL_GROUPS * POOL_GROUP_DIM
POOL_WINDOWS = (2, 4, 8, 16)
DIFF_HEADS = 4
DIFF_HEAD_DIM = 64
DIFF_WIDTH = DIFF_HEADS * 2 * DIFF_HEAD_DIM
GQA_HEADS = 8
GQA_KV_HEADS = 2
GQA_HEAD_DIM = 64
GQA_GROUP = GQA_HEADS // GQA_KV_HEADS
GQA_WIDTH = GQA_HEADS * GQA_HEAD_DIM
GQA_KV_WIDTH = GQA_KV_HEADS * GQA_HEAD_DIM
N_BRANCHES = 3
BRANCH_WIDTH = 512
IN_SPLITS = (POOL_WIDTH, POOL_WIDTH, DIFF_WIDTH, DIFF_WIDTH, DIFF_WIDTH, DIFF_WIDTH,
             GQA_WIDTH, GQA_KV_WIDTH, GQA_KV_WIDTH, GQA_WIDTH, N_BRANCHES * D_MODEL)
IN_WIDTH = 2 * POOL_WIDTH + 4 * DIFF_WIDTH + 2 * GQA_WIDTH + 2 * GQA_KV_WIDTH + N_BRANCHES * D_MODEL
ROPE_AXIS_DIM = 32
ROPE_BASE = 10000.0
Q_BLOCK = 128
NORM_EPS = 1e-6

kernel_name = "hybrid_pool_diffattn_gqa_diffusion_block"


def _rmsnorm(x, g):
    x32 = x.astype(jnp.float32)
    y = x32 * lax.rsqrt(jnp.mean(x32 * x32, axis=-1, keepdims=True) + NORM_EPS)
    return (y * g.astype(jnp.float32)).astype(x.dtype)


def _rope_tables(n):
    rows = n // GRID_W
    row = jnp.repeat(jnp.arange(rows, dtype=jnp.float32), GRID_W)
    col = jnp.tile(jnp.arange(GRID_W, dtype=jnp.float32), rows)
    inv = jnp.power(ROPE_BASE, -jnp.arange(0, ROPE_AXIS_DIM, 2, dtype=jnp.float32) / ROPE_AXIS_DIM)
    ang_r = row[:, None] * inv[None, :]
    ang_c = col[:, None] * inv[None, :]
    return (jnp.cos(ang_r), jnp.sin(ang_r), jnp.cos(ang_c), jnp.sin(ang_c))


def _rotate(x, cos, sin):
    m = x.shape[-1] // 2
    x1 = x[..., :m].astype(jnp.float32)
    x2 = x[..., m:].astype(jnp.float32)
    cos = cos[:, None, :]
    sin = sin[:, None, :]
    return jnp.concatenate([x1 * cos - x2 * sin, x1 * sin + x2 * cos], axis=-1).astype(x.dtype)


def _rope_2d(x, tables):
    cr, sr, cc, sc = tables
    return jnp.concatenate([_rotate(x[..., :ROPE_AXIS_DIM], cr, sr),
                            _rotate(x[..., ROPE_AXIS_DIM:], cc, sc)], axis=-1)


def _split_proj(p):
    out = []
    o = 0
    for w in IN_SPLITS:
        out.append(p[..., o:o + w])
        o += w
    return out


def _centred_mean_minus_self(u, window):
    n = u.shape[1]
    left = window // 2
    right = window - 1 - left
    u32 = u.astype(jnp.float32)
    cs = jnp.pad(jnp.cumsum(u32, axis=1), ((0, 0), (1, 0), (0, 0)))
    t = jnp.arange(n)
    hi = jnp.minimum(t + right + 1, n)
    lo = jnp.maximum(t - left, 0)
    s = jnp.take(cs, hi, axis=1) - jnp.take(cs, lo, axis=1)
    cnt = (hi - lo).astype(jnp.float32)[None, :, None]
    return (s / cnt - u32).astype(u.dtype)


def _pool_mixer(u, pool_w, pool_scale):
    b, n, _ = u.shape
    ug = u.reshape(b, n, POOL_GROUPS, POOL_GROUP_DIM)
    pooled = jnp.stack([_centred_mean_minus_self(ug[:, :, g], w) for g, w in enumerate(POOL_WINDOWS)], axis=2)
    mixed = jnp.einsum('bngc,gcd->bngd', pooled, pool_w).reshape(b, n, POOL_WIDTH)
    return mixed * pool_scale


def _heads(pieces, diff_q_norm, diff_k_norm, gqa_q_norm, gqa_k_norm, tables):
    b, n, _ = pieces[0].shape

    def prep(t, nh, g):
        t = _rmsnorm(t.reshape(b, n, nh, -1), g)
        return t if tables is None else _rope_2d(t, tables)

    dq = prep(pieces[2], 2 * DIFF_HEADS, diff_q_norm).reshape(b, n, DIFF_HEADS, 2, DIFF_HEAD_DIM)
    dk = prep(pieces[3], 2 * DIFF_HEADS, diff_k_norm).reshape(b, n, DIFF_HEADS, 2, DIFF_HEAD_DIM)
    dv = pieces[4].reshape(b, n, DIFF_HEADS, 2 * DIFF_HEAD_DIM)
    gq = prep(pieces[6], GQA_HEADS, gqa_q_norm).reshape(b, n, GQA_KV_HEADS, GQA_GROUP, GQA_HEAD_DIM)
    gk = prep(pieces[7], GQA_KV_HEADS, gqa_k_norm)
    gv = pieces[8].reshape(b, n, GQA_KV_HEADS, GQA_HEAD_DIM)
    return dq, dk, dv, gq, gk, gv


def _diff_core(q, k, v, lam):
    s = jnp.einsum('bqhcd,bkhcd->bhcqk', q, k).astype(jnp.float32) * (DIFF_HEAD_DIM ** -0.5)
    p = jax.nn.softmax(s, axis=-1)
    a = (p[:, :, 0] - lam * p[:, :, 1]).astype(v.dtype)
    return jnp.einsum('bhqk,bkhe->bqhe', a, v)


def _diff_post(o, diff_subln, lam_init):
    o = _rmsnorm(o, diff_subln) * (1.0 - lam_init)
    return o.reshape(o.shape[:2] + (DIFF_WIDTH,))


def _gqa_core(q, k, v):
    s = jnp.einsum('bqgrd,bkgd->bgrqk', q, k).astype(jnp.float32) * (GQA_HEAD_DIM ** -0.5)
    p = jax.nn.softmax(s, axis=-1).astype(v.dtype)
    return jnp.einsum('bgrqk,bkgd->bqgrd', p, v)


def _query_blocks(q):
    b, n = q.shape[:2]
    q = q.reshape((b, n // Q_BLOCK, Q_BLOCK) + q.shape[2:])
    return jnp.moveaxis(q, 1, 0)


def _merge_blocks(o):
    o = jnp.moveaxis(o, 0, 1)
    return o.reshape((o.shape[0], o.shape[1] * o.shape[2]) + o.shape[3:])


def _merge(pieces, pool_o, diff_o, gqa_o, w_branch, w_out):
    gqa_o = gqa_o.reshape(gqa_o.shape[:2] + (GQA_WIDTH,))
    b0 = pool_o * jax.nn.silu(pieces[1])
    b1 = diff_o * jax.nn.silu(pieces[5])
    b2 = gqa_o * jax.nn.silu(pieces[9])
    g0, g1, g2 = jnp.split(jax.nn.sigmoid(pieces[10]), N_BRANCHES, axis=-1)
    y = g0 * (b0 @ w_branch[0]) + g1 * (b1 @ w_branch[1]) + g2 * (b2 @ w_branch[2])
    return y @ w_out


def _layer(x, xc, c_act, cc_act, rope, lam_init, ctx_out,
           ada_w, ada_b, norm_g, w_in, pool_w, pool_scale,
           diff_q_norm, diff_k_norm, diff_lambda, diff_subln,
           gqa_q_norm, gqa_k_norm, w_branch, w_out):
    shift, scale, gate = jnp.split(c_act @ ada_w + ada_b, 3, axis=-1)
    shift_c, scale_c, gate_c = jnp.split(cc_act @ ada_w + ada_b, 3, axis=-1)
    h = _rmsnorm(x, norm_g) * (1.0 + scale[:, None, :]) + shift[:, None, :]
    hc = _rmsnorm(xc, norm_g) * (1.0 + scale_c) + shift_c
    p = _split_proj(h @ w_in)
    pc = _split_proj(hc @ w_in)
    dq, dk, dv, gq, gk, gv = _heads(p, diff_q_norm, diff_k_norm, gqa_q_norm, gqa_k_norm, rope)
    dqc, dkc, dvc, gqc, gkc, gvc = _heads(pc, diff_q_norm, diff_k_norm, gqa_q_norm, gqa_k_norm, None)
    lam_p = diff_lambda.astype(jnp.float32)
    lam = jnp.exp(jnp.sum(lam_p[0] * lam_p[1])) - jnp.exp(jnp.sum(lam_p[2] * lam_p[3])) + lam_init

    dk_all = jnp.concatenate([dkc, dk], axis=1)
    dv_all = jnp.concatenate([dvc, dv], axis=1)
    gk_all = jnp.concatenate([gkc, gk], axis=1)
    gv_all = jnp.concatenate([gvc, gv], axis=1)
    diff_o = _merge_blocks(lax.map(lambda qb: _diff_core(qb, dk_all, dv_all, lam), _query_blocks(dq)))
    gqa_o = _merge_blocks(lax.map(lambda qb: _gqa_core(qb, gk_all, gv_all), _query_blocks(gq)))
    pool_o = _pool_mixer(p[0], pool_w, pool_scale)
    out = _merge(p, pool_o, _diff_post(diff_o, diff_subln, lam_init), gqa_o, w_branch, w_out)
    x_new = x + gate[:, None, :] * out

    if ctx_out:
        diff_oc = _diff_core(dqc, dkc, dvc, lam)
        gqa_oc = _gqa_core(gqc, gkc, gvc)
        pool_oc = _pool_mixer(pc[0], pool_w, pool_scale)
        out_c = _merge(pc, pool_oc, _diff_post(diff_oc, diff_subln, lam_init), gqa_oc, w_branch, w_out)
        xc = xc + gate_c * out_c
    return x_new, xc


def setup_inputs(seed: int = 0) -> dict:
    key = jax.random.key(seed)
    ks = jax.random.split(key, 18)
    f32 = jnp.float32

    def nrm(k, shape, s):
        return jax.random.normal(k, shape, f32) * s

    def gain(k, shape):
        return 1.0 + 0.02 * jax.random.normal(k, shape, f32)

    return {
        "x": nrm(ks[0], (BATCH, SEQ, D_MODEL), 1.0),
        "c": nrm(ks[1], (BATCH, D_MODEL), 1.0),
        "ctx": nrm(ks[2], (BATCH, CTX_LEN, D_MODEL), 1.0),
        "c_ctx": nrm(ks[3], (D_MODEL,), 1.0),
        "ada_w": nrm(ks[4], (DEPTH, D_MODEL, 3 * D_MODEL), D_MODEL ** -0.5),
        "ada_b": nrm(ks[5], (DEPTH, 3 * D_MODEL), 0.01),
        "norm_g": gain(ks[6], (DEPTH, D_MODEL)),
        "w_in": nrm(ks[7], (DEPTH, D_MODEL, IN_WIDTH), D_MODEL ** -0.5),
        "pool_w": nrm(ks[8], (DEPTH, POOL_GROUPS, POOL_GROUP_DIM, POOL_GROUP_DIM), POOL_GROUP_DIM ** -0.5),
        "pool_scale": gain(ks[9], (DEPTH, POOL_WIDTH)),
        "diff_q_norm": gain(ks[10], (DEPTH, DIFF_HEAD_DIM)),
        "diff_k_norm": gain(ks[11], (DEPTH, DIFF_HEAD_DIM)),
        "diff_lambda": nrm(ks[12], (DEPTH, 4, DIFF_HEAD_DIM), 0.1),
        "diff_subln": gain(ks[13], (DEPTH, 2 * DIFF_HEAD_DIM)),
        "gqa_q_norm": gain(ks[14], (DEPTH, GQA_HEAD_DIM)),
        "gqa_k_norm": gain(ks[15], (DEPTH, GQA_HEAD_DIM)),
        "w_branch": nrm(ks[16], (DEPTH, N_BRANCHES, BRANCH_WIDTH, D_MODEL), BRANCH_WIDTH ** -0.5),
        "w_out": nrm(ks[17], (DEPTH, D_MODEL, D_MODEL), D_MODEL ** -0.5),
    }


def reference(x, c, ctx, c_ctx, ada_w, ada_b, norm_g, w_in, pool_w, pool_scale,
              diff_q_norm, diff_k_norm, diff_lambda, diff_subln, gqa_q_norm, gqa_k_norm,
              w_branch, w_out):
    rope = _rope_tables(x.shape[1])
    c_act = jax.nn.silu(c)
    cc_act = jax.nn.silu(c_ctx)
    xc = ctx
    for l in range(DEPTH):
        lam_init = 0.8 - 0.6 * math.exp(-0.3 * l)
        x, xc = _layer(x, xc, c_act, cc_act, rope, lam_init, l < DEPTH - 1,
                       ada_w[l], ada_b[l], norm_g[l], w_in[l], pool_w[l], pool_scale[l],
                       diff_q_norm[l], diff_k_norm[l], diff_lambda[l], diff_subln[l],
                       gqa_q_norm[l], gqa_k_norm[l], w_branch[l], w_out[l])
    return x
```

```python
import math
from contextlib import ExitStack
import numpy as np
import concourse.bass as bass
import concourse.mybir as mybir
from concourse.bass_utils import run_bass_kernel_spmd

F32 = mybir.dt.float32
BF16 = mybir.dt.bfloat16
ALU = mybir.AluOpType
AF = mybir.ActivationFunctionType
AX = mybir.AxisListType

D = 1024
NB = 4
NCORE = 8
DEPTH = 2
GRID_W = 64
CTX = 256
EPS = 1e-6
WIN = (2, 4, 8, 16)
PT = 130
NITEM = 11
ROWS = NITEM * 128
IN_W = 7424
SAME_SYNC = True


class Buf:
    def __init__(self, name):
        self.name = name
        self.w = None
        self.r = {}


class Eng:
    def __init__(self, name, e, sem, same):
        self.name, self.e, self.sem, self.cnt, self.seen, self.same = name, e, sem, 0, {}, same


class Trk:
    def __init__(self, nc):
        self.nc = nc
        self.sems = {}
        self.pe = self._eng("pe", nc.tensor, False)
        self.act = self._eng("act", nc.scalar, SAME_SYNC)
        self.dve = self._eng("dve", nc.vector, SAME_SYNC)
        self.pool = self._eng("pool", nc.gpsimd, SAME_SYNC)
        self.sp = self._eng("sp", nc.sync, False)
        self.engs = [self.pe, self.act, self.dve, self.pool, self.sp]
        self.free_dsems = {}
        self.dcnt = {}
        self.nd = 0

    def _eng(self, name, e, same):
        s = self.nc.alloc_semaphore("s_" + name)
        self.sems[name] = s
        return Eng(name, e, name, same)

    def dsem(self, qn):
        fl = self.free_dsems.setdefault(qn, [])
        if fl:
            return fl.pop()
        k = "d%s%d" % (qn, self.nd)
        self.nd += 1
        self.sems[k] = self.nc.alloc_semaphore("s_" + k)
        self.dcnt[k] = 0
        return k

    def _waits(self, eng, reads, writes):
        need = {}
        for b in reads:
            if b.w:
                need[b.w[0]] = max(need.get(b.w[0], 0), b.w[1])
        for b in writes:
            if b.w:
                need[b.w[0]] = max(need.get(b.w[0], 0), b.w[1])
            for k, v in b.r.items():
                need[k] = max(need.get(k, 0), v)
        for k, v in need.items():
            if k == eng.sem and not eng.same:
                continue
            if k == eng.sem and v > eng.cnt:
                continue
            if eng.seen.get(k, 0) < v:
                eng.e.wait_ge(self.sems[k], v)
                eng.seen[k] = v

    def op(self, eng, fn, reads=(), writes=(), inc=True):
        self._waits(eng, reads, writes)
        ins = fn()
        if inc:
            eng.cnt += 1
            ins.then_inc(self.sems[eng.sem], 1)
            val = eng.cnt
        else:
            val = eng.cnt + 1
        for b in reads:
            b.r[eng.sem] = max(b.r.get(eng.sem, 0), val)
        for b in writes:
            b.w = (eng.sem, val)
            b.r = {}
        return ins

    def dma(self, q, out, in_, owner, reads=(), writes=(), **kw):
        self._waits(q, reads, writes)
        if not hasattr(owner, "ds"):
            owner.ds = {}
        if q.name not in owner.ds:
            owner.ds[q.name] = self.dsem(q.name)
        k = owner.ds[q.name]
        q.e.dma_start(out=out, in_=in_, **kw).then_inc(self.sems[k], 16)
        self.dcnt[k] += 16
        val = self.dcnt[k]
        for b in reads:
            b.r[k] = max(b.r.get(k, 0), val)
        for b in writes:
            b.w = (k, val)
            b.r = {}

    def release(self, bufs):
        for b in bufs:
            if hasattr(b, "ds"):
                for qn, k in b.ds.items():
                    self.free_dsems.setdefault(qn, []).append(k)
                del b.ds

    def barrier(self):
        for e in self.engs:
            for o in self.engs:
                if o is e or o is self.sp:
                    continue
                if e.seen.get(o.sem, 0) < o.cnt:
                    e.e.wait_ge(self.sems[o.sem], o.cnt)
                    e.seen[o.sem] = o.cnt
            for k, v in self.dcnt.items():
                if v and e.seen.get(k, 0) < v:
                    e.e.wait_ge(self.sems[k], v)
                    e.seen[k] = v


def build(LT):
    GSL = min(4, LT)
    GS = max(GSL, 2)
    NGB = LT // GSL
    GC = 2
    NT = 2 + NB * LT
    NTOK = NT * 128
    L = max(NT * PT, NB * 2 * 512)
    NKT = 2 + NCORE * LT
    nc = bass.Bass("TRN2", target_bir_lowering=False)

    def din(name, shape, dt=F32):
        return nc.dram_tensor(name, list(shape), dt, kind="ExternalInput")

    xin = din("xin", [NTOK, D])
    cT_d = din("cT", [128, 8, 5])
    rope_d = din("rope", [NTOK, 128])
    poolA_d = din("poolA", [128, 4, 9, 128])
    ident_d = din("ident", [128, 128])
    sel_d = din("sel", [5, 5, 128])
    ada_w = din("ada_w", [DEPTH, D, 3 * D])
    ada_bT = din("ada_bT", [DEPTH, 128, 24])
    ada_b = din("ada_b", [DEPTH, 3 * D])
    norm_gT = din("norm_gT", [DEPTH, 128, 8])
    w_in = din("w_in", [DEPTH, D, IN_W])
    pool_w = din("pool_w", [DEPTH, 4, 128, 128])
    pool_sT = din("pool_sT", [DEPTH, 128, 4])
    qkg = din("qkg", [DEPTH, 4 * 64])
    dlam = din("dlam", [DEPTH, 256])
    subln = din("subln", [DEPTH, 128])
    w_br = din("w_branch", [DEPTH, 3, 512, D])
    w_out = din("w_out", [DEPTH, D, D])
    yout = nc.dram_tensor("y", [NB * LT * 128, D], F32, kind="ExternalOutput")

    kvloc = [nc.dram_tensor("kvloc%d" % l, [ROWS, L], BF16) for l in range(DEPTH)]
    kvall = [nc.dram_tensor("kvall%d" % l, [NCORE * ROWS, L], BF16) for l in range(DEPTH)]
    qT_d = nc.dram_tensor("qT_d", [8, 128, NTOK], BF16)
    hT_d = nc.dram_tensor("hT_d", [8, 128, NTOK], BF16)
    u_d = nc.dram_tensor("u_d", [NTOK, 512], BF16)
    ao_d = nc.dram_tensor("ao_d", [NTOK, D], BF16)
    x1_d = nc.dram_tensor("x1_d", [NTOK, D], F32)
    d_kvloc = [Buf("kvloc%d" % l) for l in range(DEPTH)]
    d_kvall = [Buf("kvall%d" % l) for l in range(DEPTH)]
    d_qT, d_hT, d_u, d_ao, d_x1 = Buf("qT"), Buf("hT"), Buf("u"), Buf("ao"), Buf("x1")

    T = Trk(nc)
    pe, act, dve, pool, sp = T.pe, T.act, T.dve, T.pool, T.sp
    ccsem = nc.alloc_semaphore("ccsem")
    cc_n = [0]

    PS = nc.alloc_psum_tensor("PS", [128, 8, 512], F32)
    PSB = PS.ap().bitcast(BF16)
    PSF = PS.ap()
    bank = [Buf("bank%d" % i) for i in range(8)]

    def sb(name, shape, dt):
        t = nc.alloc_sbuf_tensor("sb_" + name, list(shape), dt)
        return t.ap(), Buf(name)

    ident, b_ident = sb("ident", [128, 128], BF16)
    sel, b_sel = sb("sel", [5, 5, 128], F32)
    cT, b_cT = sb("cT", [128, 8, 5], F32)
    scT, b_scT = sb("scT", [128, 8, 5], F32)
    sgc, b_sgc = sb("sgc", [128, 8, 5], F32)
    modrows = [sb("modrows%d" % l, [5, D], F32) for l in range(DEPTH)]
    modT = [sb("modT%d" % l, [128, 24, 5], F32) for l in range(DEPTH)]
    geffT = [sb("geffT%d" % l, [128, 8, 5], F32) for l in range(DEPTH)]
    abT, b_abT = sb("abT", [128, DEPTH, 24], F32)
    ngT, b_ngT = sb("ngT", [128, DEPTH, 8], F32)
    poolA, b_poolA = sb("poolA", [128, 4, 9, 128], BF16)
    epsb, b_eps = sb("epsb", [128, 1], F32)
    pro = ExitStack()
    abrow = pro.enter_context(nc.sbuf_tensor("sb_abrow", [5, DEPTH, 3 * D], F32)).ap()
    b_abrow = Buf("abrow")
    mrs = pro.enter_context(nc.sbuf_tensor("sb_mrs", [5, 512], F32)).ap()
    b_mrs = Buf("mrs")

    T.dma(pool, ident, ident_d.ap(), b_ident, writes=[b_ident])
    T.dma(pool, poolA, poolA_d.ap(), b_poolA, writes=[b_poolA])
    T.dma(sp, sel, sel_d.ap(), b_sel, writes=[b_sel])
    T.dma(sp, cT, cT_d.ap(), b_cT, writes=[b_cT])
    T.dma(sp, abT, ada_bT.ap().rearrange("l p j -> p l j"), b_abT, writes=[b_abT])
    T.dma(sp, ngT, norm_gT.ap().rearrange("l p j -> p l j"), b_ngT, writes=[b_ngT])
    for m in range(5):
        T.dma(sp, abrow[m:m + 1], ada_b.ap().rearrange("(o l) n -> o l n", o=1), b_abrow, writes=[b_abrow])
    T.op(dve, lambda: nc.vector.memset(epsb, EPS), writes=[b_eps])
    T.op(act, lambda: nc.scalar.activation(out=sgc, in_=cT, func=AF.Sigmoid), reads=[b_cT], writes=[b_sgc])
    T.op(dve, lambda: nc.vector.tensor_tensor(out=scT, in0=cT, in1=sgc, op=ALU.mult), reads=[b_cT, b_sgc], writes=[b_scT])

    adab = [(pro.enter_context(nc.sbuf_tensor("sb_adablk%d" % i, [128, 8, 512], F32)).ap(), Buf("adab%d" % i)) for i in range(2)]
    blk_i = 0
    for l in range(DEPTH):
        mr, b_mr = modrows[l]
        mT, b_mT = modT[l]
        for cb in range(6):
            blk, b_blk = adab[blk_i % 2]
            blk_i += 1
            T.dma(sp, blk, ada_w.ap()[l].rearrange("(k p) n -> p k n", p=128)[:, :, cb * 512:(cb + 1) * 512],
                  b_blk, writes=[b_blk])
            bk = bank[cb % 2]
            for k in range(8):
                T.op(pe, lambda k=k: nc.tensor.matmul(PSF[0:5, cb % 2, :], lhsT=scT[:, k, :], rhs=blk[:, k, :],
                                                      start=(k == 0), stop=(k == 7)),
                     reads=[b_scT, b_blk], writes=[bk], inc=(k == 7))
            mdst = mr[:, (cb - 4) * 512:(cb - 3) * 512] if cb >= 4 else mrs
            T.op(dve, lambda: nc.vector.tensor_tensor(out=mdst, in0=PSF[0:5, cb % 2, :],
                                                      in1=abrow[:, l, cb * 512:(cb + 1) * 512], op=ALU.add),
                 reads=[bk, b_abrow], writes=[b_mr if cb >= 4 else b_mrs])
            for jj in range(4):
                ch = cb * 4 + jj
                bk2 = bank[2 + ch % 2]
                for k in range(8):
                    T.op(pe, lambda k=k: nc.tensor.matmul(PSF[:, 2 + ch % 2, 0:5], lhsT=blk[:, k, jj * 128:(jj + 1) * 128],
                                                          rhs=scT[:, k, :], start=(k == 0), stop=(k == 7)),
                         reads=[b_scT, b_blk], writes=[bk2], inc=(k == 7))
                T.op(dve, lambda: nc.vector.tensor_scalar(out=mT[:, ch, :], in0=PSF[:, 2 + ch % 2, 0:5],
                                                          scalar1=abT[:, l, ch:ch + 1], scalar2=None, op0=ALU.add),
                     reads=[bk2, b_abT], writes=[b_mT])
        gT, b_gT = geffT[l]
        T.op(dve, lambda: nc.vector.tensor_scalar(out=gT, in0=mT[:, 8:16, :], scalar1=1.0, scalar2=None, op0=ALU.add),
             reads=[b_mT], writes=[b_gT])
        T.op(dve, lambda: nc.vector.tensor_tensor(out=gT, in0=gT, in1=ngT[:, l, :].unsqueeze(2).broadcast_to([128, 8, 5]),
                                                  op=ALU.mult), reads=[b_gT, b_ngT], writes=[b_gT])

    T.barrier()
    T.release([b_abrow, adab[0][1], adab[1][1]])
    pro.close()
    for l in range(DEPTH):
        last = (l == DEPTH - 1)
        lam_init = 0.8 - 0.6 * math.exp(-0.3 * l)
        mr, b_mr = modrows[l]
        mT, b_mT = modT[l]
        gT, b_gT = geffT[l]
        xsrc, d_xsrc = (xin.ap(), None) if l == 0 else (x1_d.ap(), d_x1)
        T.barrier()

        with ExitStack() as es:
            WA_t = es.enter_context(nc.sbuf_tensor("WA%d" % l, [128, 8, 2816], BF16))
            xa_t = es.enter_context(nc.sbuf_tensor("xa%d" % l, [128, 2, D], F32))
            xn_t = es.enter_context(nc.sbuf_tensor("xn%d" % l, [128, 2, D], BF16))
            junk_t = es.enter_context(nc.sbuf_tensor("junk%d" % l, [128, D], F32))
            st_t = es.enter_context(nc.sbuf_tensor("st%d" % l, [128, 8], F32))
            hTg_t = es.enter_context(nc.sbuf_tensor("hTg%d" % l, [128, 8, GS * 128], BF16))
            qk_t = es.enter_context(nc.sbuf_tensor("qk%d" % l, [128, 26, 64], F32))
            sq_t = es.enter_context(nc.sbuf_tensor("sq%d" % l, [128, 26, 64], F32))
            t1_t = es.enter_context(nc.sbuf_tensor("t1%d" % l, [128, 26, 64], F32))
            t2_t = es.enter_context(nc.sbuf_tensor("t2%d" % l, [128, 26, 64], F32))
            qkb_t = es.enter_context(nc.sbuf_tensor("qkb%d" % l, [128, 26, 64], BF16))
            ss_t = es.enter_context(nc.sbuf_tensor("ss%d" % l, [128, 26], F32))
            g4_t = es.enter_context(nc.sbuf_tensor("g4%d" % l, [128, 4, 64], F32))
            Gbc_t = es.enter_context(nc.sbuf_tensor("Gbc%d" % l, [128, 26, 64], F32))
            rp_t = es.enter_context(nc.sbuf_tensor("rp%d" % l, [128, 2, 128], F32))
            vst_t = es.enter_context(nc.sbuf_tensor("vst%d" % l, [128, 4, GS, PT], BF16))
            vgst_t = es.enter_context(nc.sbuf_tensor("vgst%d" % l, [128, GS, 2, 65], BF16))
            qst_t = es.enter_context(nc.sbuf_tensor("qst%d" % l, [128, 8, GS * 128], BF16))
            kst_t = es.enter_context(nc.sbuf_tensor("kst%d" % l, [128, 5, GS * 128], BF16))
            ust_t = es.enter_context(nc.sbuf_tensor("ust%d" % l, [128, GS, 512], BF16))
            WA, xa, xn, junk, st, hTg = WA_t.ap(), xa_t.ap(), xn_t.ap(), junk_t.ap(), st_t.ap(), hTg_t.ap()
            qk, sq, t1, t2, qkb, ss, g4, Gbc, rp = (qk_t.ap(), sq_t.ap(), t1_t.ap(), t2_t.ap(), qkb_t.ap(),
                                                    ss_t.ap(), g4_t.ap(), Gbc_t.ap(), rp_t.ap())
            vst, vgst, qst, kst, ust = vst_t.ap(), vgst_t.ap(), qst_t.ap(), kst_t.ap(), ust_t.ap()
            b_WA = [Buf("WA%d" % i) for i in range(8)]
            b_xa = [Buf("xa0"), Buf("xa1")]
            b_xn = [Buf("xn0"), Buf("xn1")]
            b_rp = [Buf("rp0"), Buf("rp1")]
            b_junk, b_st, b_hTg, b_qk, b_sq, b_t1, b_t2, b_qkb, b_ss, b_g4, b_Gbc = [Buf(n) for n in
                ("junk", "st", "hTg", "qk", "sq", "t1", "t2", "qkb", "ss", "g4", "Gbc")]
            b_vst, b_vgst, b_qst, b_kst, b_ust = [Buf(n) for n in ("vst", "vgst", "qst", "kst", "ust")]
            phase_bufs = b_WA + b_xa + b_rp + [b_g4, b_vst, b_vgst, b_qst, b_kst, b_ust, b_hTg]

            wsrc = w_in.ap()[l].rearrange("(k p) n -> p k n", p=128)
            segs = [(0, 0, 512), (512, 1024, 512), (1024, 1536, 512), (1536, 2048, 512)]
            for r in range(4):
                segs.append((2048 + r * 128, 3072 + r * 64, 64))
                segs.append((2048 + r * 128 + 64, 3072 + (4 + r) * 64, 64))
            segs += [(2560, 3584, 128), (2688, 3712, 128)]
            for i, (do, so, n) in enumerate(segs):
                bw = b_WA[min(i, 7)] if i < 4 else b_WA[4 + (i % 4)]
                T.dma(pool, WA[:, :, do:do + n], wsrc[:, :, so:so + n], bw, writes=[bw])
            T.dma(sp, g4.rearrange("p t d -> p (t d)"), qkg.ap()[l].partition_broadcast(128), b_g4, writes=[b_g4])
            for (h0, h1, ti) in ((0, 8, 0), (8, 16, 1), (16, 24, 2), (24, 26, 3)):
                T.op(dve, lambda: nc.vector.tensor_copy(out=Gbc[:, h0:h1, :],
                                                        in_=g4[:, ti:ti + 1, :].broadcast_to([128, h1 - h0, 64])),
                     reads=[b_g4], writes=[b_Gbc])
            T.op(dve, lambda: nc.vector.memset(vst, 1.0), writes=[b_vst])
            T.op(dve, lambda: nc.vector.memset(vgst, 1.0), writes=[b_vgst])

            groups = [(0, 2, 4)] + [(2 + b * LT + g * GSL, GSL, b) for b in range(NB) for g in range(NGB)]
            pj = 0
            for (t0, gs, m) in groups:
                n = gs * 128
                for j in range(gs):
                    tile = t0 + j
                    s = tile % 2
                    T.dma(sp, xa[:, s, :], xsrc[tile * 128:(tile + 1) * 128, :], b_xa[s],
                          reads=[d_xsrc] if d_xsrc else [], writes=[b_xa[s]])
                    T.dma(sp, rp[:, s, :], rope_d.ap()[tile * 128:(tile + 1) * 128, :], b_rp[s], writes=[b_rp[s]])
                    T.op(dve, lambda: nc.vector.tensor_tensor(out=junk, in0=xa[:, s, :], in1=xa[:, s, :], op=ALU.mult),
                         reads=[b_xa[s]], writes=[b_junk])
                    T.op(dve, lambda: nc.vector.tensor_reduce(out=st[:, 0:1], in_=junk, axis=AX.X, op=ALU.add),
                         reads=[b_junk], writes=[b_st])
                    T.op(act, lambda: nc.scalar.activation(out=st[:, 1:2], in_=st[:, 0:1], func=AF.Ln, scale=1.0 / D, bias=epsb),
                         reads=[b_st, b_eps], writes=[b_st])
                    T.op(act, lambda: nc.scalar.activation(out=st[:, 2:3], in_=st[:, 1:2], func=AF.Exp, scale=-0.5),
                         reads=[b_st], writes=[b_st])
                    T.op(dve, lambda: nc.vector.tensor_scalar(out=xn[:, s, :], in0=xa[:, s, :], scalar1=st[:, 2:3],
                                                              scalar2=None, op0=ALU.mult),
                         reads=[b_xa[s], b_st], writes=[b_xn[s]])
                    bt = bank[4 + s]
                    for k in range(8):
                        T.op(pe, lambda k=k: nc.tensor.transpose(out=PSB[:, 4 + s, k * 128:(k + 1) * 128],
                                                                 in_=xn[:, s, k * 128:(k + 1) * 128], identity=ident),
                             reads=[b_xn[s], b_ident], writes=[bt], inc=(k == 7))
                    for k in range(8):
                        T.op(dve, lambda k=k: nc.vector.tensor_scalar(out=hTg[:, k, j * 128:(j + 1) * 128],
                                                                      in0=PSB[:, 4 + s, k * 128:(k + 1) * 128],
                                                                      scalar1=gT[:, k, m:m + 1], scalar2=mT[:, k, m:m + 1],
                                                                      op0=ALU.mult, op1=ALU.add),
                             reads=[bt, b_gT, b_mT], writes=[b_hTg])
                    for cbk in range(6):
                        ncol = 512 if cbk < 5 else 256
                        bi = pj % 4
                        pj += 1
                        bk = bank[bi]
                        for k in range(8):
                            T.op(pe, lambda k=k: nc.tensor.matmul(PSF[:, bi, 0:ncol], lhsT=hTg[:, k, j * 128:(j + 1) * 128],
                                                                  rhs=WA[:, k, cbk * 512:cbk * 512 + ncol],
                                                                  start=(k == 0), stop=(k == 7)),
                                 reads=[b_hTg] + b_WA, writes=[bk], inc=(k == 7))
                        src = PSF[:, bi, 0:ncol]
                        if cbk == 0:
                            T.op(act, lambda: nc.scalar.copy(out=ust[:, j, :], in_=src), reads=[bk], writes=[b_ust])
                        elif cbk == 3:
                            T.op(act, lambda: nc.scalar.copy(out=vst[:, :, j, 0:128],
                                                             in_=src.rearrange("p (h e) -> p h e", h=4)),
                                 reads=[bk], writes=[b_vst])
                        elif cbk == 5:
                            T.op(act, lambda: nc.scalar.copy(out=qk[:, 24:26, :], in_=src[:, 0:128].rearrange("p (h e) -> p h e", h=2)),
                                 reads=[bk], writes=[b_qk])
                            T.op(act, lambda: nc.scalar.copy(out=vgst[:, j, :, 0:64],
                                                             in_=src[:, 128:256].rearrange("p (h e) -> p h e", h=2)),
                                 reads=[bk], writes=[b_vgst])
                        else:
                            hh = {1: 0, 2: 8, 4: 16}[cbk]
                            T.op(act, lambda: nc.scalar.copy(out=qk[:, hh:hh + 8, :], in_=src.rearrange("p (h e) -> p h e", h=8)),
                                 reads=[bk], writes=[b_qk])
                    T.op(dve, lambda: nc.vector.tensor_tensor(out=sq, in0=qk, in1=qk, op=ALU.mult), reads=[b_qk], writes=[b_sq])
                    T.op(dve, lambda: nc.vector.tensor_reduce(out=ss, in_=sq, axis=AX.X, op=ALU.add), reads=[b_sq], writes=[b_ss])
                    T.op(act, lambda: nc.scalar.activation(out=ss, in_=ss, func=AF.Ln, scale=1.0 / 64, bias=epsb),
                         reads=[b_ss, b_eps], writes=[b_ss])
                    T.op(act, lambda: nc.scalar.activation(out=ss, in_=ss, func=AF.Exp, scale=-0.5), reads=[b_ss], writes=[b_ss])
                    T.op(dve, lambda: nc.vector.tensor_tensor(out=sq, in0=qk, in1=ss.unsqueeze(2).broadcast_to([128, 26, 64]),
                                                              op=ALU.mult), reads=[b_qk, b_ss], writes=[b_sq])
                    T.op(pool, lambda: nc.gpsimd.tensor_tensor(out=sq, in0=sq, in1=Gbc, op=ALU.mult),
                         reads=[b_sq, b_Gbc], writes=[b_sq])
                    cc_ap = rp[:, s, 0:64].unsqueeze(1).broadcast_to([128, 26, 64])
                    T.op(pool, lambda: nc.gpsimd.tensor_tensor(out=t1, in0=sq, in1=cc_ap, op=ALU.mult),
                         reads=[b_sq, b_rp[s]], writes=[b_t1])
                    sq5 = sq.rearrange("p h (a f e) -> p h a f e", a=2, f=2)
                    t25 = t2.rearrange("p h (a f e) -> p h a f e", a=2, f=2)
                    ss5 = rp[:, s, 64:128].rearrange("p (a f e) -> p a f e", a=2, f=2)
                    for f in range(2):
                        T.op(dve, lambda f=f: nc.vector.tensor_tensor(
                            out=t25[:, :, :, f, :], in0=sq5[:, :, :, 1 - f, :],
                            in1=ss5[:, :, f, :].unsqueeze(1).broadcast_to([128, 26, 2, 16]), op=ALU.mult),
                            reads=[b_sq, b_rp[s]], writes=[b_t2])
                    T.op(dve, lambda: nc.vector.tensor_tensor(out=qkb, in0=t1, in1=t2, op=ALU.add),
                         reads=[b_t1, b_t2], writes=[b_qkb])
                    blocks = [(0, i) for i in range(4)] + [(1, i) for i in range(4)] + [(2, i) for i in range(4)] + [(3, 0)]
                    for bi2, (ty, i) in enumerate(blocks):
                        hs = {0: 0, 1: 8, 2: 16, 3: 24}[ty] + 2 * i
                        pb = 6 + bi2 // 8
                        T.op(pe, lambda: nc.tensor.transpose(out=PSB[:, pb, (bi2 % 8) * 128:(bi2 % 8 + 1) * 128],
                                                             in_=qkb[:, hs:hs + 2, :].rearrange("p h e -> p (h e)"),
                                                             identity=ident),
                             reads=[b_qkb, b_ident], writes=[bank[pb]], inc=(bi2 in (7, 12)))
                    T.op(act, lambda: nc.scalar.copy(out=qst[:, 0:4, j * 128:(j + 1) * 128],
                                                     in_=PSB[:, 6, 0:512].rearrange("p (i e) -> p i e", i=4)),
                         reads=[bank[6]], writes=[b_qst])
                    T.op(act, lambda: nc.scalar.copy(out=kst[:, 0:4, j * 128:(j + 1) * 128],
                                                     in_=PSB[:, 6, 512:1024].rearrange("p (i e) -> p i e", i=4)),
                         reads=[bank[6]], writes=[b_kst])
                    T.op(act, lambda: nc.scalar.copy(out=qst[:, 4:8, j * 128:(j + 1) * 128],
                                                     in_=PSB[:, 7, 0:512].rearrange("p (i e) -> p i e", i=4)),
                         reads=[bank[7]], writes=[b_qst])
                    T.op(act, lambda: nc.scalar.copy(out=kst[:, 4, j * 128:(j + 1) * 128], in_=PSB[:, 7, 512:640]),
                         reads=[bank[7]], writes=[b_kst])
                tk = slice(t0 * 128, t0 * 128 + n)
                T.dma(sp, hT_d.ap().rearrange("k p t -> p k t")[:, :, tk], hTg[:, :, 0:n], b_hTg, reads=[b_hTg], writes=[d_hT])
                T.dma(sp, qT_d.ap().rearrange("k p t -> p k t")[:, :, tk], qst[:, :, 0:n], b_qst, reads=[b_qst], writes=[d_qT])
                T.dma(sp, u_d.ap().rearrange("(t p) c -> p t c", p=128)[:, t0:t0 + gs, :], ust[:, 0:gs, :], b_ust,
                      reads=[b_ust], writes=[d_u])
                kl = kvloc[l].ap()
                for it in range(5):
                    T.dma(sp, kl[it * 128:(it + 1) * 128, 0:NT * PT].rearrange("p (t e) -> p t e", e=PT)[:, t0:t0 + gs, 0:128],
                          kst[:, it, 0:n].rearrange("p (t e) -> p t e", e=128), b_kst, reads=[b_kst], writes=[d_kvloc[l]])
                for h in range(4):
                    T.dma(sp, kl[(5 + h) * 128:(6 + h) * 128, t0 * PT:(t0 + gs) * PT].rearrange("p (t e) -> p t e", e=PT),
                          vst[:, h, 0:gs, :], b_vst, reads=[b_vst], writes=[d_kvloc[l]])
                T.dma(sp, kl[9 * 128:10 * 128, t0 * PT:(t0 + gs) * PT].rearrange("p (t c e) -> p t c e", c=2, e=65),
                      vgst[:, 0:gs, :, :], b_vgst, reads=[b_vgst], writes=[d_kvloc[l]])
                if m < 4:
                    g_in_b = (t0 - 2 - m * LT) // GSL
                    if g_in_b == 0:
                        T.dma(sp, kl[10 * 128:10 * 128 + 8, (m * 2) * 512:(m * 2 + 1) * 512], ust[0:8, 0, :], b_ust,
                              reads=[b_ust], writes=[d_kvloc[l]])
                    if g_in_b == NGB - 1:
                        T.dma(sp, kl[10 * 128:10 * 128 + 8, (m * 2 + 1) * 512:(m * 2 + 2) * 512], ust[120:128, gs - 1, :],
                              b_ust, reads=[b_ust], writes=[d_kvloc[l]])
            T.barrier()
            T.release(phase_bufs)

        T._waits(pool, [d_kvloc[l]], [d_kvall[l]])
        nc.gpsimd.collective_compute("AllGather", ALU.bypass, replica_groups=[list(range(NCORE))],
                                     ins=[kvloc[l].ap().opt()], outs=[kvall[l].ap().opt()]).then_inc(ccsem)
        cc_n[0] += 1
        for e in T.engs:
            e.e.wait_ge(ccsem, cc_n[0])

        with ExitStack() as es:
            kt_t = es.enter_context(nc.sbuf_tensor("kt%d" % l, [128, 2, NKT, PT], BF16))
            vt_t = es.enter_context(nc.sbuf_tensor("vt%d" % l, [128, 2, NKT, PT], BF16))
            qt_t = es.enter_context(nc.sbuf_tensor("qt%d" % l, [128, 4, max(LT, 2) * 128], BF16))
            pt_t = es.enter_context(nc.sbuf_tensor("pt%d" % l, [128, 3, 2, GS * 128], BF16))
            aost_t = es.enter_context(nc.sbuf_tensor("aost%d" % l, [128, 2, GS, 128], BF16))
            pw_t = es.enter_context(nc.sbuf_tensor("pw%d" % l, [128, 4, 8], F32))
            o1_t = es.enter_context(nc.sbuf_tensor("o1%d" % l, [128, 4, 128], F32))
            o2_t = es.enter_context(nc.sbuf_tensor("o2%d" % l, [128, 4, 128], F32))
            osb_t = es.enter_context(nc.sbuf_tensor("osb%d" % l, [128, 2, 8, 129], F32))
            dl_t = es.enter_context(nc.sbuf_tensor("dl%d" % l, [128, 256], F32))
            lam_t = es.enter_context(nc.sbuf_tensor("lam%d" % l, [128, 8], F32))
            gsub_t = es.enter_context(nc.sbuf_tensor("gsub%d" % l, [128, 128], F32))
            kt, vt, qt, pt, aost, pw, o1, o2, dl, lam, gsub = (kt_t.ap(), vt_t.ap(), qt_t.ap(), pt_t.ap(), aost_t.ap(),
                                                             pw_t.ap(), o1_t.ap(), o2_t.ap(), dl_t.ap(), lam_t.ap(),
                                                             gsub_t.ap())
            b_kt = [Buf("kt0"), Buf("kt1")]
            b_vt = [Buf("vt0"), Buf("vt1")]
            b_qt = [Buf("qt%d" % i) for i in range(4)]
            b_pt = [Buf("pt%d" % i) for i in range(3)]
            b_aost = [Buf("aost0"), Buf("aost1")]
            b_dl, b_lam, b_gsub = [Buf(n) for n in ("dl", "lam", "gsub")]
            b_pw = [Buf("pw%d" % i) for i in range(4)]
            b_o1 = [Buf("o1%d" % i) for i in range(4)]
            b_o2 = [Buf("o2%d" % i) for i in range(4)]
            b_osb = [Buf("osb0"), Buf("osb1")]
            osb = osb_t.ap()
            phase_bufs = b_kt + b_vt + b_qt + b_aost + [b_dl, b_gsub]
            T.dma(sp, dl, dlam.ap()[l].partition_broadcast(128), b_dl, writes=[b_dl])
            T.dma(sp, gsub, subln.ap()[l].partition_broadcast(128), b_gsub, writes=[b_gsub])
            for i in range(2):
                T.op(dve, lambda i=i: nc.vector.tensor_tensor(out=o1[:, 0, 0:64], in0=dl[:, i * 128:i * 128 + 64],
                                                              in1=dl[:, i * 128 + 64:i * 128 + 128], op=ALU.mult),
                     reads=[b_dl], writes=[b_o1[0]])
                T.op(dve, lambda i=i: nc.vector.tensor_reduce(out=lam[:, i:i + 1], in_=o1[:, 0, 0:64], axis=AX.X, op=ALU.add),
                     reads=[b_o1[0]], writes=[b_lam])
            T.op(act, lambda: nc.scalar.activation(out=lam[:, 2:4], in_=lam[:, 0:2], func=AF.Exp), reads=[b_lam], writes=[b_lam])
            T.op(dve, lambda: nc.vector.tensor_tensor(out=lam[:, 4:5], in0=lam[:, 3:4], in1=lam[:, 2:3], op=ALU.subtract),
                 reads=[b_lam], writes=[b_lam])
            T.op(dve, lambda: nc.vector.tensor_scalar(out=lam[:, 5:6], in0=lam[:, 4:5], scalar1=-lam_init, scalar2=None,
                                                      op0=ALU.add), reads=[b_lam], writes=[b_lam])
            T.op(dve, lambda: nc.vector.tensor_scalar(out=gsub, in0=gsub, scalar1=1.0 - lam_init, scalar2=None, op0=ALU.mult),
                 reads=[b_gsub], writes=[b_gsub])
            neglam = lam[:, 5:6]
            ka = kvall[l].ap()
            kar = ka.rearrange("(r q) n -> q r n", q=ROWS)
            kl = kvloc[l].ap()
            ctr = {"kv": 0, "q": 0, "pt": 0, "s": 0, "ao": 0}

            def load_kv(dst, bdst, slot, item, b, local):
                if local:
                    T.dma(sp, dst[:, slot, 0:2, :], kl[item * 128:(item + 1) * 128, 0:2 * PT].rearrange("p (t e) -> p t e", e=PT),
                          bdst[slot], reads=[d_kvloc[l]], writes=[bdst[slot]])
                    return
                T.dma(sp, dst[:, slot, 0:2, :],
                      ka[2 * b * ROWS + item * 128:2 * b * ROWS + (item + 1) * 128, 0:2 * PT].rearrange("p (t e) -> p t e", e=PT),
                      bdst[slot], reads=[d_kvall[l]], writes=[bdst[slot]])
                T.dma(sp, dst[:, slot, 2:NKT, :].rearrange("p (r t) e -> p r (t e)", r=NCORE),
                      kar[item * 128:(item + 1) * 128, :, (2 + b * LT) * PT:(2 + b * LT + LT) * PT],
                      bdst[slot], reads=[d_kvall[l]], writes=[bdst[slot]])

            def attend(kslot, vslot, qslot, q0, nq, nkt, dv, vsel, ao_cols):
                QB = nq * 128
                PI = 129

                def oacc(c, j):
                    idx = c * nq + j
                    return PSF[:, 4 + idx // 3, (idx % 3) * PI:(idx % 3) * PI + dv + 1], bank[4 + idx // 3]

                def qk_step(i):
                    sbi = ctr["s"] % 2
                    ctr["s"] += 1
                    for c in range(2):
                        T.op(pe, lambda c=c: nc.tensor.matmul(PSF[:, 2 * sbi + c, 0:QB], lhsT=kt[c * 64:(c + 1) * 64, kslot, i, 0:128],
                                                              rhs=qt[c * 64:(c + 1) * 64, qslot, q0:q0 + QB], start=True, stop=True),
                             reads=[b_kt[kslot], b_qt[qslot]], writes=[bank[2 * sbi + c]], inc=(c == 1))
                    return sbi

                def exp_step(sbi):
                    pi = ctr["pt"] % 3
                    ctr["pt"] += 1
                    T.op(act, lambda: nc.scalar.activation(out=pt[:, pi, :, 0:QB], in_=PSF[:, 2 * sbi:2 * sbi + 2, 0:QB],
                                                           func=AF.Exp, scale=0.125),
                         reads=[bank[2 * sbi], bank[2 * sbi + 1]], writes=[b_pt[pi]])
                    return pi

                def av_step(i, pi):
                    for c in range(2):
                        for j in range(nq):
                            o_ap, o_b = oacc(c, j)
                            T.op(pe, lambda c=c, j=j, o_ap=o_ap: nc.tensor.matmul(
                                o_ap, lhsT=pt[:, pi, c, j * 128:(j + 1) * 128], rhs=vsel(vslot, i, c),
                                start=(i == 0), stop=(i == nkt - 1)),
                                reads=[b_pt[pi], b_vt[vslot]], writes=[o_b], inc=(c == 1 and j == nq - 1))

                pend = []
                for i in range(nkt):
                    sbi = qk_step(i)
                    pi = exp_step(sbi)
                    pend.append((i, pi))
                    if len(pend) > 1:
                        av_step(*pend.pop(0))
                while pend:
                    av_step(*pend.pop(0))
                asl = ctr["ao"] % 2
                ctr["ao"] += 1
                for idx in range(2 * nq):
                    o_ap, o_b = oacc(idx // nq, idx % nq)
                    T.op(dve, lambda: nc.vector.tensor_copy(out=osb[:, asl, idx, 0:dv + 1], in_=o_ap), reads=[o_b],
                         writes=[b_osb[asl]])
                for j in range(nq):
                    o0 = osb[:, asl, j, :]
                    o1a = osb[:, asl, nq + j, :]
                    pwj = pw[:, j, :]
                    rd = [b_osb[asl]]
                    T.op(dve, lambda: nc.vector.reciprocal(out=pwj[:, 0:1], in_=o0[:, dv:dv + 1]), reads=rd, writes=[b_pw[j]])
                    T.op(dve, lambda: nc.vector.reciprocal(out=pwj[:, 1:2], in_=o1a[:, dv:dv + 1]), reads=rd, writes=[b_pw[j]])
                    if dv == 128:
                        T.op(dve, lambda: nc.vector.tensor_scalar(out=o1[:, j, :], in0=o1a[:, 0:128], scalar1=pwj[:, 1:2], scalar2=neglam,
                                                                  op0=ALU.mult, op1=ALU.mult),
                             reads=rd + [b_pw[j], b_lam], writes=[b_o1[j]])
                        T.op(dve, lambda: nc.vector.scalar_tensor_tensor(out=o2[:, j, :], in0=o0[:, 0:128], scalar=pwj[:, 0:1], in1=o1[:, j, :],
                                                                         op0=ALU.mult, op1=ALU.add),
                             reads=rd + [b_pw[j], b_o1[j]], writes=[b_o2[j]])
                        T.op(pool, lambda: nc.gpsimd.tensor_tensor(out=o1[:, j, :], in0=o2[:, j, :], in1=o2[:, j, :], op=ALU.mult),
                             reads=[b_o2[j]], writes=[b_o1[j]])
                        T.op(dve, lambda: nc.vector.tensor_reduce(out=pwj[:, 2:3], in_=o1[:, j, :], axis=AX.X, op=ALU.add),
                             reads=[b_o1[j]], writes=[b_pw[j]])
                        T.op(act, lambda: nc.scalar.activation(out=pwj[:, 3:4], in_=pwj[:, 2:3], func=AF.Ln, scale=1.0 / 128, bias=epsb),
                             reads=[b_pw[j], b_eps], writes=[b_pw[j]])
                        T.op(act, lambda: nc.scalar.activation(out=pwj[:, 4:5], in_=pwj[:, 3:4], func=AF.Exp, scale=-0.5),
                             reads=[b_pw[j]], writes=[b_pw[j]])
                        T.op(dve, lambda: nc.vector.scalar_tensor_tensor(out=aost[:, asl, j, :], in0=o2[:, j, :], scalar=pwj[:, 4:5],
                                                                         in1=gsub, op0=ALU.mult, op1=ALU.mult),
                             reads=[b_o2[j], b_pw[j], b_gsub], writes=[b_aost[asl]])
                    else:
                        T.op(dve, lambda: nc.vector.tensor_scalar(out=aost[:, asl, j, 0:64], in0=o0[:, 0:64], scalar1=pwj[:, 0:1],
                                                                  scalar2=None, op0=ALU.mult),
                             reads=rd + [b_pw[j]], writes=[b_aost[asl]])
                        T.op(dve, lambda: nc.vector.tensor_scalar(out=aost[:, asl, j, 64:128], in0=o1a[:, 0:64], scalar1=pwj[:, 1:2],
                                                                  scalar2=None, op0=ALU.mult),
                             reads=rd + [b_pw[j]], writes=[b_aost[asl]])
                aor = ao_d.ap().rearrange("(t p) c -> p t c", p=128)
                tq0 = ao_cols[0]
                if dv == 128:
                    T.dma(pool, aor[:, tq0:tq0 + nq, ao_cols[1]:ao_cols[1] + 128], aost[:, asl, 0:nq, :], b_aost[asl],
                          reads=[b_aost[asl]], writes=[d_ao])
                else:
                    for c in range(2):
                        T.dma(pool, aor[:, tq0:tq0 + nq, ao_cols[1 + c]:ao_cols[1 + c] + 64], aost[:, asl, 0:nq, c * 64:(c + 1) * 64],
                              b_aost[asl], reads=[b_aost[asl]], writes=[d_ao])

            vsel_d = lambda vslot, i, c: vt[:, vslot, i, 0:129]
            vsel_g = lambda vslot, i, c: vt[:, vslot, i, c * 65:c * 65 + 65]
            qsrc = qT_d.ap()

            seqs = ([(0, True)] if not last else []) + [(b, False) for b in range(NB)]
            kgroups = []
            for (b, local) in seqs:
                for h in range(4):
                    kgroups.append(dict(b=b, local=local, kitem=h, vitem=5 + h, qs=[h], dv=128))
                kgroups.append(dict(b=b, local=local, kitem=4, vitem=9, qs=[4, 5, 6, 7], dv=64))
            jobs = [(gi, q) for gi, g in enumerate(kgroups) for q in g["qs"]]

            def geom(g):
                if g["local"]:
                    return 0, 2, 2
                return (2 + g["b"] * LT) * 128, LT, NKT

            def issue_kv(gi):
                g = kgroups[gi]
                load_kv(kt, b_kt, gi % 2, g["kitem"], g["b"], g["local"])
                load_kv(vt, b_vt, gi % 2, g["vitem"], g["b"], g["local"])

            def issue_q(ji):
                gi, q = jobs[ji]
                qtok0, nqt, nkt = geom(kgroups[gi])
                T.dma(sp, qt[:, ji % 4, 0:nqt * 128], qsrc[q, :, qtok0:qtok0 + nqt * 128], b_qt[ji % 4], reads=[d_qT],
                      writes=[b_qt[ji % 4]])

            issue_kv(0)
            issue_q(0)
            seen_g = set()
            for ji, (gi, q) in enumerate(jobs):
                g = kgroups[gi]
                if gi not in seen_g:
                    seen_g.add(gi)
                    if gi + 1 < len(kgroups):
                        issue_kv(gi + 1)
                if ji + 1 < len(jobs):
                    issue_q(ji + 1)
                qtok0, nqt, nkt = geom(g)
                qblocks = [(qb * GS, min(GS, nqt - qb * GS)) for qb in range((nqt + GS - 1) // GS)]
                for (qj, nq) in qblocks:
                    if g["dv"] == 128:
                        attend(gi % 2, gi % 2, ji % 4, qj * 128, nq, nkt, 128, vsel_d, (qtok0 // 128 + qj, q * 128))
                    else:
                        r = q - 4
                        attend(gi % 2, gi % 2, ji % 4, qj * 128, nq, nkt, 64, vsel_g,
                               (qtok0 // 128 + qj, 512 + r * 64, 512 + (4 + r) * 64))
            T.barrier()
            T.release(phase_bufs)

        with ExitStack() as es:
            WC_t = es.enter_context(nc.sbuf_tensor("WC%d" % l, [128, 8, 4608], BF16))
            WB_t = es.enter_context(nc.sbuf_tensor("WB%d" % l, [128, 12, D], BF16))
            WO_t = es.enter_context(nc.sbuf_tensor("WO%d" % l, [128, 8, D], BF16))
            PW_t = es.enter_context(nc.sbuf_tensor("PW%d" % l, [128, 4, 128], BF16))
            psT_t = es.enter_context(nc.sbuf_tensor("psT%d" % l, [128, 4], F32))
            Uh_t = es.enter_context(nc.sbuf_tensor("Uh%d" % l, [64, NB * 2 * 512], BF16))
            hTc_t = es.enter_context(nc.sbuf_tensor("hTc%d" % l, [128, 8, GC * 128], BF16))
            aog_t = es.enter_context(nc.sbuf_tensor("aog%d" % l, [128, GC, D], BF16))
            ug_t = es.enter_context(nc.sbuf_tensor("ug%d" % l, [128, GC + 2, 512], BF16))
            plT_t = es.enter_context(nc.sbuf_tensor("plT%d" % l, [128, 4, GC * 128], BF16))
            sig_t = es.enter_context(nc.sbuf_tensor("sig%d" % l, [128, 2, GC * 128], F32))
            sz_t = es.enter_context(nc.sbuf_tensor("sz%d" % l, [128, 2, GC * 128], BF16))
            bT_t = es.enter_context(nc.sbuf_tensor("bT%d" % l, [128, 12, GC * 128], BF16))
            sg_t = es.enter_context(nc.sbuf_tensor("sg%d" % l, [128, 3, GC * 128], BF16))
            acc_t = es.enter_context(nc.sbuf_tensor("acc%d" % l, [128, 2, GC * 128], F32))
            yT_t = es.enter_context(nc.sbuf_tensor("yT%d" % l, [128, 8, GC * 128], BF16))
            gbc_t = es.enter_context(nc.sbuf_tensor("gbc%d" % l, [128, D], F32))
            xc_t = es.enter_context(nc.sbuf_tensor("xc%d" % l, [128, 2, D], F32))
            xo_t = es.enter_context(nc.sbuf_tensor("xo%d" % l, [128, 2, D], F32))
            WC, WB, WO, PW, psT, Uh, hTc, aog, ug, plT = (WC_t.ap(), WB_t.ap(), WO_t.ap(), PW_t.ap(), psT_t.ap(), Uh_t.ap(),
                                                       hTc_t.ap(), aog_t.ap(), ug_t.ap(), plT_t.ap())
            sig, sz, bT, sg, acc, yT, gbc, xc, xo = (sig_t.ap(), sz_t.ap(), bT_t.ap(), sg_t.ap(), acc_t.ap(), yT_t.ap(),
                                                  gbc_t.ap(), xc_t.ap(), xo_t.ap())
            b_WC = [Buf("WC%d" % i) for i in range(9)]
            b_WB = [Buf("WB%d" % i) for i in range(3)]
            b_WO, b_PW, b_psT, b_Uh, b_hTc, b_aog, b_ug, b_plT = [Buf(n) for n in ("WO", "PW", "psT", "Uh", "hTc", "aog", "ug", "plT")]
            b_sig = [Buf("sig0"), Buf("sig1")]
            b_sz = [Buf("sz0"), Buf("sz1")]
            b_bT, b_yT, b_gbc = Buf("bT"), Buf("yT"), Buf("gbc")
            b_sg = [Buf("sg%d" % i) for i in range(3)]
            b_acc = [Buf("acc0"), Buf("acc1")]
            b_xc = [Buf("xc0"), Buf("xc1")]
            b_xo = [Buf("xo0"), Buf("xo1")]
            phase_bufs = b_WC + b_WB + [b_WO, b_PW, b_psT, b_Uh, b_hTc, b_aog, b_ug] + b_xc + b_xo
            wsrc = w_in.ap()[l].rearrange("(k p) n -> p k n", p=128)
            csegs = [(0, 512, 512), (512, 2560, 512), (1024, 3840, 512)] + [(1536 + i * 512, 4352 + i * 512, 512) for i in range(6)]
            for i, (do, so, n_) in enumerate(csegs):
                T.dma(pool, WC[:, :, do:do + n_], wsrc[:, :, so:so + n_], b_WC[i], writes=[b_WC[i]])
            for i in range(3):
                T.dma(pool, WB[:, i * 4:(i + 1) * 4, :], w_br.ap()[l, i].rearrange("(k p) n -> p k n", p=128), b_WB[i], writes=[b_WB[i]])
            T.dma(pool, WO, w_out.ap()[l].rearrange("(k p) n -> p k n", p=128), b_WO, writes=[b_WO])
            T.dma(pool, PW, pool_w.ap()[l].rearrange("g c d -> c g d"), b_PW, writes=[b_PW])
            T.dma(sp, psT, pool_sT.ap()[l], b_psT, writes=[b_psT])
            ka = kvall[l].ap()
            for r in range(NCORE):
                T.dma(sp, Uh[r * 8:(r + 1) * 8, :], ka[r * ROWS + 10 * 128:r * ROWS + 10 * 128 + 8, 0:NB * 2 * 512], b_Uh,
                      reads=[d_kvall[l]], writes=[b_Uh])
            bctr = [0]

            def nb():
                i = bctr[0] % 8
                bctr[0] += 1
                return i

            def WCall(k):
                return b_WC

            gcl = min(2, LT)
            groups = [(2 + b * LT + g * gcl, gcl, b, g) for b in range(NB) for g in range(LT // gcl)]
            if not last:
                groups = [(0, 2, 4, 0)] + groups
            xdst = yout.ap() if last else x1_d.ap()
            for (t0, gs, m, gi) in groups:
                n = gs * 128
                tk = slice(t0 * 128, t0 * 128 + n)
                T.dma(sp, hTc[:, :, 0:n], hT_d.ap().rearrange("k p t -> p k t")[:, :, tk], b_hTc, reads=[d_hT], writes=[b_hTc])
                T.dma(sp, aog[:, 0:gs, :], ao_d.ap().rearrange("(t p) c -> p t c", p=128)[:, t0:t0 + gs, :], b_aog,
                      reads=[d_ao], writes=[b_aog])
                seq0, seq1 = (0, 2) if m == 4 else (2 + m * LT, 2 + m * LT + LT)
                lo, hi = max(t0 - 1, seq0), min(t0 + gs + 1, seq1)
                T.dma(sp, ug[:, lo - (t0 - 1):hi - (t0 - 1), :], u_d.ap().rearrange("(t p) c -> p t c", p=128)[:, lo:hi, :], b_ug,
                      reads=[d_u], writes=[b_ug])
                for hf in range(2):
                    bi = nb()
                    T.op(pe, lambda: nc.tensor.matmul(PSF[:, bi, :], lhsT=sel[:, m, :], rhs=mr[:, hf * 512:(hf + 1) * 512],
                                                      start=True, stop=True), reads=[b_sel, b_mr], writes=[bank[bi]])
                    T.op(act, lambda: nc.scalar.copy(out=gbc[:, hf * 512:(hf + 1) * 512], in_=PSF[:, bi, :]), reads=[bank[bi]],
                         writes=[b_gbc])
                for g in range(4):
                    bi = nb()
                    for j in range(gs):
                        tile = t0 + j
                        srcs = []
                        if m == 4:
                            srcs.append((ug[:, j + 1, g * 128:(g + 1) * 128], poolA[:, g, 3 + j, :]))
                            if j == 0:
                                srcs.append((ug[:, j + 2, g * 128:(g + 1) * 128], poolA[:, g, 6, :]))
                            else:
                                srcs.append((ug[:, j, g * 128:(g + 1) * 128], poolA[:, g, 5, :]))
                        else:
                            tb = tile - (2 + m * LT)
                            slot = 0 if tb == 0 else (2 if tb == LT - 1 else 1)
                            srcs.append((ug[:, j + 1, g * 128:(g + 1) * 128], poolA[:, g, slot, :]))
                            if tb > 0:
                                srcs.append((ug[:, j, g * 128:(g + 1) * 128], poolA[:, g, 5, :]))
                            else:
                                srcs.append((Uh[0:64, (m * 2 + 1) * 512 + g * 128:(m * 2 + 1) * 512 + (g + 1) * 128],
                                             poolA[0:64, g, 7, :]))
                            if tb < LT - 1:
                                srcs.append((ug[:, j + 2, g * 128:(g + 1) * 128], poolA[:, g, 6, :]))
                            else:
                                srcs.append((Uh[0:64, (m * 2) * 512 + g * 128:(m * 2) * 512 + (g + 1) * 128],
                                             poolA[0:64, g, 8, :]))
                        for si, (lh, rh) in enumerate(srcs):
                            T.op(pe, lambda lh=lh, rh=rh, si=si: nc.tensor.matmul(PSF[:, bi, j * 128:(j + 1) * 128], lhsT=lh, rhs=rh,
                                                                                  start=(si == 0), stop=(si == len(srcs) - 1)),
                                 reads=[b_ug, b_Uh, b_poolA], writes=[bank[bi]], inc=(si == len(srcs) - 1))
                    T.op(act, lambda: nc.scalar.copy(out=plT[:, g, 0:n], in_=PSF[:, bi, 0:n]), reads=[bank[bi]], writes=[b_plT])

                def zgate(zc, slot):
                    bi = nb()
                    for k in range(8):
                        T.op(pe, lambda k=k: nc.tensor.matmul(PSF[:, bi, 0:n], lhsT=WC[:, k, zc * 128:(zc + 1) * 128], rhs=hTc[:, k, 0:n],
                                                              start=(k == 0), stop=(k == 7)),
                             reads=[b_hTc] + b_WC, writes=[bank[bi]], inc=(k == 7))
                    T.op(act, lambda: nc.scalar.activation(out=sig[:, slot, 0:n], in_=PSF[:, bi, 0:n], func=AF.Sigmoid),
                         reads=[bank[bi]], writes=[b_sig[slot]])
                    T.op(dve, lambda: nc.vector.tensor_tensor(out=sz[:, slot, 0:n], in0=PSF[:, bi, 0:n], in1=sig[:, slot, 0:n],
                                                              op=ALU.mult), reads=[bank[bi], b_sig[slot]], writes=[b_sz[slot]])

                zi = 0
                for g in range(4):
                    slot = zi % 2
                    zi += 1
                    zgate(g, slot)
                    bi = nb()
                    T.op(pe, lambda: nc.tensor.matmul(PSF[:, bi, 0:n], lhsT=PW[:, g, :], rhs=plT[:, g, 0:n], start=True, stop=True),
                         reads=[b_PW, b_plT], writes=[bank[bi]])
                    T.op(dve, lambda: nc.vector.scalar_tensor_tensor(out=bT[:, g, 0:n], in0=PSF[:, bi, 0:n], scalar=psT[:, g:g + 1],
                                                                     in1=sz[:, slot, 0:n], op0=ALU.mult, op1=ALU.mult),
                         reads=[bank[bi], b_psT, b_sz[slot]], writes=[b_bT])
                for ch in range(8):
                    slot = zi % 2
                    zi += 1
                    zgate(4 + ch, slot)
                    bi = nb()
                    for j in range(gs):
                        T.op(pe, lambda j=j: nc.tensor.transpose(out=PSB[:, bi, j * 128:(j + 1) * 128],
                                                                 in_=aog[:, j, ch * 128:(ch + 1) * 128], identity=ident),
                             reads=[b_aog, b_ident], writes=[bank[bi]], inc=(j == gs - 1))
                    T.op(dve, lambda: nc.vector.tensor_tensor(out=bT[:, 4 + ch, 0:n], in0=PSB[:, bi, 0:n], in1=sz[:, slot, 0:n],
                                                              op=ALU.mult), reads=[bank[bi], b_sz[slot]], writes=[b_bT])
                for oc in range(8):
                    a = oc % 2
                    for i in range(3):
                        bi = nb()
                        for k in range(8):
                            c0 = 1536 + i * 1024 + oc * 128
                            T.op(pe, lambda k=k: nc.tensor.matmul(PSF[:, bi, 0:n], lhsT=WC[:, k, c0:c0 + 128], rhs=hTc[:, k, 0:n],
                                                                  start=(k == 0), stop=(k == 7)),
                                 reads=[b_hTc] + b_WC, writes=[bank[bi]], inc=(k == 7))
                        T.op(act, lambda: nc.scalar.activation(out=sg[:, i, 0:n], in_=PSF[:, bi, 0:n], func=AF.Sigmoid),
                             reads=[bank[bi]], writes=[b_sg[i]])
                        bi2 = nb()
                        for kk in range(4):
                            T.op(pe, lambda kk=kk: nc.tensor.matmul(PSF[:, bi2, 0:n], lhsT=WB[:, i * 4 + kk, oc * 128:(oc + 1) * 128],
                                                                    rhs=bT[:, i * 4 + kk, 0:n], start=(kk == 0), stop=(kk == 3)),
                                 reads=[b_bT] + b_WB, writes=[bank[bi2]], inc=(kk == 3))
                        if i == 0:
                            T.op(dve, lambda: nc.vector.tensor_tensor(out=acc[:, a, 0:n], in0=PSF[:, bi2, 0:n], in1=sg[:, i, 0:n],
                                                                      op=ALU.mult), reads=[bank[bi2], b_sg[i]], writes=[b_acc[a]])
                        else:
                            T.op(dve, lambda: nc.vector.tensor_tensor(out=sig[:, 0, 0:n], in0=PSF[:, bi2, 0:n], in1=sg[:, i, 0:n],
                                                                      op=ALU.mult), reads=[bank[bi2], b_sg[i]], writes=[b_sig[0]])
                            if i == 1:
                                T.op(pool, lambda: nc.gpsimd.tensor_tensor(out=acc[:, a, 0:n], in0=acc[:, a, 0:n], in1=sig[:, 0, 0:n],
                                                                           op=ALU.add), reads=[b_acc[a], b_sig[0]], writes=[b_acc[a]])
                            else:
                                T.op(pool, lambda: nc.gpsimd.tensor_tensor(out=yT[:, oc, 0:n], in0=acc[:, a, 0:n], in1=sig[:, 0, 0:n],
                                                                           op=ALU.add), reads=[b_acc[a], b_sig[0]], writes=[b_yT])
                for j in range(gs):
                    tile = t0 + j
                    s = tile % 2
                    T.dma(sp, xc[:, s, :], xsrc[tile * 128:(tile + 1) * 128, :], b_xc[s], reads=[d_xsrc] if d_xsrc else [],
                          writes=[b_xc[s]])
                    for hf in range(2):
                        bi = nb()
                        for k in range(8):
                            T.op(pe, lambda k=k: nc.tensor.matmul(PSF[:, bi, :], lhsT=yT[:, k, j * 128:(j + 1) * 128],
                                                                  rhs=WO[:, k, hf * 512:(hf + 1) * 512], start=(k == 0), stop=(k == 7)),
                                 reads=[b_yT, b_WO], writes=[bank[bi]], inc=(k == 7))
                        T.op(dve, lambda: nc.vector.tensor_tensor(out=xo[:, s, hf * 512:(hf + 1) * 512], in0=PSF[:, bi, :],
                                                                  in1=gbc[:, hf * 512:(hf + 1) * 512], op=ALU.mult),
                             reads=[bank[bi], b_gbc], writes=[b_xo[s]])
                    T.op(pool, lambda: nc.gpsimd.tensor_tensor(out=xo[:, s, :], in0=xo[:, s, :], in1=xc[:, s, :], op=ALU.add),
                         reads=[b_xo[s], b_xc[s]], writes=[b_xo[s]])
                    if last:
                        orow = (tile - 2) * 128
                        T.dma(sp, xdst[orow:orow + 128, :], xo[:, s, :], b_xo[s], reads=[b_xo[s]], writes=[])
                    else:
                        T.dma(sp, xdst[tile * 128:(tile + 1) * 128, :], xo[:, s, :], b_xo[s], reads=[b_xo[s]], writes=[d_x1])
            T.barrier()
            T.release(phase_bufs)
    T.barrier()
    return nc


def _pool_mat(in_pos, out_pos, w, n):
    left = w // 2
    right = w - 1 - left
    A = np.zeros((len(in_pos), len(out_pos)), np.float32)
    for oi, t in enumerate(out_pos):
        if t < 0 or t >= n:
            continue
        lo, hi = max(t - left, 0), min(t + right + 1, n)
        for ii, s in enumerate(in_pos):
            if lo <= s < hi:
                A[ii, oi] += 1.0 / (hi - lo)
            if s == t:
                A[ii, oi] -= 1.0
    return A


def _host_consts(c, LT, seq):
    NT = 2 + NB * LT
    inv = np.power(np.float32(10000.0), -np.arange(0, 32, 2, dtype=np.float32) / np.float32(32)).astype(np.float32)
    rope = np.zeros((NT * 128, 128), np.float32)
    rope[:, 0:64] = 1.0
    pos = c * LT * 128 + np.arange(LT * 128)
    row = (pos // GRID_W).astype(np.float32)
    col = (pos % GRID_W).astype(np.float32)
    ar = (row[:, None] * inv[None, :]).astype(np.float32)
    ac = (col[:, None] * inv[None, :]).astype(np.float32)
    cc = np.zeros((LT * 128, 2, 2, 16), np.float32)
    ssn = np.zeros((LT * 128, 2, 2, 16), np.float32)
    for a, ang in enumerate((ar, ac)):
        cc[:, a, 0] = np.cos(ang)
        cc[:, a, 1] = np.cos(ang)
        ssn[:, a, 0] = -np.sin(ang)
        ssn[:, a, 1] = np.sin(ang)
    for b in range(NB):
        r0 = (2 + b * LT) * 128
        rope[r0:r0 + LT * 128, 0:64] = cc.reshape(-1, 64)
        rope[r0:r0 + LT * 128, 64:128] = ssn.reshape(-1, 64)
    pa = np.zeros((128, 4, 9, 128), np.float32)
    ar128 = np.arange(128)
    g0 = c * LT * 128
    for g, w in enumerate(WIN):
        pa[:, g, 0] = _pool_mat(g0 + ar128, g0 + ar128, w, seq)
        mid = g0 + 128 if LT > 2 else seq // 2 // 128 * 128
        pa[:, g, 1] = _pool_mat(mid + ar128, mid + ar128, w, seq)
        gl = g0 + (LT - 1) * 128
        pa[:, g, 2] = _pool_mat(gl + ar128, gl + ar128, w, seq)
        pa[:, g, 3] = _pool_mat(ar128, ar128, w, CTX)
        pa[:, g, 4] = _pool_mat(128 + ar128, 128 + ar128, w, CTX)
        pa[:, g, 5] = _pool_mat(ar128, 128 + ar128, w, 1 << 30)
        pa[:, g, 6] = _pool_mat(128 + ar128, ar128, w, 1 << 30)
        hp_in = np.concatenate([(r + 1) * LT * 128 - 8 + np.arange(8) for r in range(NCORE)])
        hp_in = np.where(np.repeat(np.arange(NCORE), 8) == c - 1, hp_in, -10 ** 6)
        pa[0:64, g, 7] = _pool_mat(hp_in, g0 + ar128, w, seq)
        hn_in = np.concatenate([r * LT * 128 + np.arange(8) for r in range(NCORE)])
        hn_in = np.where(np.repeat(np.arange(NCORE), 8) == c + 1, hn_in, -10 ** 6)
        pa[0:64, g, 8] = _pool_mat(hn_in, g0 + (LT - 1) * 128 + ar128, w, seq)
    return rope, pa


_NC_CACHE = {}


def kernel(x, c, ctx, c_ctx, ada_w, ada_b, norm_g, w_in, pool_w, pool_scale, diff_q_norm, diff_k_norm,
           diff_lambda, diff_subln, gqa_q_norm, gqa_k_norm, w_branch, w_out):
    f = lambda a: np.ascontiguousarray(np.asarray(a, dtype=np.float32))
    x, c, ctx, c_ctx = f(x), f(c), f(ctx), f(c_ctx)
    seq = x.shape[1]
    LT = seq // (NCORE * 128)
    if LT not in _NC_CACHE:
        _NC_CACHE[LT] = build(LT)
    nc = _NC_CACHE[LT]
    cmat = np.concatenate([c, c_ctx[None, :]], 0)
    shared = {
        "cT": f(cmat.reshape(5, 8, 128).transpose(2, 1, 0)),
        "ident": np.eye(128, dtype=np.float32),
        "sel": f(np.eye(5, dtype=np.float32)[:, :, None] * np.ones((1, 1, 128), np.float32)),
        "ada_w": f(ada_w),
        "ada_bT": f(np.asarray(ada_b).reshape(DEPTH, 24, 128).transpose(0, 2, 1)),
        "ada_b": f(ada_b),
        "norm_gT": f(np.asarray(norm_g).reshape(DEPTH, 8, 128).transpose(0, 2, 1)),
        "w_in": f(w_in),
        "pool_w": f(pool_w),
        "pool_sT": f(np.asarray(pool_scale).reshape(DEPTH, 4, 128).transpose(0, 2, 1)),
        "qkg": f(np.concatenate([np.asarray(diff_q_norm), np.asarray(diff_k_norm), np.asarray(gqa_q_norm),
                                 np.asarray(gqa_k_norm)], 1)),
        "dlam": f(np.asarray(diff_lambda).reshape(DEPTH, 256)),
        "subln": f(diff_subln),
        "w_branch": f(w_branch),
        "w_out": f(w_out),
    }
    in_maps = []
    for ci in range(NCORE):
        rope, pa = _host_consts(ci, LT, seq)
        xi = np.concatenate([ctx[ci // 2]] + [x[b, ci * LT * 128:(ci + 1) * LT * 128] for b in range(NB)], 0)
        d = dict(shared)
        d.update({"xin": f(xi), "rope": rope, "poolA": pa})
        in_maps.append(d)
    res = run_bass_kernel_spmd(nc, in_maps, core_ids=list(range(NCORE)))
    out = np.zeros((NB, seq, D), np.float32)
    for ci in range(NCORE):
        yy = np.asarray(res.results[ci]["y"]).reshape(NB, LT * 128, D)
        out[:, ci * LT * 128:(ci + 1) * LT * 128, :] = yy
    return out
```

```python
import math
from contextlib import ExitStack
import numpy as np
import concourse.bass as bass
import concourse.mybir as mybir
from concourse.bass_utils import run_bass_kernel_spmd

F32 = mybir.dt.float32
BF16 = mybir.dt.bfloat16
ALU = mybir.AluOpType
AF = mybir.ActivationFunctionType
AX = mybir.AxisListType

D = 1024
NB = 4
NCORE = 8
DEPTH = 2
GRID_W = 64
CTX = 256
EPS = 1e-6
WIN = (2, 4, 8, 16)
PT = 130
NITEM = 11
ROWS = NITEM * 128
IN_W = 7424
SAME_SYNC = True


class Buf:
    def __init__(self, name):
        self.name = name
        self.w = None
        self.r = {}


class Eng:
    def __init__(self, name, e, sem, same):
        self.name, self.e, self.sem, self.cnt, self.seen, self.same = name, e, sem, 0, {}, same


class Trk:
    def __init__(self, nc):
        self.nc = nc
        self.sems = {}
        self.pe = self._eng("pe", nc.tensor, False)
        self.act = self._eng("act", nc.scalar, SAME_SYNC)
        self.dve = self._eng("dve", nc.vector, SAME_SYNC)
        self.pool = self._eng("pool", nc.gpsimd, SAME_SYNC)
        self.sp = self._eng("sp", nc.sync, False)
        self.engs = [self.pe, self.act, self.dve, self.pool, self.sp]
        self.free_dsems = {}
        self.dcnt = {}
        self.nd = 0

    def _eng(self, name, e, same):
        s = self.nc.alloc_semaphore("s_" + name)
        self.sems[name] = s
        return Eng(name, e, name, same)

    def dsem(self, qn):
        fl = self.free_dsems.setdefault(qn, [])
        if fl:
            return fl.pop()
        k = "d%s%d" % (qn, self.nd)
        self.nd += 1
        self.sems[k] = self.nc.alloc_semaphore("s_" + k)
        self.dcnt[k] = 0
        return k

    def _waits(self, eng, reads, writes):
        need = {}
        for b in reads:
            if b.w:
                need[b.w[0]] = max(need.get(b.w[0], 0), b.w[1])
        for b in writes:
            if b.w:
                need[b.w[0]] = max(need.get(b.w[0], 0), b.w[1])
            for k, v in b.r.items():
                need[k] = max(need.get(k, 0), v)
        for k, v in need.items():
            if k == eng.sem and not eng.same:
                continue
            if k == eng.sem and v > eng.cnt:
                continue
            if eng.seen.get(k, 0) < v:
                eng.e.wait_ge(self.sems[k], v)
                eng.seen[k] = v

    def op(self, eng, fn, reads=(), writes=(), inc=True):
        self._waits(eng, reads, writes)
        ins = fn()
        if inc:
            eng.cnt += 1
            ins.then_inc(self.sems[eng.sem], 1)
            val = eng.cnt
        else:
            val = eng.cnt + 1
        for b in reads:
            b.r[eng.sem] = max(b.r.get(eng.sem, 0), val)
        for b in writes:
            b.w = (eng.sem, val)
            b.r = {}
        return ins

    def dma(self, q, out, in_, owner, reads=(), writes=(), **kw):
        self._waits(q, reads, writes)
        if not hasattr(owner, "ds"):
            owner.ds = {}
        if q.name not in owner.ds:
            owner.ds[q.name] = self.dsem(q.name)
        k = owner.ds[q.name]
        q.e.dma_start(out=out, in_=in_, **kw).then_inc(self.sems[k], 16)
        self.dcnt[k] += 16
        val = self.dcnt[k]
        for b in reads:
            b.r[k] = max(b.r.get(k, 0), val)
        for b in writes:
            b.w = (k, val)
            b.r = {}

    def release(self, bufs):
        for b in bufs:
            if hasattr(b, "ds"):
                for qn, k in b.ds.items():
                    self.free_dsems.setdefault(qn, []).append(k)
                del b.ds

    def barrier(self):
        for e in self.engs:
            for o in self.engs:
                if o is e or o is self.sp:
                    continue
                if e.seen.get(o.sem, 0) < o.cnt:
                    e.e.wait_ge(self.sems[o.sem], o.cnt)
                    e.seen[o.sem] = o.cnt
            for k, v in self.dcnt.items():
                if v and e.seen.get(k, 0) < v:
                    e.e.wait_ge(self.sems[k], v)
                    e.seen[k] = v


def build(LT):
    GSL = min(4, LT)
    GS = max(GSL, 2)
    NGB = LT // GSL
    GC = 2
    NT = 2 + NB * LT
    NTOK = NT * 128
    L = max(NT * PT, NB * 2 * 512)
    NKT = 2 + NCORE * LT
    nc = bass.Bass("TRN2", target_bir_lowering=False)

    def din(name, shape, dt=F32):
        return nc.dram_tensor(name, list(shape), dt, kind="ExternalInput")

    xin = din("xin", [NTOK, D])
    cT_d = din("cT", [128, 8, 5])
    rope_d = din("rope", [NTOK, 128])
    poolA_d = din("poolA", [128, 4, 9, 128])
    ident_d = din("ident", [128, 128])
    sel_d = din("sel", [5, 5, 128])
    ada_w = din("ada_w", [DEPTH, D, 3 * D])
    ada_bT = din("ada_bT", [DEPTH, 128, 24])
    ada_b = din("ada_b", [DEPTH, 3 * D])
    norm_gT = din("norm_gT", [DEPTH, 128, 8])
    w_in = din("w_in", [DEPTH, D, IN_W])
    pool_w = din("pool_w", [DEPTH, 4, 128, 128])
    pool_sT = din("pool_sT", [DEPTH, 128, 4])
    qkg = din("qkg", [DEPTH, 4 * 64])
    dlam = din("dlam", [DEPTH, 256])
    subln = din("subln", [DEPTH, 128])
    w_br = din("w_branch", [DEPTH, 3, 512, D])
    w_out = din("w_out", [DEPTH, D, D])
    yout = nc.dram_tensor("y", [NB * LT * 128, D], F32, kind="ExternalOutput")

    R10 = 10 * 128
    kvloc = [[nc.dram_tensor("kvloc%d_%d" % (l, sg), [R10, (2 if sg == 4 else LT) * PT], BF16) for sg in range(5)]
             for l in range(DEPTH)]
    kvall = [[nc.dram_tensor("kvall%d_%d" % (l, sg), [NCORE * R10, (2 if sg == 4 else LT) * PT], BF16) for sg in range(5)]
             for l in range(DEPTH)]
    uloc = [nc.dram_tensor("uloc%d" % l, [8, NB * 2 * 512], BF16) for l in range(DEPTH)]
    uall = [nc.dram_tensor("uall%d" % l, [NCORE * 8, NB * 2 * 512], BF16) for l in range(DEPTH)]
    qT_d = nc.dram_tensor("qT_d", [8, 128, NTOK], BF16)
    hT_d = nc.dram_tensor("hT_d", [8, 128, NTOK], BF16)
    u_d = nc.dram_tensor("u_d", [NTOK, 512], BF16)
    ao_d = nc.dram_tensor("ao_d", [NTOK, D], BF16)
    x1_d = nc.dram_tensor("x1_d", [NTOK, D], F32)
    d_kvloc = [[Buf("kvloc%d_%d" % (l, sg)) for sg in range(5)] for l in range(DEPTH)]
    d_kvall = [[Buf("kvall%d_%d" % (l, sg)) for sg in range(5)] for l in range(DEPTH)]
    d_uloc = [Buf("uloc%d" % l) for l in range(DEPTH)]
    d_uall = [Buf("uall%d" % l) for l in range(DEPTH)]
    d_qT, d_hT, d_u, d_ao, d_x1 = Buf("qT"), Buf("hT"), Buf("u"), Buf("ao"), Buf("x1")

    T = Trk(nc)
    pe, act, dve, pool, sp = T.pe, T.act, T.dve, T.pool, T.sp
    T.sems["cc"] = nc.alloc_semaphore("ccsem")
    cc_n = [0]

    def gather(src_t, dst_t, d_dst, store_bufs):
        for sbuf in store_bufs:
            for qn, k in getattr(sbuf, "ds", {}).items():
                v = T.dcnt[k]
                if pool.seen.get(k, 0) < v:
                    nc.gpsimd.wait_ge(T.sems[k], v)
                    pool.seen[k] = v
        nc.gpsimd.collective_compute("AllGather", ALU.bypass, replica_groups=[list(range(NCORE))],
                                     ins=[src_t.ap().opt()], outs=[dst_t.ap().opt()]).then_inc(T.sems["cc"])
        cc_n[0] += 1
        d_dst.w = ("cc", cc_n[0])
        d_dst.r = {}

    PS = nc.alloc_psum_tensor("PS", [128, 8, 512], F32)
    PSB = PS.ap().bitcast(BF16)
    PSF = PS.ap()
    bank = [Buf("bank%d" % i) for i in range(8)]

    def sb(name, shape, dt):
        t = nc.alloc_sbuf_tensor("sb_" + name, list(shape), dt)
        return t.ap(), Buf(name)

    ident, b_ident = sb("ident", [128, 128], BF16)
    sel, b_sel = sb("sel", [5, 5, 128], F32)
    cT, b_cT = sb("cT", [128, 8, 5], F32)
    scT, b_scT = sb("scT", [128, 8, 5], F32)
    sgc, b_sgc = sb("sgc", [128, 8, 5], F32)
    modrows = [sb("modrows%d" % l, [5, D], F32) for l in range(DEPTH)]
    modT = [sb("modT%d" % l, [128, 24, 5], F32) for l in range(DEPTH)]
    geffT = [sb("geffT%d" % l, [128, 8, 5], F32) for l in range(DEPTH)]
    abT, b_abT = sb("abT", [128, DEPTH, 24], F32)
    ngT, b_ngT = sb("ngT", [128, DEPTH, 8], F32)
    poolA, b_poolA = sb("poolA", [128, 4, 9, 128], BF16)
    epsb, b_eps = sb("epsb", [128, 1], F32)
    pro = ExitStack()
    abrow = pro.enter_context(nc.sbuf_tensor("sb_abrow", [5, DEPTH, 3 * D], F32)).ap()
    b_abrow = Buf("abrow")
    mrs = pro.enter_context(nc.sbuf_tensor("sb_mrs", [5, 512], F32)).ap()
    b_mrs = Buf("mrs")

    T.dma(pool, ident, ident_d.ap(), b_ident, writes=[b_ident])
    T.dma(pool, poolA, poolA_d.ap(), b_poolA, writes=[b_poolA])
    T.dma(sp, sel, sel_d.ap(), b_sel, writes=[b_sel])
    T.dma(sp, cT, cT_d.ap(), b_cT, writes=[b_cT])
    T.dma(sp, abT, ada_bT.ap().rearrange("l p j -> p l j"), b_abT, writes=[b_abT])
    T.dma(sp, ngT, norm_gT.ap().rearrange("l p j -> p l j"), b_ngT, writes=[b_ngT])
    for m in range(5):
        T.dma(sp, abrow[m:m + 1], ada_b.ap().rearrange("(o l) n -> o l n", o=1), b_abrow, writes=[b_abrow])
    T.op(dve, lambda: nc.vector.memset(epsb, EPS), writes=[b_eps])
    T.op(act, lambda: nc.scalar.activation(out=sgc, in_=cT, func=AF.Sigmoid), reads=[b_cT], writes=[b_sgc])
    T.op(dve, lambda: nc.vector.tensor_tensor(out=scT, in0=cT, in1=sgc, op=ALU.mult), reads=[b_cT, b_sgc], writes=[b_scT])

    adab = [(pro.enter_context(nc.sbuf_tensor("sb_adablk%d" % i, [128, 8, 512], F32)).ap(), Buf("adab%d" % i)) for i in range(2)]
    blk_i = 0
    for l in range(DEPTH):
        mr, b_mr = modrows[l]
        mT, b_mT = modT[l]
        for cb in range(6):
            blk, b_blk = adab[blk_i % 2]
            blk_i += 1
            T.dma(sp, blk, ada_w.ap()[l].rearrange("(k p) n -> p k n", p=128)[:, :, cb * 512:(cb + 1) * 512],
                  b_blk, writes=[b_blk])
            bk = bank[cb % 2]
            for k in range(8):
                T.op(pe, lambda k=k: nc.tensor.matmul(PSF[0:5, cb % 2, :], lhsT=scT[:, k, :], rhs=blk[:, k, :],
                                                      start=(k == 0), stop=(k == 7)),
                     reads=[b_scT, b_blk], writes=[bk], inc=(k == 7))
            mdst = mr[:, (cb - 4) * 512:(cb - 3) * 512] if cb >= 4 else mrs
            T.op(dve, lambda: nc.vector.tensor_tensor(out=mdst, in0=PSF[0:5, cb % 2, :],
                                                      in1=abrow[:, l, cb * 512:(cb + 1) * 512], op=ALU.add),
                 reads=[bk, b_abrow], writes=[b_mr if cb >= 4 else b_mrs])
            for jj in range(4):
                ch = cb * 4 + jj
                bk2 = bank[2 + ch % 2]
                for k in range(8):
                    T.op(pe, lambda k=k: nc.tensor.matmul(PSF[:, 2 + ch % 2, 0:5], lhsT=blk[:, k, jj * 128:(jj + 1) * 128],
                                                          rhs=scT[:, k, :], start=(k == 0), stop=(k == 7)),
                         reads=[b_scT, b_blk], writes=[bk2], inc=(k == 7))
                T.op(dve, lambda: nc.vector.tensor_scalar(out=mT[:, ch, :], in0=PSF[:, 2 + ch % 2, 0:5],
                                                          scalar1=abT[:, l, ch:ch + 1], scalar2=None, op0=ALU.add),
                     reads=[bk2, b_abT], writes=[b_mT])
        gT, b_gT = geffT[l]
        T.op(dve, lambda: nc.vector.tensor_scalar(out=gT, in0=mT[:, 8:16, :], scalar1=1.0, scalar2=None, op0=ALU.add),
             reads=[b_mT], writes=[b_gT])
        T.op(dve, lambda: nc.vector.tensor_tensor(out=gT, in0=gT, in1=ngT[:, l, :].unsqueeze(2).broadcast_to([128, 8, 5]),
                                                  op=ALU.mult), reads=[b_gT, b_ngT], writes=[b_gT])

    T.barrier()
    T.release([b_abrow, adab[0][1], adab[1][1]])
    pro.close()
    for l in range(DEPTH):
        last = (l == DEPTH - 1)
        lam_init = 0.8 - 0.6 * math.exp(-0.3 * l)
        mr, b_mr = modrows[l]
        mT, b_mT = modT[l]
        gT, b_gT = geffT[l]
        xsrc, d_xsrc = (xin.ap(), None) if l == 0 else (x1_d.ap(), d_x1)
        T.barrier()

        with ExitStack() as es:
            WA_t = es.enter_context(nc.sbuf_tensor("WA%d" % l, [128, 8, 2816], BF16))
            xa_t = es.enter_context(nc.sbuf_tensor("xa%d" % l, [128, 2, D], F32))
            xn_t = es.enter_context(nc.sbuf_tensor("xn%d" % l, [128, 2, D], BF16))
            junk_t = es.enter_context(nc.sbuf_tensor("junk%d" % l, [128, D], F32))
            st_t = es.enter_context(nc.sbuf_tensor("st%d" % l, [128, 8], F32))
            hTg_t = es.enter_context(nc.sbuf_tensor("hTg%d" % l, [128, 8, GS * 128], BF16))
            qk_t = es.enter_context(nc.sbuf_tensor("qk%d" % l, [128, 26, 64], F32))
            sq_t = es.enter_context(nc.sbuf_tensor("sq%d" % l, [128, 26, 64], F32))
            t1_t = es.enter_context(nc.sbuf_tensor("t1%d" % l, [128, 26, 64], F32))
            t2_t = es.enter_context(nc.sbuf_tensor("t2%d" % l, [128, 26, 64], F32))
            qkb_t = es.enter_context(nc.sbuf_tensor("qkb%d" % l, [128, 26, 64], BF16))
            ss_t = es.enter_context(nc.sbuf_tensor("ss%d" % l, [128, 26], F32))
            g4_t = es.enter_context(nc.sbuf_tensor("g4%d" % l, [128, 4, 64], F32))
            Gbc_t = es.enter_context(nc.sbuf_tensor("Gbc%d" % l, [128, 26, 64], F32))
            rp_t = es.enter_context(nc.sbuf_tensor("rp%d" % l, [128, 2, 128], F32))
            vst_t = es.enter_context(nc.sbuf_tensor("vst%d" % l, [128, 4, GS, PT], BF16))
            vgst_t = es.enter_context(nc.sbuf_tensor("vgst%d" % l, [128, GS, 2, 65], BF16))
            qst_t = es.enter_context(nc.sbuf_tensor("qst%d" % l, [128, 8, GS * 128], BF16))
            kst_t = es.enter_context(nc.sbuf_tensor("kst%d" % l, [128, 5, GS * 128], BF16))
            ust_t = es.enter_context(nc.sbuf_tensor("ust%d" % l, [128, GS, 512], BF16))
            WA, xa, xn, junk, st, hTg = WA_t.ap(), xa_t.ap(), xn_t.ap(), junk_t.ap(), st_t.ap(), hTg_t.ap()
            qk, sq, t1, t2, qkb, ss, g4, Gbc, rp = (qk_t.ap(), sq_t.ap(), t1_t.ap(), t2_t.ap(), qkb_t.ap(),
                                                    ss_t.ap(), g4_t.ap(), Gbc_t.ap(), rp_t.ap())
            vst, vgst, qst, kst, ust = vst_t.ap(), vgst_t.ap(), qst_t.ap(), kst_t.ap(), ust_t.ap()
            b_WA = [Buf("WA%d" % i) for i in range(8)]
            b_xa = [Buf("xa0"), Buf("xa1")]
            b_xn = [Buf("xn0"), Buf("xn1")]
            b_rp = [Buf("rp0"), Buf("rp1")]
            b_junk, b_st, b_hTg, b_qk, b_sq, b_t1, b_t2, b_qkb, b_ss, b_g4, b_Gbc = [Buf(n) for n in
                ("junk", "st", "hTg", "qk", "sq", "t1", "t2", "qkb", "ss", "g4", "Gbc")]
            b_vst, b_vgst, b_qst, b_kst, b_ust = [Buf(n) for n in ("vst", "vgst", "qst", "kst", "ust")]
            phase_bufs = b_WA + b_xa + b_rp + [b_g4, b_vst, b_vgst, b_qst, b_kst, b_ust, b_hTg]

            wsrc = w_in.ap()[l].rearrange("(k p) n -> p k n", p=128)
            segs = [(0, 0, 512), (512, 1024, 512), (1024, 1536, 512), (1536, 2048, 512)]
            for r in range(4):
                segs.append((2048 + r * 128, 3072 + r * 64, 64))
                segs.append((2048 + r * 128 + 64, 3072 + (4 + r) * 64, 64))
            segs += [(2560, 3584, 128), (2688, 3712, 128)]
            for i, (do, so, n) in enumerate(segs):
                bw = b_WA[min(i, 7)] if i < 4 else b_WA[4 + (i % 4)]
                T.dma(pool, WA[:, :, do:do + n], wsrc[:, :, so:so + n], bw, writes=[bw])
            T.dma(sp, g4.rearrange("p t d -> p (t d)"), qkg.ap()[l].partition_broadcast(128), b_g4, writes=[b_g4])
            for (h0, h1, ti) in ((0, 8, 0), (8, 16, 1), (16, 24, 2), (24, 26, 3)):
                T.op(dve, lambda: nc.vector.tensor_copy(out=Gbc[:, h0:h1, :],
                                                        in_=g4[:, ti:ti + 1, :].broadcast_to([128, h1 - h0, 64])),
                     reads=[b_g4], writes=[b_Gbc])
            T.op(dve, lambda: nc.vector.memset(vst, 1.0), writes=[b_vst])
            T.op(dve, lambda: nc.vector.memset(vgst, 1.0), writes=[b_vgst])

            groups = [(0, 2, 4)] + [(2 + b * LT + g * GSL, GSL, b) for b in range(NB) for g in range(NGB)]
            pj = 0
            for (t0, gs, m) in groups:
                n = gs * 128
                for j in range(gs):
                    tile = t0 + j
                    s = tile % 2
                    T.dma(sp, xa[:, s, :], xsrc[tile * 128:(tile + 1) * 128, :], b_xa[s],
                          reads=[d_xsrc] if d_xsrc else [], writes=[b_xa[s]])
                    T.dma(sp, rp[:, s, :], rope_d.ap()[tile * 128:(tile + 1) * 128, :], b_rp[s], writes=[b_rp[s]])
                    T.op(dve, lambda: nc.vector.tensor_tensor(out=junk, in0=xa[:, s, :], in1=xa[:, s, :], op=ALU.mult),
                         reads=[b_xa[s]], writes=[b_junk])
                    T.op(dve, lambda: nc.vector.tensor_reduce(out=st[:, 0:1], in_=junk, axis=AX.X, op=ALU.add),
                         reads=[b_junk], writes=[b_st])
                    T.op(act, lambda: nc.scalar.activation(out=st[:, 1:2], in_=st[:, 0:1], func=AF.Ln, scale=1.0 / D, bias=epsb),
                         reads=[b_st, b_eps], writes=[b_st])
                    T.op(act, lambda: nc.scalar.activation(out=st[:, 2:3], in_=st[:, 1:2], func=AF.Exp, scale=-0.5),
                         reads=[b_st], writes=[b_st])
                    T.op(dve, lambda: nc.vector.tensor_scalar(out=xn[:, s, :], in0=xa[:, s, :], scalar1=st[:, 2:3],
                                                              scalar2=None, op0=ALU.mult),
                         reads=[b_xa[s], b_st], writes=[b_xn[s]])
                    bt = bank[4 + s]
                    for k in range(8):
                        T.op(pe, lambda k=k: nc.tensor.transpose(out=PSB[:, 4 + s, k * 128:(k + 1) * 128],
                                                                 in_=xn[:, s, k * 128:(k + 1) * 128], identity=ident),
                             reads=[b_xn[s], b_ident], writes=[bt], inc=(k == 7))
                    for k in range(8):
                        T.op(dve, lambda k=k: nc.vector.tensor_scalar(out=hTg[:, k, j * 128:(j + 1) * 128],
                                                                      in0=PSB[:, 4 + s, k * 128:(k + 1) * 128],
                                                                      scalar1=gT[:, k, m:m + 1], scalar2=mT[:, k, m:m + 1],
                                                                      op0=ALU.mult, op1=ALU.add),
                             reads=[bt, b_gT, b_mT], writes=[b_hTg])
                    for cbk in range(6):
                        ncol = 512 if cbk < 5 else 256
                        bi = pj % 4
                        pj += 1
                        bk = bank[bi]
                        for k in range(8):
                            T.op(pe, lambda k=k: nc.tensor.matmul(PSF[:, bi, 0:ncol], lhsT=hTg[:, k, j * 128:(j + 1) * 128],
                                                                  rhs=WA[:, k, cbk * 512:cbk * 512 + ncol],
                                                                  start=(k == 0), stop=(k == 7)),
                                 reads=[b_hTg] + b_WA, writes=[bk], inc=(k == 7))
                        src = PSF[:, bi, 0:ncol]
                        if cbk == 0:
                            T.op(act, lambda: nc.scalar.copy(out=ust[:, j, :], in_=src), reads=[bk], writes=[b_ust])
                        elif cbk == 3:
                            T.op(act, lambda: nc.scalar.copy(out=vst[:, :, j, 0:128],
                                                             in_=src.rearrange("p (h e) -> p h e", h=4)),
                                 reads=[bk], writes=[b_vst])
                        elif cbk == 5:
                            T.op(act, lambda: nc.scalar.copy(out=qk[:, 24:26, :], in_=src[:, 0:128].rearrange("p (h e) -> p h e", h=2)),
                                 reads=[bk], writes=[b_qk])
                            T.op(act, lambda: nc.scalar.copy(out=vgst[:, j, :, 0:64],
                                                             in_=src[:, 128:256].rearrange("p (h e) -> p h e", h=2)),
                                 reads=[bk], writes=[b_vgst])
                        else:
                            hh = {1: 0, 2: 8, 4: 16}[cbk]
                            T.op(act, lambda: nc.scalar.copy(out=qk[:, hh:hh + 8, :], in_=src.rearrange("p (h e) -> p h e", h=8)),
                                 reads=[bk], writes=[b_qk])
                    T.op(dve, lambda: nc.vector.tensor_tensor(out=sq, in0=qk, in1=qk, op=ALU.mult), reads=[b_qk], writes=[b_sq])
                    T.op(dve, lambda: nc.vector.tensor_reduce(out=ss, in_=sq, axis=AX.X, op=ALU.add), reads=[b_sq], writes=[b_ss])
                    T.op(act, lambda: nc.scalar.activation(out=ss, in_=ss, func=AF.Ln, scale=1.0 / 64, bias=epsb),
                         reads=[b_ss, b_eps], writes=[b_ss])
                    T.op(act, lambda: nc.scalar.activation(out=ss, in_=ss, func=AF.Exp, scale=-0.5), reads=[b_ss], writes=[b_ss])
                    T.op(dve, lambda: nc.vector.tensor_tensor(out=sq, in0=qk, in1=ss.unsqueeze(2).broadcast_to([128, 26, 64]),
                                                              op=ALU.mult), reads=[b_qk, b_ss], writes=[b_sq])
                    T.op(pool, lambda: nc.gpsimd.tensor_tensor(out=sq, in0=sq, in1=Gbc, op=ALU.mult),
                         reads=[b_sq, b_Gbc], writes=[b_sq])
                    cc_ap = rp[:, s, 0:64].unsqueeze(1).broadcast_to([128, 26, 64])
                    T.op(pool, lambda: nc.gpsimd.tensor_tensor(out=t1, in0=sq, in1=cc_ap, op=ALU.mult),
                         reads=[b_sq, b_rp[s]], writes=[b_t1])
                    sq5 = sq.rearrange("p h (a f e) -> p h a f e", a=2, f=2)
                    t25 = t2.rearrange("p h (a f e) -> p h a f e", a=2, f=2)
                    ss5 = rp[:, s, 64:128].rearrange("p (a f e) -> p a f e", a=2, f=2)
                    for f in range(2):
                        T.op(dve, lambda f=f: nc.vector.tensor_tensor(
                            out=t25[:, :, :, f, :], in0=sq5[:, :, :, 1 - f, :],
                            in1=ss5[:, :, f, :].unsqueeze(1).broadcast_to([128, 26, 2, 16]), op=ALU.mult),
                            reads=[b_sq, b_rp[s]], writes=[b_t2])
                    T.op(dve, lambda: nc.vector.tensor_tensor(out=qkb, in0=t1, in1=t2, op=ALU.add),
                         reads=[b_t1, b_t2], writes=[b_qkb])
                    blocks = [(0, i) for i in range(4)] + [(1, i) for i in range(4)] + [(2, i) for i in range(4)] + [(3, 0)]
                    for bi2, (ty, i) in enumerate(blocks):
                        hs = {0: 0, 1: 8, 2: 16, 3: 24}[ty] + 2 * i
                        pb = 6 + bi2 // 8
                        T.op(pe, lambda: nc.tensor.transpose(out=PSB[:, pb, (bi2 % 8) * 128:(bi2 % 8 + 1) * 128],
                                                             in_=qkb[:, hs:hs + 2, :].rearrange("p h e -> p (h e)"),
                                                             identity=ident),
                             reads=[b_qkb, b_ident], writes=[bank[pb]], inc=(bi2 in (7, 12)))
                    T.op(act, lambda: nc.scalar.copy(out=qst[:, 0:4, j * 128:(j + 1) * 128],
                                                     in_=PSB[:, 6, 0:512].rearrange("p (i e) -> p i e", i=4)),
                         reads=[bank[6]], writes=[b_qst])
                    T.op(act, lambda: nc.scalar.copy(out=kst[:, 0:4, j * 128:(j + 1) * 128],
                                                     in_=PSB[:, 6, 512:1024].rearrange("p (i e) -> p i e", i=4)),
                         reads=[bank[6]], writes=[b_kst])
                    T.op(act, lambda: nc.scalar.copy(out=qst[:, 4:8, j * 128:(j + 1) * 128],
                                                     in_=PSB[:, 7, 0:512].rearrange("p (i e) -> p i e", i=4)),
                         reads=[bank[7]], writes=[b_qst])
                    T.op(act, lambda: nc.scalar.copy(out=kst[:, 4, j * 128:(j + 1) * 128], in_=PSB[:, 7, 512:640]),
                         reads=[bank[7]], writes=[b_kst])
                tk = slice(t0 * 128, t0 * 128 + n)
                T.dma(sp, hT_d.ap().rearrange("k p t -> p k t")[:, :, tk], hTg[:, :, 0:n], b_hTg, reads=[b_hTg], writes=[d_hT])
                T.dma(sp, qT_d.ap().rearrange("k p t -> p k t")[:, :, tk], qst[:, :, 0:n], b_qst, reads=[b_qst], writes=[d_qT])
                T.dma(sp, u_d.ap().rearrange("(t p) c -> p t c", p=128)[:, t0:t0 + gs, :], ust[:, 0:gs, :], b_ust,
                      reads=[b_ust], writes=[d_u])
                sg = m
                tl0 = t0 if m == 4 else t0 - (2 + m * LT)
                kl = kvloc[l][sg].ap()
                dk = d_kvloc[l][sg]
                for it in range(5):
                    T.dma(sp, kl[it * 128:(it + 1) * 128, :].rearrange("p (t e) -> p t e", e=PT)[:, tl0:tl0 + gs, 0:128],
                          kst[:, it, 0:n].rearrange("p (t e) -> p t e", e=128), b_kst, reads=[b_kst], writes=[dk])
                for h in range(4):
                    T.dma(sp, kl[(5 + h) * 128:(6 + h) * 128, tl0 * PT:(tl0 + gs) * PT].rearrange("p (t e) -> p t e", e=PT),
                          vst[:, h, 0:gs, :], b_vst, reads=[b_vst], writes=[dk])
                T.dma(sp, kl[9 * 128:10 * 128, tl0 * PT:(tl0 + gs) * PT].rearrange("p (t c e) -> p t c e", c=2, e=65),
                      vgst[:, 0:gs, :, :], b_vgst, reads=[b_vgst], writes=[dk])
                if m < 4:
                    g_in_b = (t0 - 2 - m * LT) // GSL
                    ul = uloc[l].ap()
                    if g_in_b == 0:
                        T.dma(sp, ul[0:8, (m * 2) * 512:(m * 2 + 1) * 512], ust[0:8, 0, :], b_ust,
                              reads=[b_ust], writes=[d_uloc[l]])
                    if g_in_b == NGB - 1:
                        T.dma(sp, ul[0:8, (m * 2 + 1) * 512:(m * 2 + 2) * 512], ust[120:128, gs - 1, :],
                              b_ust, reads=[b_ust], writes=[d_uloc[l]])
                    if g_in_b == NGB - 1:
                        gather(kvloc[l][sg], kvall[l][sg], d_kvall[l][sg], [b_kst, b_vst, b_vgst])
                        if m == NB - 1:
                            gather(uloc[l], uall[l], d_uall[l], [b_ust])
                else:
                    gather(kvloc[l][sg], kvall[l][sg], d_kvall[l][sg], [b_kst, b_vst, b_vgst])
            T.barrier()
            T.release(phase_bufs)

        with ExitStack() as es:
            kt_t = es.enter_context(nc.sbuf_tensor("kt%d" % l, [128, 2, NKT, PT], BF16))
            vt_t = es.enter_context(nc.sbuf_tensor("vt%d" % l, [128, 2, NKT, PT], BF16))
            qt_t = es.enter_context(nc.sbuf_tensor("qt%d" % l, [128, 4, max(LT, 2) * 128], BF16))
            pt_t = es.enter_context(nc.sbuf_tensor("pt%d" % l, [128, 3, 2, GS * 128], BF16))
            aost_t = es.enter_context(nc.sbuf_tensor("aost%d" % l, [128, 2, GS, 128], BF16))
            pw_t = es.enter_context(nc.sbuf_tensor("pw%d" % l, [128, 4, 8], F32))
            o1_t = es.enter_context(nc.sbuf_tensor("o1%d" % l, [128, 4, 128], F32))
            o2_t = es.enter_context(nc.sbuf_tensor("o2%d" % l, [128, 4, 128], F32))
            osb_t = es.enter_context(nc.sbuf_tensor("osb%d" % l, [128, 2, 8, 129], F32))
            dl_t = es.enter_context(nc.sbuf_tensor("dl%d" % l, [128, 256], F32))
            lam_t = es.enter_context(nc.sbuf_tensor("lam%d" % l, [128, 8], F32))
            gsub_t = es.enter_context(nc.sbuf_tensor("gsub%d" % l, [128, 128], F32))
            kt, vt, qt, pt, aost, pw, o1, o2, dl, lam, gsub = (kt_t.ap(), vt_t.ap(), qt_t.ap(), pt_t.ap(), aost_t.ap(),
                                                             pw_t.ap(), o1_t.ap(), o2_t.ap(), dl_t.ap(), lam_t.ap(),
                                                             gsub_t.ap())
            b_kt = [Buf("kt0"), Buf("kt1")]
            b_vt = [Buf("vt0"), Buf("vt1")]
            b_qt = [Buf("qt%d" % i) for i in range(4)]
            b_pt = [Buf("pt%d" % i) for i in range(3)]
            b_aost = [Buf("aost0"), Buf("aost1")]
            b_dl, b_lam, b_gsub = [Buf(n) for n in ("dl", "lam", "gsub")]
            b_pw = [Buf("pw%d" % i) for i in range(4)]
            b_o1 = [Buf("o1%d" % i) for i in range(4)]
            b_o2 = [Buf("o2%d" % i) for i in range(4)]
            b_osb = [Buf("osb0"), Buf("osb1")]
            osb = osb_t.ap()
            phase_bufs = b_kt + b_vt + b_qt + b_aost + [b_dl, b_gsub]
            T.dma(sp, dl, dlam.ap()[l].partition_broadcast(128), b_dl, writes=[b_dl])
            T.dma(sp, gsub, subln.ap()[l].partition_broadcast(128), b_gsub, writes=[b_gsub])
            for i in range(2):
                T.op(dve, lambda i=i: nc.vector.tensor_tensor(out=o1[:, 0, 0:64], in0=dl[:, i * 128:i * 128 + 64],
                                                              in1=dl[:, i * 128 + 64:i * 128 + 128], op=ALU.mult),
                     reads=[b_dl], writes=[b_o1[0]])
                T.op(dve, lambda i=i: nc.vector.tensor_reduce(out=lam[:, i:i + 1], in_=o1[:, 0, 0:64], axis=AX.X, op=ALU.add),
                     reads=[b_o1[0]], writes=[b_lam])
            T.op(act, lambda: nc.scalar.activation(out=lam[:, 2:4], in_=lam[:, 0:2], func=AF.Exp), reads=[b_lam], writes=[b_lam])
            T.op(dve, lambda: nc.vector.tensor_tensor(out=lam[:, 4:5], in0=lam[:, 3:4], in1=lam[:, 2:3], op=ALU.subtract),
                 reads=[b_lam], writes=[b_lam])
            T.op(dve, lambda: nc.vector.tensor_scalar(out=lam[:, 5:6], in0=lam[:, 4:5], scalar1=-lam_init, scalar2=None,
                                                      op0=ALU.add), reads=[b_lam], writes=[b_lam])
            T.op(dve, lambda: nc.vector.tensor_scalar(out=gsub, in0=gsub, scalar1=1.0 - lam_init, scalar2=None, op0=ALU.mult),
                 reads=[b_gsub], writes=[b_gsub])
            neglam = lam[:, 5:6]
            ctr = {"kv": 0, "q": 0, "pt": 0, "s": 0, "ao": 0}

            def load_kv(dst, bdst, slot, item, b, local):
                if local:
                    T.dma(sp, dst[:, slot, 0:2, :], kvloc[l][4].ap()[item * 128:(item + 1) * 128, :].rearrange("p (t e) -> p t e", e=PT),
                          bdst[slot], reads=[d_kvloc[l][4]], writes=[bdst[slot]])
                    return
                T.dma(sp, dst[:, slot, 0:2, :],
                      kvall[l][4].ap()[2 * b * R10 + item * 128:2 * b * R10 + (item + 1) * 128, :].rearrange("p (t e) -> p t e", e=PT),
                      bdst[slot], reads=[d_kvall[l][4]], writes=[bdst[slot]])
                T.dma(sp, dst[:, slot, 2:NKT, :].rearrange("p (r t) e -> p r (t e)", r=NCORE),
                      kvall[l][b].ap().rearrange("(r q) n -> q r n", q=R10)[item * 128:(item + 1) * 128, :, :],
                      bdst[slot], reads=[d_kvall[l][b]], writes=[bdst[slot]])

            def attend(kslot, vslot, qslot, q0, nq, nkt, dv, vsel, ao_cols):
                QB = nq * 128
                PI = 129

                def oacc(c, j):
                    idx = c * nq + j
                    return PSF[:, 4 + idx // 3, (idx % 3) * PI:(idx % 3) * PI + dv + 1], bank[4 + idx // 3]

                def qk_step(i):
                    sbi = ctr["s"] % 2
                    ctr["s"] += 1
                    for c in range(2):
                        T.op(pe, lambda c=c: nc.tensor.matmul(PSF[:, 2 * sbi + c, 0:QB], lhsT=kt[c * 64:(c + 1) * 64, kslot, i, 0:128],
                                                              rhs=qt[c * 64:(c + 1) * 64, qslot, q0:q0 + QB], start=True, stop=True),
                             reads=[b_kt[kslot], b_qt[qslot]], writes=[bank[2 * sbi + c]], inc=(c == 1))
                    return sbi

                def exp_step(sbi):
                    pi = ctr["pt"] % 3
                    ctr["pt"] += 1
                    T.op(act, lambda: nc.scalar.activation(out=pt[:, pi, :, 0:QB], in_=PSF[:, 2 * sbi:2 * sbi + 2, 0:QB],
                                                           func=AF.Exp, scale=0.125),
                         reads=[bank[2 * sbi], bank[2 * sbi + 1]], writes=[b_pt[pi]])
                    return pi

                def av_step(i, pi):
                    for c in range(2):
                        for j in range(nq):
                            o_ap, o_b = oacc(c, j)
                            T.op(pe, lambda c=c, j=j, o_ap=o_ap: nc.tensor.matmul(
                                o_ap, lhsT=pt[:, pi, c, j * 128:(j + 1) * 128], rhs=vsel(vslot, i, c),
                                start=(i == 0), stop=(i == nkt - 1)),
                                reads=[b_pt[pi], b_vt[vslot]], writes=[o_b], inc=(c == 1 and j == nq - 1))

                sq_ = [qk_step(i) for i in range(min(2, nkt))]
                for i in range(nkt):
                    pi = exp_step(sq_[i])
                    if i + 2 < nkt:
                        sq_.append(qk_step(i + 2))
                    av_step(i, pi)
                asl = ctr["ao"] % 2
                ctr["ao"] += 1
                for idx in range(2 * nq):
                    o_ap, o_b = oacc(idx // nq, idx % nq)
                    T.op(dve, lambda: nc.vector.tensor_copy(out=osb[:, asl, idx, 0:dv + 1], in_=o_ap), reads=[o_b],
                         writes=[b_osb[asl]])
                for j in range(nq):
                    o0 = osb[:, asl, j, :]
                    o1a = osb[:, asl, nq + j, :]
                    pwj = pw[:, j, :]
                    rd = [b_osb[asl]]
                    T.op(dve, lambda: nc.vector.reciprocal(out=pwj[:, 0:1], in_=o0[:, dv:dv + 1]), reads=rd, writes=[b_pw[j]])
                    T.op(dve, lambda: nc.vector.reciprocal(out=pwj[:, 1:2], in_=o1a[:, dv:dv + 1]), reads=rd, writes=[b_pw[j]])
                    if dv == 128:
                        T.op(dve, lambda: nc.vector.tensor_scalar(out=o1[:, j, :], in0=o1a[:, 0:128], scalar1=pwj[:, 1:2], scalar2=neglam,
                                                                  op0=ALU.mult, op1=ALU.mult),
                             reads=rd + [b_pw[j], b_lam], writes=[b_o1[j]])
                        T.op(dve, lambda: nc.vector.scalar_tensor_tensor(out=o2[:, j, :], in0=o0[:, 0:128], scalar=pwj[:, 0:1], in1=o1[:, j, :],
                                                                         op0=ALU.mult, op1=ALU.add),
                             reads=rd + [b_pw[j], b_o1[j]], writes=[b_o2[j]])
                        T.op(pool, lambda: nc.gpsimd.tensor_tensor(out=o1[:, j, :], in0=o2[:, j, :], in1=o2[:, j, :], op=ALU.mult),
                             reads=[b_o2[j]], writes=[b_o1[j]])
                        T.op(dve, lambda: nc.vector.tensor_reduce(out=pwj[:, 2:3], in_=o1[:, j, :], axis=AX.X, op=ALU.add),
                             reads=[b_o1[j]], writes=[b_pw[j]])
                        T.op(act, lambda: nc.scalar.activation(out=pwj[:, 3:4], in_=pwj[:, 2:3], func=AF.Ln, scale=1.0 / 128, bias=epsb),
                             reads=[b_pw[j], b_eps], writes=[b_pw[j]])
                        T.op(act, lambda: nc.scalar.activation(out=pwj[:, 4:5], in_=pwj[:, 3:4], func=AF.Exp, scale=-0.5),
                             reads=[b_pw[j]], writes=[b_pw[j]])
                        T.op(dve, lambda: nc.vector.scalar_tensor_tensor(out=aost[:, asl, j, :], in0=o2[:, j, :], scalar=pwj[:, 4:5],
                                                                         in1=gsub, op0=ALU.mult, op1=ALU.mult),
                             reads=[b_o2[j], b_pw[j], b_gsub], writes=[b_aost[asl]])
                    else:
                        T.op(dve, lambda: nc.vector.tensor_scalar(out=aost[:, asl, j, 0:64], in0=o0[:, 0:64], scalar1=pwj[:, 0:1],
                                                                  scalar2=None, op0=ALU.mult),
                             reads=rd + [b_pw[j]], writes=[b_aost[asl]])
                        T.op(dve, lambda: nc.vector.tensor_scalar(out=aost[:, asl, j, 64:128], in0=o1a[:, 0:64], scalar1=pwj[:, 1:2],
                                                                  scalar2=None, op0=ALU.mult),
                             reads=rd + [b_pw[j]], writes=[b_aost[asl]])
                aor = ao_d.ap().rearrange("(t p) c -> p t c", p=128)
                tq0 = ao_cols[0]
                if dv == 128:
                    T.dma(pool, aor[:, tq0:tq0 + nq, ao_cols[1]:ao_cols[1] + 128], aost[:, asl, 0:nq, :], b_aost[asl],
                          reads=[b_aost[asl]], writes=[d_ao])
                else:
                    for c in range(2):
                        T.dma(pool, aor[:, tq0:tq0 + nq, ao_cols[1 + c]:ao_cols[1 + c] + 64], aost[:, asl, 0:nq, c * 64:(c + 1) * 64],
                              b_aost[asl], reads=[b_aost[asl]], writes=[d_ao])

            vsel_d = lambda vslot, i, c: vt[:, vslot, i, 0:129]
            vsel_g = lambda vslot, i, c: vt[:, vslot, i, c * 65:c * 65 + 65]
            qsrc = qT_d.ap()

            seqs = ([(0, True)] if not last else []) + [(b, False) for b in range(NB)]
            kgroups = []
            for (b, local) in seqs:
                for h in range(4):
                    kgroups.append(dict(b=b, local=local, kitem=h, vitem=5 + h, qs=[h], dv=128))
                kgroups.append(dict(b=b, local=local, kitem=4, vitem=9, qs=[4, 5, 6, 7], dv=64))
            jobs = [(gi, q) for gi, g in enumerate(kgroups) for q in g["qs"]]

            def geom(g):
                if g["local"]:
                    return 0, 2, 2
                return (2 + g["b"] * LT) * 128, LT, NKT

            def issue_kv(gi):
                g = kgroups[gi]
                load_kv(kt, b_kt, gi % 2, g["kitem"], g["b"], g["local"])
                load_kv(vt, b_vt, gi % 2, g["vitem"], g["b"], g["local"])

            def issue_q(ji):
                gi, q = jobs[ji]
                qtok0, nqt, nkt = geom(kgroups[gi])
                T.dma(sp, qt[:, ji % 4, 0:nqt * 128], qsrc[q, :, qtok0:qtok0 + nqt * 128], b_qt[ji % 4], reads=[d_qT],
                      writes=[b_qt[ji % 4]])

            issue_kv(0)
            issue_q(0)
            seen_g = set()
            for ji, (gi, q) in enumerate(jobs):
                g = kgroups[gi]
                if gi not in seen_g:
                    seen_g.add(gi)
                    if gi + 1 < len(kgroups):
                        issue_kv(gi + 1)
                if ji + 1 < len(jobs):
                    issue_q(ji + 1)
                qtok0, nqt, nkt = geom(g)
                qblocks = [(qb * GS, min(GS, nqt - qb * GS)) for qb in range((nqt + GS - 1) // GS)]
                for (qj, nq) in qblocks:
                    if g["dv"] == 128:
                        attend(gi % 2, gi % 2, ji % 4, qj * 128, nq, nkt, 128, vsel_d, (qtok0 // 128 + qj, q * 128))
                    else:
                        r = q - 4
                        attend(gi % 2, gi % 2, ji % 4, qj * 128, nq, nkt, 64, vsel_g,
                               (qtok0 // 128 + qj, 512 + r * 64, 512 + (4 + r) * 64))
            T.barrier()
            T.release(phase_bufs)

        with ExitStack() as es:
            WC_t = es.enter_context(nc.sbuf_tensor("WC%d" % l, [128, 8, 4608], BF16))
            WB_t = es.enter_context(nc.sbuf_tensor("WB%d" % l, [128, 12, D], BF16))
            WO_t = es.enter_context(nc.sbuf_tensor("WO%d" % l, [128, 8, D], BF16))
            PW_t = es.enter_context(nc.sbuf_tensor("PW%d" % l, [128, 4, 128], BF16))
            psT_t = es.enter_context(nc.sbuf_tensor("psT%d" % l, [128, 4], F32))
            Uh_t = es.enter_context(nc.sbuf_tensor("Uh%d" % l, [64, NB * 2 * 512], BF16))
            hTc_t = es.enter_context(nc.sbuf_tensor("hTc%d" % l, [128, 8, GC * 128], BF16))
            aog_t = es.enter_context(nc.sbuf_tensor("aog%d" % l, [128, GC, D], BF16))
            ug_t = es.enter_context(nc.sbuf_tensor("ug%d" % l, [128, GC + 2, 512], BF16))
            plT_t = es.enter_context(nc.sbuf_tensor("plT%d" % l, [128, 4, GC * 128], BF16))
            sig_t = es.enter_context(nc.sbuf_tensor("sig%d" % l, [128, 2, GC * 128], F32))
            sz_t = es.enter_context(nc.sbuf_tensor("sz%d" % l, [128, 2, GC * 128], BF16))
            bT_t = es.enter_context(nc.sbuf_tensor("bT%d" % l, [128, 12, GC * 128], BF16))
            sg_t = es.enter_context(nc.sbuf_tensor("sg%d" % l, [128, 3, GC * 128], BF16))
            acc_t = es.enter_context(nc.sbuf_tensor("acc%d" % l, [128, 2, GC * 128], F32))
            yT_t = es.enter_context(nc.sbuf_tensor("yT%d" % l, [128, 8, GC * 128], BF16))
            gbc_t = es.enter_context(nc.sbuf_tensor("gbc%d" % l, [128, D], F32))
            xc_t = es.enter_context(nc.sbuf_tensor("xc%d" % l, [128, 2, D], F32))
            xo_t = es.enter_context(nc.sbuf_tensor("xo%d" % l, [128, 2, D], F32))
            WC, WB, WO, PW, psT, Uh, hTc, aog, ug, plT = (WC_t.ap(), WB_t.ap(), WO_t.ap(), PW_t.ap(), psT_t.ap(), Uh_t.ap(),
                                                       hTc_t.ap(), aog_t.ap(), ug_t.ap(), plT_t.ap())
            sig, sz, bT, sg, acc, yT, gbc, xc, xo = (sig_t.ap(), sz_t.ap(), bT_t.ap(), sg_t.ap(), acc_t.ap(), yT_t.ap(),
                                                  gbc_t.ap(), xc_t.ap(), xo_t.ap())
            b_WC = [Buf("WC%d" % i) for i in range(9)]
            b_WB = [Buf("WB%d" % i) for i in range(3)]
            b_WO, b_PW, b_psT, b_Uh, b_hTc, b_aog, b_ug, b_plT = [Buf(n) for n in ("WO", "PW", "psT", "Uh", "hTc", "aog", "ug", "plT")]
            b_sig = [Buf("sig0"), Buf("sig1")]
            b_sz = [Buf("sz0"), Buf("sz1")]
            b_bT, b_yT, b_gbc = Buf("bT"), Buf("yT"), Buf("gbc")
            b_sg = [Buf("sg%d" % i) for i in range(3)]
            b_acc = [Buf("acc0"), Buf("acc1")]
            b_xc = [Buf("xc0"), Buf("xc1")]
            b_xo = [Buf("xo0"), Buf("xo1")]
            phase_bufs = b_WC + b_WB + [b_WO, b_PW, b_psT, b_Uh, b_hTc, b_aog, b_ug] + b_xc + b_xo
            wsrc = w_in.ap()[l].rearrange("(k p) n -> p k n", p=128)
            csegs = [(0, 512, 512), (512, 2560, 512), (1024, 3840, 512)] + [(1536 + i * 512, 4352 + i * 512, 512) for i in range(6)]
            for i, (do, so, n_) in enumerate(csegs):
                T.dma(pool, WC[:, :, do:do + n_], wsrc[:, :, so:so + n_], b_WC[i], writes=[b_WC[i]])
            for i in range(3):
                T.dma(pool, WB[:, i * 4:(i + 1) * 4, :], w_br.ap()[l, i].rearrange("(k p) n -> p k n", p=128), b_WB[i], writes=[b_WB[i]])
            T.dma(pool, WO, w_out.ap()[l].rearrange("(k p) n -> p k n", p=128), b_WO, writes=[b_WO])
            T.dma(pool, PW, pool_w.ap()[l].rearrange("g c d -> c g d"), b_PW, writes=[b_PW])
            T.dma(sp, psT, pool_sT.ap()[l], b_psT, writes=[b_psT])
            T.dma(sp, Uh[0:64, :], uall[l].ap(), b_Uh, reads=[d_uall[l]], writes=[b_Uh])
            bctr = [0]

            def nb():
                i = bctr[0] % 8
                bctr[0] += 1
                return i

            def WCall(k):
                return b_WC

            gcl = min(2, LT)
            groups = [(2 + b * LT + g * gcl, gcl, b, g) for b in range(NB) for g in range(LT // gcl)]
            if not last:
                groups = [(0, 2, 4, 0)] + groups
            xdst = yout.ap() if last else x1_d.ap()
            for (t0, gs, m, gi) in groups:
                n = gs * 128
                tk = slice(t0 * 128, t0 * 128 + n)
                T.dma(sp, hTc[:, :, 0:n], hT_d.ap().rearrange("k p t -> p k t")[:, :, tk], b_hTc, reads=[d_hT], writes=[b_hTc])
                T.dma(sp, aog[:, 0:gs, :], ao_d.ap().rearrange("(t p) c -> p t c", p=128)[:, t0:t0 + gs, :], b_aog,
                      reads=[d_ao], writes=[b_aog])
                seq0, seq1 = (0, 2) if m == 4 else (2 + m * LT, 2 + m * LT + LT)
                lo, hi = max(t0 - 1, seq0), min(t0 + gs + 1, seq1)
                T.dma(sp, ug[:, lo - (t0 - 1):hi - (t0 - 1), :], u_d.ap().rearrange("(t p) c -> p t c", p=128)[:, lo:hi, :], b_ug,
                      reads=[d_u], writes=[b_ug])
                for hf in range(2):
                    bi = nb()
                    T.op(pe, lambda: nc.tensor.matmul(PSF[:, bi, :], lhsT=sel[:, m, :], rhs=mr[:, hf * 512:(hf + 1) * 512],
                                                      start=True, stop=True), reads=[b_sel, b_mr], writes=[bank[bi]])
                    T.op(act, lambda: nc.scalar.copy(out=gbc[:, hf * 512:(hf + 1) * 512], in_=PSF[:, bi, :]), reads=[bank[bi]],
                         writes=[b_gbc])
                for g in range(4):
                    bi = nb()
                    for j in range(gs):
                        tile = t0 + j
                        srcs = []
                        if m == 4:
                            srcs.append((ug[:, j + 1, g * 128:(g + 1) * 128], poolA[:, g, 3 + j, :]))
                            if j == 0:
                                srcs.append((ug[:, j + 2, g * 128:(g + 1) * 128], poolA[:, g, 6, :]))
                            else:
                                srcs.append((ug[:, j, g * 128:(g + 1) * 128], poolA[:, g, 5, :]))
                        else:
                            tb = tile - (2 + m * LT)
                            slot = 0 if tb == 0 else (2 if tb == LT - 1 else 1)
                            srcs.append((ug[:, j + 1, g * 128:(g + 1) * 128], poolA[:, g, slot, :]))
                            if tb > 0:
                                srcs.append((ug[:, j, g * 128:(g + 1) * 128], poolA[:, g, 5, :]))
                            else:
                                srcs.append((Uh[0:64, (m * 2 + 1) * 512 + g * 128:(m * 2 + 1) * 512 + (g + 1) * 128],
                                             poolA[0:64, g, 7, :]))
                            if tb < LT - 1:
                                srcs.append((ug[:, j + 2, g * 128:(g + 1) * 128], poolA[:, g, 6, :]))
                            else:
                                srcs.append((Uh[0:64, (m * 2) * 512 + g * 128:(m * 2) * 512 + (g + 1) * 128],
                                             poolA[0:64, g, 8, :]))
                        for si, (lh, rh) in enumerate(srcs):
                            T.op(pe, lambda lh=lh, rh=rh, si=si: nc.tensor.matmul(PSF[:, bi, j * 128:(j + 1) * 128], lhsT=lh, rhs=rh,
                                                                                  start=(si == 0), stop=(si == len(srcs) - 1)),
                                 reads=[b_ug, b_Uh, b_poolA], writes=[bank[bi]], inc=(si == len(srcs) - 1))
                    T.op(act, lambda: nc.scalar.copy(out=plT[:, g, 0:n], in_=PSF[:, bi, 0:n]), reads=[bank[bi]], writes=[b_plT])

                def zgate(zc, slot):
                    bi = nb()
                    for k in range(8):
                        T.op(pe, lambda k=k: nc.tensor.matmul(PSF[:, bi, 0:n], lhsT=WC[:, k, zc * 128:(zc + 1) * 128], rhs=hTc[:, k, 0:n],
                                                              start=(k == 0), stop=(k == 7)),
                             reads=[b_hTc] + b_WC, writes=[bank[bi]], inc=(k == 7))
                    T.op(act, lambda: nc.scalar.activation(out=sig[:, slot, 0:n], in_=PSF[:, bi, 0:n], func=AF.Sigmoid),
                         reads=[bank[bi]], writes=[b_sig[slot]])
                    T.op(dve, lambda: nc.vector.tensor_tensor(out=sz[:, slot, 0:n], in0=PSF[:, bi, 0:n], in1=sig[:, slot, 0:n],
                                                              op=ALU.mult), reads=[bank[bi], b_sig[slot]], writes=[b_sz[slot]])

                zi = 0
                for g in range(4):
                    slot = zi % 2
                    zi += 1
                    zgate(g, slot)
                    bi = nb()
                    T.op(pe, lambda: nc.tensor.matmul(PSF[:, bi, 0:n], lhsT=PW[:, g, :], rhs=plT[:, g, 0:n], start=True, stop=True),
                         reads=[b_PW, b_plT], writes=[bank[bi]])
                    T.op(dve, lambda: nc.vector.scalar_tensor_tensor(out=bT[:, g, 0:n], in0=PSF[:, bi, 0:n], scalar=psT[:, g:g + 1],
                                                                     in1=sz[:, slot, 0:n], op0=ALU.mult, op1=ALU.mult),
                         reads=[bank[bi], b_psT, b_sz[slot]], writes=[b_bT])
                for ch in range(8):
                    slot = zi % 2
                    zi += 1
                    zgate(4 + ch, slot)
                    bi = nb()
                    for j in range(gs):
                        T.op(pe, lambda j=j: nc.tensor.transpose(out=PSB[:, bi, j * 128:(j + 1) * 128],
                                                                 in_=aog[:, j, ch * 128:(ch + 1) * 128], identity=ident),
                             reads=[b_aog, b_ident], writes=[bank[bi]], inc=(j == gs - 1))
                    T.op(dve, lambda: nc.vector.tensor_tensor(out=bT[:, 4 + ch, 0:n], in0=PSB[:, bi, 0:n], in1=sz[:, slot, 0:n],
                                                              op=ALU.mult), reads=[bank[bi], b_sz[slot]], writes=[b_bT])
                for oc in range(8):
                    a = oc % 2
                    for i in range(3):
                        bi = nb()
                        for k in range(8):
                            c0 = 1536 + i * 1024 + oc * 128
                            T.op(pe, lambda k=k: nc.tensor.matmul(PSF[:, bi, 0:n], lhsT=WC[:, k, c0:c0 + 128], rhs=hTc[:, k, 0:n],
                                                                  start=(k == 0), stop=(k == 7)),
                                 reads=[b_hTc] + b_WC, writes=[bank[bi]], inc=(k == 7))
                        T.op(act, lambda: nc.scalar.activation(out=sg[:, i, 0:n], in_=PSF[:, bi, 0:n], func=AF.Sigmoid),
                             reads=[bank[bi]], writes=[b_sg[i]])
                        bi2 = nb()
                        for kk in range(4):
                            T.op(pe, lambda kk=kk: nc.tensor.matmul(PSF[:, bi2, 0:n], lhsT=WB[:, i * 4 + kk, oc * 128:(oc + 1) * 128],
                                                                    rhs=bT[:, i * 4 + kk, 0:n], start=(kk == 0), stop=(kk == 3)),
                                 reads=[b_bT] + b_WB, writes=[bank[bi2]], inc=(kk == 3))
                        if i == 0:
                            T.op(dve, lambda: nc.vector.tensor_tensor(out=acc[:, a, 0:n], in0=PSF[:, bi2, 0:n], in1=sg[:, i, 0:n],
                                                                      op=ALU.mult), reads=[bank[bi2], b_sg[i]], writes=[b_acc[a]])
                        else:
                            T.op(dve, lambda: nc.vector.tensor_tensor(out=sig[:, 0, 0:n], in0=PSF[:, bi2, 0:n], in1=sg[:, i, 0:n],
                                                                      op=ALU.mult), reads=[bank[bi2], b_sg[i]], writes=[b_sig[0]])
                            if i == 1:
                                T.op(pool, lambda: nc.gpsimd.tensor_tensor(out=acc[:, a, 0:n], in0=acc[:, a, 0:n], in1=sig[:, 0, 0:n],
                                                                           op=ALU.add), reads=[b_acc[a], b_sig[0]], writes=[b_acc[a]])
                            else:
                                T.op(pool, lambda: nc.gpsimd.tensor_tensor(out=yT[:, oc, 0:n], in0=acc[:, a, 0:n], in1=sig[:, 0, 0:n],
                                                                           op=ALU.add), reads=[b_acc[a], b_sig[0]], writes=[b_yT])
                for j in range(gs):
                    tile = t0 + j
                    s = tile % 2
                    T.dma(sp, xc[:, s, :], xsrc[tile * 128:(tile + 1) * 128, :], b_xc[s], reads=[d_xsrc] if d_xsrc else [],
                          writes=[b_xc[s]])
                    for hf in range(2):
                        bi = nb()
                        for k in range(8):
                            T.op(pe, lambda k=k: nc.tensor.matmul(PSF[:, bi, :], lhsT=yT[:, k, j * 128:(j + 1) * 128],
                                                                  rhs=WO[:, k, hf * 512:(hf + 1) * 512], start=(k == 0), stop=(k == 7)),
                                 reads=[b_yT, b_WO], writes=[bank[bi]], inc=(k == 7))
                        T.op(dve, lambda: nc.vector.tensor_tensor(out=xo[:, s, hf * 512:(hf + 1) * 512], in0=PSF[:, bi, :],
                                                                  in1=gbc[:, hf * 512:(hf + 1) * 512], op=ALU.mult),
                             reads=[bank[bi], b_gbc], writes=[b_xo[s]])
                    T.op(pool, lambda: nc.gpsimd.tensor_tensor(out=xo[:, s, :], in0=xo[:, s, :], in1=xc[:, s, :], op=ALU.add),
                         reads=[b_xo[s], b_xc[s]], writes=[b_xo[s]])
                    if last:
                        orow = (tile - 2) * 128
                        T.dma(sp, xdst[orow:orow + 128, :], xo[:, s, :], b_xo[s], reads=[b_xo[s]], writes=[])
                    else:
                        T.dma(sp, xdst[tile * 128:(tile + 1) * 128, :], xo[:, s, :], b_xo[s], reads=[b_xo[s]], writes=[d_x1])
            T.barrier()
            T.release(phase_bufs)
    T.barrier()
    return nc


def _pool_mat(in_pos, out_pos, w, n):
    left = w // 2
    right = w - 1 - left
    A = np.zeros((len(in_pos), len(out_pos)), np.float32)
    for oi, t in enumerate(out_pos):
        if t < 0 or t >= n:
            continue
        lo, hi = max(t - left, 0), min(t + right + 1, n)
        for ii, s in enumerate(in_pos):
            if lo <= s < hi:
                A[ii, oi] += 1.0 / (hi - lo)
            if s == t:
                A[ii, oi] -= 1.0
    return A


def _host_consts(c, LT, seq):
    NT = 2 + NB * LT
    inv = np.power(np.float32(10000.0), -np.arange(0, 32, 2, dtype=np.float32) / np.float32(32)).astype(np.float32)
    rope = np.zeros((NT * 128, 128), np.float32)
    rope[:, 0:64] = 1.0
    pos = c * LT * 128 + np.arange(LT * 128)
    row = (pos // GRID_W).astype(np.float32)
    col = (pos % GRID_W).astype(np.float32)
    ar = (row[:, None] * inv[None, :]).astype(np.float32)
    ac = (col[:, None] * inv[None, :]).astype(np.float32)
    cc = np.zeros((LT * 128, 2, 2, 16), np.float32)
    ssn = np.zeros((LT * 128, 2, 2, 16), np.float32)
    for a, ang in enumerate((ar, ac)):
        cc[:, a, 0] = np.cos(ang)
        cc[:, a, 1] = np.cos(ang)
        ssn[:, a, 0] = -np.sin(ang)
        ssn[:, a, 1] = np.sin(ang)
    for b in range(NB):
        r0 = (2 + b * LT) * 128
        rope[r0:r0 + LT * 128, 0:64] = cc.reshape(-1, 64)
        rope[r0:r0 + LT * 128, 64:128] = ssn.reshape(-1, 64)
    pa = np.zeros((128, 4, 9, 128), np.float32)
    ar128 = np.arange(128)
    g0 = c * LT * 128
    for g, w in enumerate(WIN):
        pa[:, g, 0] = _pool_mat(g0 + ar128, g0 + ar128, w, seq)
        mid = g0 + 128 if LT > 2 else seq // 2 // 128 * 128
        pa[:, g, 1] = _pool_mat(mid + ar128, mid + ar128, w, seq)
        gl = g0 + (LT - 1) * 128
        pa[:, g, 2] = _pool_mat(gl + ar128, gl + ar128, w, seq)
        pa[:, g, 3] = _pool_mat(ar128, ar128, w, CTX)
        pa[:, g, 4] = _pool_mat(128 + ar128, 128 + ar128, w, CTX)
        pa[:, g, 5] = _pool_mat(ar128, 128 + ar128, w, 1 << 30)
        pa[:, g, 6] = _pool_mat(128 + ar128, ar128, w, 1 << 30)
        hp_in = np.concatenate([(r + 1) * LT * 128 - 8 + np.arange(8) for r in range(NCORE)])
        hp_in = np.where(np.repeat(np.arange(NCORE), 8) == c - 1, hp_in, -10 ** 6)
        pa[0:64, g, 7] = _pool_mat(hp_in, g0 + ar128, w, seq)
        hn_in = np.concatenate([r * LT * 128 + np.arange(8) for r in range(NCORE)])
        hn_in = np.where(np.repeat(np.arange(NCORE), 8) == c + 1, hn_in, -10 ** 6)
        pa[0:64, g, 8] = _pool_mat(hn_in, g0 + (LT - 1) * 128 + ar128, w, seq)
    return rope, pa


_NC_CACHE = {}


def kernel(x, c, ctx, c_ctx, ada_w, ada_b, norm_g, w_in, pool_w, pool_scale, diff_q_norm, diff_k_norm,
           diff_lambda, diff_subln, gqa_q_norm, gqa_k_norm, w_branch, w_out):
    f = lambda a: np.ascontiguousarray(np.asarray(a, dtype=np.float32))
    x, c, ctx, c_ctx = f(x), f(c), f(ctx), f(c_ctx)
    seq = x.shape[1]
    LT = seq // (NCORE * 128)
    if LT not in _NC_CACHE:
        _NC_CACHE[LT] = build(LT)
    nc = _NC_CACHE[LT]
    cmat = np.concatenate([c, c_ctx[None, :]], 0)
    shared = {
        "cT": f(cmat.reshape(5, 8, 128).transpose(2, 1, 0)),
        "ident": np.eye(128, dtype=np.float32),
        "sel": f(np.eye(5, dtype=np.float32)[:, :, None] * np.ones((1, 1, 128), np.float32)),
        "ada_w": f(ada_w),
        "ada_bT": f(np.asarray(ada_b).reshape(DEPTH, 24, 128).transpose(0, 2, 1)),
        "ada_b": f(ada_b),
        "norm_gT": f(np.asarray(norm_g).reshape(DEPTH, 8, 128).transpose(0, 2, 1)),
        "w_in": f(w_in),
        "pool_w": f(pool_w),
        "pool_sT": f(np.asarray(pool_scale).reshape(DEPTH, 4, 128).transpose(0, 2, 1)),
        "qkg": f(np.concatenate([np.asarray(diff_q_norm), np.asarray(diff_k_norm), np.asarray(gqa_q_norm),
                                 np.asarray(gqa_k_norm)], 1)),
        "dlam": f(np.asarray(diff_lambda).reshape(DEPTH, 256)),
        "subln": f(diff_subln),
        "w_branch": f(w_branch),
        "w_out": f(w_out),
    }
    in_maps = []
    for ci in range(NCORE):
        rope, pa = _host_consts(ci, LT, seq)
        xi = np.concatenate([ctx[ci // 2]] + [x[b, ci * LT * 128:(ci + 1) * LT * 128] for b in range(NB)], 0)
        d = dict(shared)
        d.update({"xin": f(xi), "rope": rope, "poolA": pa})
        in_maps.append(d)
    res = run_bass_kernel_spmd(nc, in_maps, core_ids=list(range(NCORE)))
    out = np.zeros((NB, seq, D), np.float32)
    for ci in range(NCORE):
        yy = np.asarray(res.results[ci]["y"]).reshape(NB, LT * 128, D)
        out[:, ci * LT * 128:(ci + 1) * LT * 128, :] = yy
    return out
```

```python
import math
from contextlib import ExitStack
import numpy as np
import concourse.bass as bass
import concourse.mybir as mybir
from concourse.bass_utils import run_bass_kernel_spmd

F32 = mybir.dt.float32
BF16 = mybir.dt.bfloat16
ALU = mybir.AluOpType
AF = mybir.ActivationFunctionType
AX = mybir.AxisListType

D = 1024
NB = 4
NCORE = 8
DEPTH = 2
GRID_W = 64
CTX = 256
EPS = 1e-6
WIN = (2, 4, 8, 16)
PT = 130
NITEM = 11
ROWS = NITEM * 128
IN_W = 7424
SAME_SYNC = True


class Buf:
    def __init__(self, name):
        self.name = name
        self.w = None
        self.r = {}


class Eng:
    def __init__(self, name, e, sem, same):
        self.name, self.e, self.sem, self.cnt, self.seen, self.same = name, e, sem, 0, {}, same


class Trk:
    def __init__(self, nc):
        self.nc = nc
        self.sems = {}
        self.pe = self._eng("pe", nc.tensor, False)
        self.act = self._eng("act", nc.scalar, SAME_SYNC)
        self.dve = self._eng("dve", nc.vector, SAME_SYNC)
        self.pool = self._eng("pool", nc.gpsimd, SAME_SYNC)
        self.sp = self._eng("sp", nc.sync, False)
        self.engs = [self.pe, self.act, self.dve, self.pool, self.sp]
        self.free_dsems = {}
        self.dcnt = {}
        self.nd = 0

    def _eng(self, name, e, same):
        s = self.nc.alloc_semaphore("s_" + name)
        self.sems[name] = s
        return Eng(name, e, name, same)

    def dsem(self, qn):
        fl = self.free_dsems.setdefault(qn, [])
        if fl and qn != "pool":
            return fl.pop()
        k = "d%s%d" % (qn, self.nd)
        self.nd += 1
        self.sems[k] = self.nc.alloc_semaphore("s_" + k)
        self.dcnt[k] = 0
        return k

    def _waits(self, eng, reads, writes):
        need = {}
        for b in reads:
            if b.w:
                need[b.w[0]] = max(need.get(b.w[0], 0), b.w[1])
        for b in writes:
            if b.w:
                need[b.w[0]] = max(need.get(b.w[0], 0), b.w[1])
            for k, v in b.r.items():
                need[k] = max(need.get(k, 0), v)
        for k, v in need.items():
            if k == eng.sem and not eng.same:
                continue
            if k == eng.sem and v > eng.cnt:
                continue
            if eng.seen.get(k, 0) < v:
                eng.e.wait_ge(self.sems[k], v)
                eng.seen[k] = v

    def op(self, eng, fn, reads=(), writes=(), inc=True):
        self._waits(eng, reads, writes)
        ins = fn()
        if inc:
            eng.cnt += 1
            ins.then_inc(self.sems[eng.sem], 1)
            val = eng.cnt
        else:
            val = eng.cnt + 1
        for b in reads:
            b.r[eng.sem] = max(b.r.get(eng.sem, 0), val)
        for b in writes:
            b.w = (eng.sem, val)
            b.r = {}
        return ins

    def dma(self, q, out, in_, owner, reads=(), writes=(), **kw):
        self._waits(q, reads, writes)
        if not hasattr(owner, "ds"):
            owner.ds = {}
        if q.name not in owner.ds or q.name == "pool":
            assert not (q.name == "pool" and "pool" in owner.ds), "one SW DMA per Buf: " + owner.name
            owner.ds[q.name] = self.dsem(q.name)
        k = owner.ds[q.name]
        q.e.dma_start(out=out, in_=in_, **kw).then_inc(self.sems[k], 16)
        self.dcnt[k] += 16
        val = self.dcnt[k]
        for b in reads:
            b.r[k] = max(b.r.get(k, 0), val)
        for b in writes:
            b.w = (k, val)
            b.r = {}

    def release(self, bufs):
        for b in bufs:
            if hasattr(b, "ds"):
                for qn, k in b.ds.items():
                    if qn != "pool":
                        self.free_dsems.setdefault(qn, []).append(k)
                del b.ds

    def barrier(self):
        for e in self.engs:
            for o in self.engs:
                if o is e or o is self.sp:
                    continue
                if e.seen.get(o.sem, 0) < o.cnt:
                    e.e.wait_ge(self.sems[o.sem], o.cnt)
                    e.seen[o.sem] = o.cnt
            for k, v in self.dcnt.items():
                if v and e.seen.get(k, 0) < v:
                    e.e.wait_ge(self.sems[k], v)
                    e.seen[k] = v


def build(LT):
    GSL = min(4, LT)
    GS = max(GSL, 2)
    NGB = LT // GSL
    GC = 2
    NT = 2 + NB * LT
    NTOK = NT * 128
    L = max(NT * PT, NB * 2 * 512)
    NKT = 2 + NCORE * LT
    nc = bass.Bass("TRN2", target_bir_lowering=False)

    def din(name, shape, dt=F32):
        return nc.dram_tensor(name, list(shape), dt, kind="ExternalInput")

    xin = din("xin", [NTOK, D])
    cT_d = din("cT", [128, 8, 5])
    rope_d = din("rope", [NTOK, 128])
    poolA_d = din("poolA", [128, 4, 9, 128])
    ident_d = din("ident", [128, 128])
    sel_d = din("sel", [5, 5, 128])
    ada_w = din("ada_w", [DEPTH, D, 3 * D])
    ada_bT = din("ada_bT", [DEPTH, 128, 24])
    ada_b = din("ada_b", [DEPTH, 3 * D])
    norm_gT = din("norm_gT", [DEPTH, 128, 8])
    w_in = din("w_in", [DEPTH, D, IN_W])
    pool_w = din("pool_w", [DEPTH, 4, 128, 128])
    pool_sT = din("pool_sT", [DEPTH, 128, 4])
    qkg = din("qkg", [DEPTH, 4 * 64])
    dlam = din("dlam", [DEPTH, 256])
    subln = din("subln", [DEPTH, 128])
    w_br = din("w_branch", [DEPTH, 3, 512, D])
    w_out = din("w_out", [DEPTH, D, D])
    yout = nc.dram_tensor("y", [NB * LT * 128, D], F32, kind="ExternalOutput")

    R10 = 10 * 128
    kvloc = [[nc.dram_tensor("kvloc%d_%d" % (l, sg), [R10, (2 if sg == 4 else LT) * PT], BF16) for sg in range(5)]
             for l in range(DEPTH)]
    kvall = [[nc.dram_tensor("kvall%d_%d" % (l, sg), [NCORE * R10, (2 if sg == 4 else LT) * PT], BF16) for sg in range(5)]
             for l in range(DEPTH)]
    uloc = [nc.dram_tensor("uloc%d" % l, [8, NB * 2 * 512], BF16) for l in range(DEPTH)]
    uall = [nc.dram_tensor("uall%d" % l, [NCORE * 8, NB * 2 * 512], BF16) for l in range(DEPTH)]
    qT_d = nc.dram_tensor("qT_d", [8, 128, NTOK], BF16)
    hT_d = nc.dram_tensor("hT_d", [8, 128, NTOK], BF16)
    u_d = nc.dram_tensor("u_d", [NTOK, 512], BF16)
    ao_d = nc.dram_tensor("ao_d", [NTOK, D], BF16)
    x1_d = nc.dram_tensor("x1_d", [NTOK, D], F32)
    d_kvloc = [[Buf("kvloc%d_%d" % (l, sg)) for sg in range(5)] for l in range(DEPTH)]
    d_kvall = [[Buf("kvall%d_%d" % (l, sg)) for sg in range(5)] for l in range(DEPTH)]
    d_uloc = [Buf("uloc%d" % l) for l in range(DEPTH)]
    d_uall = [Buf("uall%d" % l) for l in range(DEPTH)]
    d_qT, d_hT, d_u, d_ao, d_x1 = Buf("qT"), Buf("hT"), Buf("u"), Buf("ao"), Buf("x1")

    T = Trk(nc)
    pe, act, dve, pool, sp = T.pe, T.act, T.dve, T.pool, T.sp
    T.sems["cc"] = nc.alloc_semaphore("ccsem")
    cc_n = [0]

    def gather(src_t, dst_t, d_dst, store_bufs):
        for sbuf in store_bufs:
            for qn, k in getattr(sbuf, "ds", {}).items():
                v = T.dcnt[k]
                if pool.seen.get(k, 0) < v:
                    nc.gpsimd.wait_ge(T.sems[k], v)
                    pool.seen[k] = v
        nc.gpsimd.collective_compute("AllGather", ALU.bypass, replica_groups=[list(range(NCORE))],
                                     ins=[src_t.ap().opt()], outs=[dst_t.ap().opt()]).then_inc(T.sems["cc"])
        cc_n[0] += 1
        d_dst.w = ("cc", cc_n[0])
        d_dst.r = {}

    PS = nc.alloc_psum_tensor("PS", [128, 8, 512], F32)
    PSB = PS.ap().bitcast(BF16)
    PSF = PS.ap()
    bank = [Buf("bank%d" % i) for i in range(8)]

    def sb(name, shape, dt):
        t = nc.alloc_sbuf_tensor("sb_" + name, list(shape), dt)
        return t.ap(), Buf(name)

    ident, b_ident = sb("ident", [128, 128], BF16)
    sel, b_sel = sb("sel", [5, 5, 128], F32)
    cT, b_cT = sb("cT", [128, 8, 5], F32)
    scT, b_scT = sb("scT", [128, 8, 5], F32)
    sgc, b_sgc = sb("sgc", [128, 8, 5], F32)
    modrows = [sb("modrows%d" % l, [5, D], F32) for l in range(DEPTH)]
    modT = [sb("modT%d" % l, [128, 24, 5], F32) for l in range(DEPTH)]
    geffT = [sb("geffT%d" % l, [128, 8, 5], F32) for l in range(DEPTH)]
    abT, b_abT = sb("abT", [128, DEPTH, 24], F32)
    ngT, b_ngT = sb("ngT", [128, DEPTH, 8], F32)
    poolA, b_poolA = sb("poolA", [128, 4, 9, 128], BF16)
    epsb, b_eps = sb("epsb", [128, 1], F32)
    pro = ExitStack()
    abrow = pro.enter_context(nc.sbuf_tensor("sb_abrow", [5, DEPTH, 3 * D], F32)).ap()
    b_abrow = Buf("abrow")
    mrs = pro.enter_context(nc.sbuf_tensor("sb_mrs", [5, 512], F32)).ap()
    b_mrs = Buf("mrs")

    T.dma(pool, ident, ident_d.ap(), b_ident, writes=[b_ident])
    T.dma(pool, poolA, poolA_d.ap(), b_poolA, writes=[b_poolA])
    T.dma(sp, sel, sel_d.ap(), b_sel, writes=[b_sel])
    T.dma(sp, cT, cT_d.ap(), b_cT, writes=[b_cT])
    T.dma(sp, abT, ada_bT.ap().rearrange("l p j -> p l j"), b_abT, writes=[b_abT])
    T.dma(sp, ngT, norm_gT.ap().rearrange("l p j -> p l j"), b_ngT, writes=[b_ngT])
    for m in range(5):
        T.dma(sp, abrow[m:m + 1], ada_b.ap().rearrange("(o l) n -> o l n", o=1), b_abrow, writes=[b_abrow])
    T.op(dve, lambda: nc.vector.memset(epsb, EPS), writes=[b_eps])
    T.op(act, lambda: nc.scalar.activation(out=sgc, in_=cT, func=AF.Sigmoid), reads=[b_cT], writes=[b_sgc])
    T.op(dve, lambda: nc.vector.tensor_tensor(out=scT, in0=cT, in1=sgc, op=ALU.mult), reads=[b_cT, b_sgc], writes=[b_scT])

    adab = [(pro.enter_context(nc.sbuf_tensor("sb_adablk%d" % i, [128, 8, 512], F32)).ap(), Buf("adab%d" % i)) for i in range(2)]
    blk_i = 0
    for l in range(DEPTH):
        mr, b_mr = modrows[l]
        mT, b_mT = modT[l]
        for cb in range(6):
            blk, b_blk = adab[blk_i % 2]
            blk_i += 1
            T.dma(sp, blk, ada_w.ap()[l].rearrange("(k p) n -> p k n", p=128)[:, :, cb * 512:(cb + 1) * 512],
                  b_blk, writes=[b_blk])
            bk = bank[cb % 2]
            for k in range(8):
                T.op(pe, lambda k=k: nc.tensor.matmul(PSF[0:5, cb % 2, :], lhsT=scT[:, k, :], rhs=blk[:, k, :],
                                                      start=(k == 0), stop=(k == 7)),
                     reads=[b_scT, b_blk], writes=[bk], inc=(k == 7))
            mdst = mr[:, (cb - 4) * 512:(cb - 3) * 512] if cb >= 4 else mrs
            T.op(dve, lambda: nc.vector.tensor_tensor(out=mdst, in0=PSF[0:5, cb % 2, :],
                                                      in1=abrow[:, l, cb * 512:(cb + 1) * 512], op=ALU.add),
                 reads=[bk, b_abrow], writes=[b_mr if cb >= 4 else b_mrs])
            for jj in range(4):
                ch = cb * 4 + jj
                bk2 = bank[2 + ch % 2]
                for k in range(8):
                    T.op(pe, lambda k=k: nc.tensor.matmul(PSF[:, 2 + ch % 2, 0:5], lhsT=blk[:, k, jj * 128:(jj + 1) * 128],
                                                          rhs=scT[:, k, :], start=(k == 0), stop=(k == 7)),
                         reads=[b_scT, b_blk], writes=[bk2], inc=(k == 7))
                T.op(dve, lambda: nc.vector.tensor_scalar(out=mT[:, ch, :], in0=PSF[:, 2 + ch % 2, 0:5],
                                                          scalar1=abT[:, l, ch:ch + 1], scalar2=None, op0=ALU.add),
                     reads=[bk2, b_abT], writes=[b_mT])
        gT, b_gT = geffT[l]
        T.op(dve, lambda: nc.vector.tensor_scalar(out=gT, in0=mT[:, 8:16, :], scalar1=1.0, scalar2=None, op0=ALU.add),
             reads=[b_mT], writes=[b_gT])
        T.op(dve, lambda: nc.vector.tensor_tensor(out=gT, in0=gT, in1=ngT[:, l, :].unsqueeze(2).broadcast_to([128, 8, 5]),
                                                  op=ALU.mult), reads=[b_gT, b_ngT], writes=[b_gT])

    T.barrier()
    T.release([b_abrow, adab[0][1], adab[1][1]])
    pro.close()
    for l in range(DEPTH):
        last = (l == DEPTH - 1)
        lam_init = 0.8 - 0.6 * math.exp(-0.3 * l)
        mr, b_mr = modrows[l]
        mT, b_mT = modT[l]
        gT, b_gT = geffT[l]
        xsrc, d_xsrc = (xin.ap(), None) if l == 0 else (x1_d.ap(), d_x1)
        T.barrier()

        with ExitStack() as es:
            WA_t = es.enter_context(nc.sbuf_tensor("WA%d" % l, [128, 8, 2816], BF16))
            xa_t = es.enter_context(nc.sbuf_tensor("xa%d" % l, [128, 2, D], F32))
            xn_t = es.enter_context(nc.sbuf_tensor("xn%d" % l, [128, 2, D], BF16))
            st_t = es.enter_context(nc.sbuf_tensor("st%d" % l, [128, 8], F32))
            hTg_t = es.enter_context(nc.sbuf_tensor("hTg%d" % l, [128, 2] + [128, 8, GS * 128][1:], BF16))
            qk_t = es.enter_context(nc.sbuf_tensor("qk%d" % l, [128, 2] + [128, 26, 64][1:], F32))
            sq_t = es.enter_context(nc.sbuf_tensor("sq%d" % l, [128, 2] + [128, 26, 64][1:], F32))
            t1_t = es.enter_context(nc.sbuf_tensor("t1%d" % l, [128, 2] + [128, 26, 64][1:], F32))
            t2_t = es.enter_context(nc.sbuf_tensor("t2%d" % l, [128, 2] + [128, 26, 64][1:], F32))
            qkb_t = es.enter_context(nc.sbuf_tensor("qkb%d" % l, [128, 2] + [128, 26, 64][1:], BF16))
            ss_t = es.enter_context(nc.sbuf_tensor("ss%d" % l, [128, 2] + [128, 26][1:], F32))
            g4_t = es.enter_context(nc.sbuf_tensor("g4%d" % l, [128, 4, 64], F32))
            Gbc_t = es.enter_context(nc.sbuf_tensor("Gbc%d" % l, [128, 26, 64], F32))
            rp_t = es.enter_context(nc.sbuf_tensor("rp%d" % l, [128, 3, 128], F32))
            vst_t = es.enter_context(nc.sbuf_tensor("vst%d" % l, [128, 2] + [128, 4, GS, PT][1:], BF16))
            vgst_t = es.enter_context(nc.sbuf_tensor("vgst%d" % l, [128, 2] + [128, GS, 2, 65][1:], BF16))
            qst_t = es.enter_context(nc.sbuf_tensor("qst%d" % l, [128, 2] + [128, 8, GS * 128][1:], BF16))
            kst_t = es.enter_context(nc.sbuf_tensor("kst%d" % l, [128, 2] + [128, 5, GS * 128][1:], BF16))
            ust_t = es.enter_context(nc.sbuf_tensor("ust%d" % l, [128, 2] + [128, GS, 512][1:], BF16))
            WA, xa, xn, st, hTg2 = WA_t.ap(), xa_t.ap(), xn_t.ap(), st_t.ap(), hTg_t.ap()
            qk2, sq2, t12, t22, qkb2, ss2, g4, Gbc, rp = (qk_t.ap(), sq_t.ap(), t1_t.ap(), t2_t.ap(), qkb_t.ap(),
                                                          ss_t.ap(), g4_t.ap(), Gbc_t.ap(), rp_t.ap())
            vst2, vgst2, qst2, kst2, ust2 = vst_t.ap(), vgst_t.ap(), qst_t.ap(), kst_t.ap(), ust_t.ap()
            b_WA = [Buf("WA%d" % i) for i in range(14)]
            b_xa = [Buf("xa0"), Buf("xa1")]
            b_xn = [Buf("xn0"), Buf("xn1")]
            b_rp = [Buf("rp0"), Buf("rp1"), Buf("rp2")]
            b_junk, b_st, b_g4, b_Gbc = [Buf(n) for n in ("junk", "st", "g4", "Gbc")]
            mk2 = lambda n_: [Buf(n_ + "0"), Buf(n_ + "1")]
            b_qk2, b_sq2, b_t12, b_t22, b_qkb2, b_ss2 = [mk2(n_) for n_ in ("qk", "sq", "t1", "t2", "qkb", "ss")]
            b_vst2, b_vgst2, b_qst2, b_kst2, b_ust2 = [mk2(n_) for n_ in ("vst", "vgst", "qst", "kst", "ust")]
            b_hTgs2 = [[Buf("hTg%d_%d" % (gp_, j_)) for j_ in range(GS)] for gp_ in range(2)]
            phase_bufs = (b_WA + b_xa + b_rp + [b_g4] + b_vst2 + b_vgst2 + b_qst2 + b_kst2 + b_ust2 + b_hTgs2[0] + b_hTgs2[1])

            wsrc = w_in.ap()[l].rearrange("(k p) n -> p k n", p=128)
            segs = [(0, 0, 512), (512, 1024, 512), (1024, 1536, 512), (1536, 2048, 512), (2560, 3584, 128), (2688, 3712, 128)]
            for i, (do, so, n) in enumerate(segs):
                T.dma(pool, WA[:, :, do:do + n], wsrc[:, :, so:so + n], b_WA[i], writes=[b_WA[i]])
            for r in range(4):
                for g_ in range(2):
                    bw = b_WA[6 + r * 2 + g_]
                    T.dma(pool, WA[:, :, 2048 + r * 128 + g_ * 64:2048 + r * 128 + (g_ + 1) * 64],
                          wsrc[:, :, 3072 + (g_ * 4 + r) * 64:3072 + (g_ * 4 + r + 1) * 64], bw, writes=[bw])
            T.dma(sp, g4.rearrange("p t d -> p (t d)"), qkg.ap()[l].partition_broadcast(128), b_g4, writes=[b_g4])
            for (h0, h1, ti) in ((0, 8, 0), (8, 16, 1), (16, 24, 2), (24, 26, 3)):
                T.op(dve, lambda: nc.vector.tensor_copy(out=Gbc[:, h0:h1, :],
                                                        in_=g4[:, ti:ti + 1, :].broadcast_to([128, h1 - h0, 64])),
                     reads=[b_g4], writes=[b_Gbc])
            T.op(dve, lambda: nc.vector.memset(vst2, 1.0), writes=b_vst2)
            T.op(dve, lambda: nc.vector.memset(vgst2, 1.0), writes=b_vgst2)

            groups = [(0, 2, 4)] + [(2 + b * LT + g * GSL, GSL, b) for b in range(NB) for g in range(NGB)]
            flat = [t0_ + j_ for (t0_, gs_, m_) in groups for j_ in range(gs_)]

            def issue_loads(tile_):
                s_ = tile_ % 2
                T.dma(sp, xa[:, s_, :], xsrc[tile_ * 128:(tile_ + 1) * 128, :], b_xa[s_],
                      reads=[d_xsrc] if d_xsrc else [], writes=[b_xa[s_]])
                T.dma(sp, rp[:, tile_ % 3, :], rope_d.ap()[tile_ * 128:(tile_ + 1) * 128, :], b_rp[tile_ % 3], writes=[b_rp[tile_ % 3]])

            pjc = [0]

            def stage1(gidx, j, nxt):
                t0, gs, m = groups[gidx]
                n = gs * 128
                gp = gidx % 2
                hTg, vst, vgst, qst, kst, ust = hTg2[:, gp], vst2[:, gp], vgst2[:, gp], qst2[:, gp], kst2[:, gp], ust2[:, gp]
                b_vst, b_vgst, b_qst, b_kst, b_ust = b_vst2[gp], b_vgst2[gp], b_qst2[gp], b_kst2[gp], b_ust2[gp]
                tile = t0 + j
                s = tile % 2
                s3 = tile % 3
                qk, sq, t1, t2, qkb, ss = qk2[:, s], sq2[:, s], t12[:, s], t22[:, s], qkb2[:, s], ss2[:, s]
                b_qk, b_sq, b_t1, b_t2, b_qkb, b_ss = b_qk2[s], b_sq2[s], b_t12[s], b_t22[s], b_qkb2[s], b_ss2[s]
                b_hTg = b_hTgs2[gp][j]
                if nxt is not None:
                    issue_loads(nxt)
                jk = t2.rearrange("p h e -> p (h e)")[:, 0:D]
                T.op(dve, lambda: nc.vector.tensor_tensor(out=jk, in0=xa[:, s, :], in1=xa[:, s, :], op=ALU.mult),
                     reads=[b_xa[s]], writes=[b_t2])
                T.op(dve, lambda: nc.vector.tensor_reduce(out=st[:, 0:1], in_=jk, axis=AX.X, op=ALU.add),
                     reads=[b_t2], writes=[b_st])
                T.op(act, lambda: nc.scalar.activation(out=st[:, 1:2], in_=st[:, 0:1], func=AF.Ln, scale=1.0 / D, bias=epsb),
                     reads=[b_st, b_eps], writes=[b_st])
                T.op(act, lambda: nc.scalar.activation(out=st[:, 2:3], in_=st[:, 1:2], func=AF.Exp, scale=-0.5),
                     reads=[b_st], writes=[b_st])
                T.op(dve, lambda: nc.vector.tensor_scalar(out=xn[:, s, :], in0=xa[:, s, :], scalar1=st[:, 2:3],
                                                          scalar2=None, op0=ALU.mult),
                     reads=[b_xa[s], b_st], writes=[b_xn[s]])
                bt = bank[4 + s]
                for k in range(8):
                    T.op(pe, lambda k=k: nc.tensor.transpose(out=PSB[:, 4 + s, k * 128:(k + 1) * 128],
                                                             in_=xn[:, s, k * 128:(k + 1) * 128], identity=ident),
                         reads=[b_xn[s], b_ident], writes=[bt], inc=(k == 7))
                for k in range(8):
                    T.op(dve, lambda k=k: nc.vector.tensor_scalar(out=hTg[:, k, j * 128:(j + 1) * 128],
                                                                  in0=PSB[:, 4 + s, k * 128:(k + 1) * 128],
                                                                  scalar1=gT[:, k, m:m + 1], scalar2=mT[:, k, m:m + 1],
                                                                  op0=ALU.mult, op1=ALU.add),
                         reads=[bt, b_gT, b_mT], writes=[b_hTg])
                for cbk in range(6):
                    ncol = 512 if cbk < 5 else 256
                    bi = pjc[0] % 4
                    pjc[0] += 1
                    bk = bank[bi]
                    for k in range(8):
                        T.op(pe, lambda k=k: nc.tensor.matmul(PSF[:, bi, 0:ncol], lhsT=hTg[:, k, j * 128:(j + 1) * 128],
                                                              rhs=WA[:, k, cbk * 512:cbk * 512 + ncol],
                                                              start=(k == 0), stop=(k == 7)),
                             reads=[b_hTg] + b_WA, writes=[bk], inc=(k == 7))
                    src = PSF[:, bi, 0:ncol]
                    if cbk == 0:
                        T.op(act, lambda: nc.scalar.copy(out=ust[:, j, :], in_=src), reads=[bk], writes=[b_ust])
                    elif cbk == 3:
                        T.op(act, lambda: nc.scalar.copy(out=vst[:, :, j, 0:128],
                                                         in_=src.rearrange("p (h e) -> p h e", h=4)),
                             reads=[bk], writes=[b_vst])
                    elif cbk == 5:
                        T.op(act, lambda: nc.scalar.copy(out=qk[:, 24:26, :], in_=src[:, 0:128].rearrange("p (h e) -> p h e", h=2)),
                             reads=[bk], writes=[b_qk])
                        T.op(act, lambda: nc.scalar.copy(out=vgst[:, j, :, 0:64],
                                                         in_=src[:, 128:256].rearrange("p (h e) -> p h e", h=2)),
                             reads=[bk], writes=[b_vgst])
                    else:
                        hh = {1: 0, 2: 8, 4: 16}[cbk]
                        T.op(act, lambda: nc.scalar.copy(out=qk[:, hh:hh + 8, :], in_=src.rearrange("p (h e) -> p h e", h=8)),
                             reads=[bk], writes=[b_qk])

            def stage2(gidx, j):
                t0, gs, m = groups[gidx]
                n = gs * 128
                gp = gidx % 2
                hTg, vst, vgst, qst, kst, ust = hTg2[:, gp], vst2[:, gp], vgst2[:, gp], qst2[:, gp], kst2[:, gp], ust2[:, gp]
                b_vst, b_vgst, b_qst, b_kst, b_ust = b_vst2[gp], b_vgst2[gp], b_qst2[gp], b_kst2[gp], b_ust2[gp]
                tile = t0 + j
                s = tile % 2
                s3 = tile % 3
                qk, sq, t1, t2, qkb, ss = qk2[:, s], sq2[:, s], t12[:, s], t22[:, s], qkb2[:, s], ss2[:, s]
                b_qk, b_sq, b_t1, b_t2, b_qkb, b_ss = b_qk2[s], b_sq2[s], b_t12[s], b_t22[s], b_qkb2[s], b_ss2[s]
                b_hTg = b_hTgs2[gp][j]
                T.op(dve, lambda: nc.vector.tensor_tensor(out=sq, in0=qk, in1=qk, op=ALU.mult), reads=[b_qk], writes=[b_sq])
                T.op(dve, lambda: nc.vector.tensor_reduce(out=ss, in_=sq, axis=AX.X, op=ALU.add), reads=[b_sq], writes=[b_ss])
                T.op(act, lambda: nc.scalar.activation(out=ss, in_=ss, func=AF.Ln, scale=1.0 / 64, bias=epsb),
                     reads=[b_ss, b_eps], writes=[b_ss])
                T.op(act, lambda: nc.scalar.activation(out=ss, in_=ss, func=AF.Exp, scale=-0.5), reads=[b_ss], writes=[b_ss])
                T.op(dve, lambda: nc.vector.tensor_tensor(out=sq, in0=qk, in1=ss.unsqueeze(2).broadcast_to([128, 26, 64]),
                                                          op=ALU.mult), reads=[b_qk, b_ss], writes=[b_sq])
                T.op(dve, lambda: nc.vector.tensor_tensor(out=sq, in0=sq, in1=Gbc, op=ALU.mult),
                     reads=[b_sq, b_Gbc], writes=[b_sq])
                cc_ap = rp[:, s3, 0:64].unsqueeze(1).broadcast_to([128, 26, 64])
                T.op(pool, lambda: nc.gpsimd.tensor_tensor(out=t1, in0=sq, in1=cc_ap, op=ALU.mult),
                     reads=[b_sq, b_rp[s3]], writes=[b_t1])
                sq5 = sq.rearrange("p h (a f e) -> p h a f e", a=2, f=2)
                t25 = t2.rearrange("p h (a f e) -> p h a f e", a=2, f=2)
                ss5 = rp[:, s3, 64:128].rearrange("p (a f e) -> p a f e", a=2, f=2)
                for f in range(2):
                    T.op(dve, lambda f=f: nc.vector.tensor_tensor(
                        out=t25[:, :, :, f, :], in0=sq5[:, :, :, 1 - f, :],
                        in1=ss5[:, :, f, :].unsqueeze(1).broadcast_to([128, 26, 2, 16]), op=ALU.mult),
                        reads=[b_sq, b_rp[s3]], writes=[b_t2])
                T.op(dve, lambda: nc.vector.tensor_tensor(out=qkb, in0=t1, in1=t2, op=ALU.add),
                     reads=[b_t1, b_t2], writes=[b_qkb])
                blocks = [(0, i) for i in range(4)] + [(1, i) for i in range(4)] + [(2, i) for i in range(4)] + [(3, 0)]
                for bi2, (ty, i) in enumerate(blocks):
                    hs = {0: 0, 1: 8, 2: 16, 3: 24}[ty] + 2 * i
                    pb = 6 + bi2 // 8
                    T.op(pe, lambda: nc.tensor.transpose(out=PSB[:, pb, (bi2 % 8) * 128:(bi2 % 8 + 1) * 128],
                                                         in_=qkb[:, hs:hs + 2, :].rearrange("p h e -> p (h e)"),
                                                         identity=ident),
                         reads=[b_qkb, b_ident], writes=[bank[pb]], inc=(bi2 in (7, 12)))
                T.op(act, lambda: nc.scalar.copy(out=qst[:, 0:4, j * 128:(j + 1) * 128],
                                                 in_=PSB[:, 6, 0:512].rearrange("p (i e) -> p i e", i=4)),
                     reads=[bank[6]], writes=[b_qst])
                T.op(act, lambda: nc.scalar.copy(out=kst[:, 0:4, j * 128:(j + 1) * 128],
                                                 in_=PSB[:, 6, 512:1024].rearrange("p (i e) -> p i e", i=4)),
                     reads=[bank[6]], writes=[b_kst])
                T.op(act, lambda: nc.scalar.copy(out=qst[:, 4:8, j * 128:(j + 1) * 128],
                                                 in_=PSB[:, 7, 0:512].rearrange("p (i e) -> p i e", i=4)),
                     reads=[bank[7]], writes=[b_qst])
                T.op(act, lambda: nc.scalar.copy(out=kst[:, 4, j * 128:(j + 1) * 128], in_=PSB[:, 7, 512:640]),
                     reads=[bank[7]], writes=[b_kst])

            def stores(gidx):
                j = 0
                t0, gs, m = groups[gidx]
                n = gs * 128
                gp = gidx % 2
                hTg, vst, vgst, qst, kst, ust = hTg2[:, gp], vst2[:, gp], vgst2[:, gp], qst2[:, gp], kst2[:, gp], ust2[:, gp]
                b_vst, b_vgst, b_qst, b_kst, b_ust = b_vst2[gp], b_vgst2[gp], b_qst2[gp], b_kst2[gp], b_ust2[gp]
                tile = t0 + j
                s = tile % 2
                s3 = tile % 3
                qk, sq, t1, t2, qkb, ss = qk2[:, s], sq2[:, s], t12[:, s], t22[:, s], qkb2[:, s], ss2[:, s]
                b_qk, b_sq, b_t1, b_t2, b_qkb, b_ss = b_qk2[s], b_sq2[s], b_t12[s], b_t22[s], b_qkb2[s], b_ss2[s]
                b_hTg = b_hTgs2[gp][j]
                tk = slice(t0 * 128, t0 * 128 + n)
                T.dma(sp, hT_d.ap().rearrange("k p t -> p k t")[:, :, tk], hTg[:, :, 0:n], b_hTgs2[gp][0], reads=b_hTgs2[gp][0:gs],
                      writes=[d_hT])
                T.dma(sp, qT_d.ap().rearrange("k p t -> p k t")[:, :, tk], qst[:, :, 0:n], b_qst, reads=[b_qst], writes=[d_qT])
                T.dma(sp, u_d.ap().rearrange("(t p) c -> p t c", p=128)[:, t0:t0 + gs, :], ust[:, 0:gs, :], b_ust,
                      reads=[b_ust], writes=[d_u])
                sg = m
                tl0 = t0 if m == 4 else t0 - (2 + m * LT)
                kl = kvloc[l][sg].ap()
                dk = d_kvloc[l][sg]
                for it in range(5):
                    T.dma(sp, kl[it * 128:(it + 1) * 128, :].rearrange("p (t e) -> p t e", e=PT)[:, tl0:tl0 + gs, 0:128],
                          kst[:, it, 0:n].rearrange("p (t e) -> p t e", e=128), b_kst, reads=[b_kst], writes=[dk])
                for h in range(4):
                    T.dma(sp, kl[(5 + h) * 128:(6 + h) * 128, tl0 * PT:(tl0 + gs) * PT].rearrange("p (t e) -> p t e", e=PT),
                          vst[:, h, 0:gs, :], b_vst, reads=[b_vst], writes=[dk])
                T.dma(sp, kl[9 * 128:10 * 128, tl0 * PT:(tl0 + gs) * PT].rearrange("p (t c e) -> p t c e", c=2, e=65),
                      vgst[:, 0:gs, :, :], b_vgst, reads=[b_vgst], writes=[dk])
                if m < 4:
                    g_in_b = (t0 - 2 - m * LT) // GSL
                    ul = uloc[l].ap()
                    if g_in_b == 0:
                        T.dma(sp, ul[0:8, (m * 2) * 512:(m * 2 + 1) * 512], ust[0:8, 0, :], b_ust,
                              reads=[b_ust], writes=[d_uloc[l]])
                    if g_in_b == NGB - 1:
                        T.dma(sp, ul[0:8, (m * 2 + 1) * 512:(m * 2 + 2) * 512], ust[120:128, gs - 1, :],
                              b_ust, reads=[b_ust], writes=[d_uloc[l]])
                    if g_in_b == NGB - 1:
                        gather(kvloc[l][sg], kvall[l][sg], d_kvall[l][sg], b_kst2 + b_vst2 + b_vgst2)
                        if m == NB - 1:
                            gather(uloc[l], uall[l], d_uall[l], b_ust2)
                else:
                    gather(kvloc[l][sg], kvall[l][sg], d_kvall[l][sg], b_kst2 + b_vst2 + b_vgst2)

            order = [(gi_, j_) for gi_, (t0_, gs_, m_) in enumerate(groups) for j_ in range(gs_)]
            issue_loads(flat[0])
            stage1(order[0][0], order[0][1], flat[1] if len(flat) > 1 else None)
            for idx, (gi_, j_) in enumerate(order):
                if idx + 1 < len(order):
                    stage1(order[idx + 1][0], order[idx + 1][1], flat[idx + 2] if idx + 2 < len(flat) else None)
                stage2(gi_, j_)
                if j_ == groups[gi_][1] - 1:
                    stores(gi_)
            T.barrier()
            T.release(phase_bufs)

        with ExitStack() as es:
            kt_t = es.enter_context(nc.sbuf_tensor("kt%d" % l, [128, 2, NKT, PT], BF16))
            vt_t = es.enter_context(nc.sbuf_tensor("vt%d" % l, [128, 2, NKT, PT], BF16))
            qt_t = es.enter_context(nc.sbuf_tensor("qt%d" % l, [128, 4, max(LT, 2) * 128], BF16))
            pt_t = es.enter_context(nc.sbuf_tensor("pt%d" % l, [128, 3, 2, GS * 128], BF16))
            aost_t = es.enter_context(nc.sbuf_tensor("aost%d" % l, [128, 2, GS, 128], BF16))
            pw_t = es.enter_context(nc.sbuf_tensor("pw%d" % l, [128, 4, 8], F32))
            o1_t = es.enter_context(nc.sbuf_tensor("o1%d" % l, [128, 4, 128], F32))
            o2_t = es.enter_context(nc.sbuf_tensor("o2%d" % l, [128, 4, 128], F32))
            osb_t = es.enter_context(nc.sbuf_tensor("osb%d" % l, [128, 2, 8, 129], F32))
            dl_t = es.enter_context(nc.sbuf_tensor("dl%d" % l, [128, 256], F32))
            lam_t = es.enter_context(nc.sbuf_tensor("lam%d" % l, [128, 8], F32))
            gsub_t = es.enter_context(nc.sbuf_tensor("gsub%d" % l, [128, 128], F32))
            kt, vt, qt, pt, aost, pw, o1, o2, dl, lam, gsub = (kt_t.ap(), vt_t.ap(), qt_t.ap(), pt_t.ap(), aost_t.ap(),
                                                             pw_t.ap(), o1_t.ap(), o2_t.ap(), dl_t.ap(), lam_t.ap(),
                                                             gsub_t.ap())
            b_kt = [Buf("kt0"), Buf("kt1")]
            b_vt = [Buf("vt0"), Buf("vt1")]
            b_qt = [Buf("qt%d" % i) for i in range(4)]
            b_pt = [Buf("pt%d" % i) for i in range(3)]
            b_aost = [Buf("aost0"), Buf("aost1")]
            b_dl, b_lam, b_gsub = [Buf(n) for n in ("dl", "lam", "gsub")]
            b_pw = [Buf("pw%d" % i) for i in range(4)]
            b_o1 = [Buf("o1%d" % i) for i in range(4)]
            b_o2 = [Buf("o2%d" % i) for i in range(4)]
            b_osb = [Buf("osb0"), Buf("osb1")]
            osb = osb_t.ap()
            phase_bufs = b_kt + b_vt + b_qt + b_aost + [b_dl, b_gsub]
            T.dma(sp, dl, dlam.ap()[l].partition_broadcast(128), b_dl, writes=[b_dl])
            T.dma(sp, gsub, subln.ap()[l].partition_broadcast(128), b_gsub, writes=[b_gsub])
            for i in range(2):
                T.op(dve, lambda i=i: nc.vector.tensor_tensor(out=o1[:, 0, 0:64], in0=dl[:, i * 128:i * 128 + 64],
                                                              in1=dl[:, i * 128 + 64:i * 128 + 128], op=ALU.mult),
                     reads=[b_dl], writes=[b_o1[0]])
                T.op(dve, lambda i=i: nc.vector.tensor_reduce(out=lam[:, i:i + 1], in_=o1[:, 0, 0:64], axis=AX.X, op=ALU.add),
                     reads=[b_o1[0]], writes=[b_lam])
            T.op(act, lambda: nc.scalar.activation(out=lam[:, 2:4], in_=lam[:, 0:2], func=AF.Exp), reads=[b_lam], writes=[b_lam])
            T.op(dve, lambda: nc.vector.tensor_tensor(out=lam[:, 4:5], in0=lam[:, 3:4], in1=lam[:, 2:3], op=ALU.subtract),
                 reads=[b_lam], writes=[b_lam])
            T.op(dve, lambda: nc.vector.tensor_scalar(out=lam[:, 5:6], in0=lam[:, 4:5], scalar1=-lam_init, scalar2=None,
                                                      op0=ALU.add), reads=[b_lam], writes=[b_lam])
            T.op(dve, lambda: nc.vector.tensor_scalar(out=gsub, in0=gsub, scalar1=1.0 - lam_init, scalar2=None, op0=ALU.mult),
                 reads=[b_gsub], writes=[b_gsub])
            neglam = lam[:, 5:6]
            ctr = {"kv": 0, "q": 0, "pt": 0, "s": 0, "ao": 0}

            def load_kv(dst, bdst, slot, item, b, local):
                if local:
                    T.dma(sp, dst[:, slot, 0:2, :], kvloc[l][4].ap()[item * 128:(item + 1) * 128, :].rearrange("p (t e) -> p t e", e=PT),
                          bdst[slot], reads=[d_kvloc[l][4]], writes=[bdst[slot]])
                    return
                T.dma(sp, dst[:, slot, 0:2, :],
                      kvall[l][4].ap()[2 * b * R10 + item * 128:2 * b * R10 + (item + 1) * 128, :].rearrange("p (t e) -> p t e", e=PT),
                      bdst[slot], reads=[d_kvall[l][4]], writes=[bdst[slot]])
                T.dma(sp, dst[:, slot, 2:NKT, :].rearrange("p (r t) e -> p r (t e)", r=NCORE),
                      kvall[l][b].ap().rearrange("(r q) n -> q r n", q=R10)[item * 128:(item + 1) * 128, :, :],
                      bdst[slot], reads=[d_kvall[l][b]], writes=[bdst[slot]])

            def attend(kslot, vslot, qslot, q0, nq, nkt, dv, vsel, ao_cols):
                QB = nq * 128
                PI = 129

                def oacc(c, j):
                    idx = c * nq + j
                    return PSF[:, 4 + idx // 3, (idx % 3) * PI:(idx % 3) * PI + dv + 1], bank[4 + idx // 3]

                def qk_step(i):
                    sbi = ctr["s"] % 2
                    ctr["s"] += 1
                    for c in range(2):
                        T.op(pe, lambda c=c: nc.tensor.matmul(PSF[:, 2 * sbi + c, 0:QB], lhsT=kt[c * 64:(c + 1) * 64, kslot, i, 0:128],
                                                              rhs=qt[c * 64:(c + 1) * 64, qslot, q0:q0 + QB], start=True, stop=True),
                             reads=[b_kt[kslot], b_qt[qslot]], writes=[bank[2 * sbi + c]], inc=(c == 1))
                    return sbi

                def exp_step(sbi):
                    pi = ctr["pt"] % 3
                    ctr["pt"] += 1
                    T.op(act, lambda: nc.scalar.activation(out=pt[:, pi, :, 0:QB], in_=PSF[:, 2 * sbi:2 * sbi + 2, 0:QB],
                                                           func=AF.Exp, scale=0.125),
                         reads=[bank[2 * sbi], bank[2 * sbi + 1]], writes=[b_pt[pi]])
                    return pi

                def av_step(i, pi):
                    for c in range(2):
                        for j in range(nq):
                            o_ap, o_b = oacc(c, j)
                            T.op(pe, lambda c=c, j=j, o_ap=o_ap: nc.tensor.matmul(
                                o_ap, lhsT=pt[:, pi, c, j * 128:(j + 1) * 128], rhs=vsel(vslot, i, c),
                                start=(i == 0), stop=(i == nkt - 1)),
                                reads=[b_pt[pi], b_vt[vslot]], writes=[o_b], inc=(c == 1 and j == nq - 1))

                sq_ = [qk_step(i) for i in range(min(2, nkt))]
                for i in range(nkt):
                    pi = exp_step(sq_[i])
                    if i + 2 < nkt:
                        sq_.append(qk_step(i + 2))
                    av_step(i, pi)
                asl = ctr["ao"] % 2
                ctr["ao"] += 1
                for idx in range(2 * nq):
                    o_ap, o_b = oacc(idx // nq, idx % nq)
                    T.op(dve, lambda: nc.vector.tensor_copy(out=osb[:, asl, idx, 0:dv + 1], in_=o_ap), reads=[o_b],
                         writes=[b_osb[asl]])
                for j in range(nq):
                    o0 = osb[:, asl, j, :]
                    o1a = osb[:, asl, nq + j, :]
                    pwj = pw[:, j, :]
                    rd = [b_osb[asl]]
                    T.op(dve, lambda: nc.vector.reciprocal(out=pwj[:, 0:1], in_=o0[:, dv:dv + 1]), reads=rd, writes=[b_pw[j]])
                    T.op(dve, lambda: nc.vector.reciprocal(out=pwj[:, 1:2], in_=o1a[:, dv:dv + 1]), reads=rd, writes=[b_pw[j]])
                    if dv == 128:
                        T.op(dve, lambda: nc.vector.tensor_scalar(out=o1[:, j, :], in0=o1a[:, 0:128], scalar1=pwj[:, 1:2], scalar2=neglam,
                                                                  op0=ALU.mult, op1=ALU.mult),
                             reads=rd + [b_pw[j], b_lam], writes=[b_o1[j]])
                        T.op(dve, lambda: nc.vector.scalar_tensor_tensor(out=o2[:, j, :], in0=o0[:, 0:128], scalar=pwj[:, 0:1], in1=o1[:, j, :],
                                                                         op0=ALU.mult, op1=ALU.add),
                             reads=rd + [b_pw[j], b_o1[j]], writes=[b_o2[j]])
                        T.op(pool, lambda: nc.gpsimd.tensor_tensor(out=o1[:, j, :], in0=o2[:, j, :], in1=o2[:, j, :], op=ALU.mult),
                             reads=[b_o2[j]], writes=[b_o1[j]])
                        T.op(dve, lambda: nc.vector.tensor_reduce(out=pwj[:, 2:3], in_=o1[:, j, :], axis=AX.X, op=ALU.add),
                             reads=[b_o1[j]], writes=[b_pw[j]])
                        T.op(act, lambda: nc.scalar.activation(out=pwj[:, 3:4], in_=pwj[:, 2:3], func=AF.Ln, scale=1.0 / 128, bias=epsb),
                             reads=[b_pw[j], b_eps], writes=[b_pw[j]])
                        T.op(act, lambda: nc.scalar.activation(out=pwj[:, 4:5], in_=pwj[:, 3:4], func=AF.Exp, scale=-0.5),
                             reads=[b_pw[j]], writes=[b_pw[j]])
                        T.op(dve, lambda: nc.vector.scalar_tensor_tensor(out=aost[:, asl, j, :], in0=o2[:, j, :], scalar=pwj[:, 4:5],
                                                                         in1=gsub, op0=ALU.mult, op1=ALU.mult),
                             reads=[b_o2[j], b_pw[j], b_gsub], writes=[b_aost[asl]])
                    else:
                        T.op(dve, lambda: nc.vector.tensor_scalar(out=aost[:, asl, j, 0:64], in0=o0[:, 0:64], scalar1=pwj[:, 0:1],
                                                                  scalar2=None, op0=ALU.mult),
                             reads=rd + [b_pw[j]], writes=[b_aost[asl]])
                        T.op(dve, lambda: nc.vector.tensor_scalar(out=aost[:, asl, j, 64:128], in0=o1a[:, 0:64], scalar1=pwj[:, 1:2],
                                                                  scalar2=None, op0=ALU.mult),
                             reads=rd + [b_pw[j]], writes=[b_aost[asl]])
                aor = ao_d.ap().rearrange("(t p) c -> p t c", p=128)
                tq0 = ao_cols[0]
                if dv == 128:
                    T.dma(sp, aor[:, tq0:tq0 + nq, ao_cols[1]:ao_cols[1] + 128], aost[:, asl, 0:nq, :], b_aost[asl],
                          reads=[b_aost[asl]], writes=[d_ao])
                else:
                    for c in range(2):
                        T.dma(sp, aor[:, tq0:tq0 + nq, ao_cols[1 + c]:ao_cols[1 + c] + 64], aost[:, asl, 0:nq, c * 64:(c + 1) * 64],
                              b_aost[asl], reads=[b_aost[asl]], writes=[d_ao])

            vsel_d = lambda vslot, i, c: vt[:, vslot, i, 0:129]
            vsel_g = lambda vslot, i, c: vt[:, vslot, i, c * 65:c * 65 + 65]
            qsrc = qT_d.ap()

            seqs = ([(0, True)] if not last else []) + [(b, False) for b in range(NB)]
            kgroups = []
            for (b, local) in seqs:
                for h in range(4):
                    kgroups.append(dict(b=b, local=local, kitem=h, vitem=5 + h, qs=[h], dv=128))
                kgroups.append(dict(b=b, local=local, kitem=4, vitem=9, qs=[4, 5, 6, 7], dv=64))
            jobs = [(gi, q) for gi, g in enumerate(kgroups) for q in g["qs"]]

            def geom(g):
                if g["local"]:
                    return 0, 2, 2
                return (2 + g["b"] * LT) * 128, LT, NKT

            def issue_kv(gi):
                g = kgroups[gi]
                load_kv(kt, b_kt, gi % 2, g["kitem"], g["b"], g["local"])
                load_kv(vt, b_vt, gi % 2, g["vitem"], g["b"], g["local"])

            def issue_q(ji):
                gi, q = jobs[ji]
                qtok0, nqt, nkt = geom(kgroups[gi])
                T.dma(sp, qt[:, ji % 4, 0:nqt * 128], qsrc[q, :, qtok0:qtok0 + nqt * 128], b_qt[ji % 4], reads=[d_qT],
                      writes=[b_qt[ji % 4]])

            issue_kv(0)
            issue_q(0)
            seen_g = set()
            for ji, (gi, q) in enumerate(jobs):
                g = kgroups[gi]
                if gi not in seen_g:
                    seen_g.add(gi)
                    if gi + 1 < len(kgroups):
                        issue_kv(gi + 1)
                if ji + 1 < len(jobs):
                    issue_q(ji + 1)
                qtok0, nqt, nkt = geom(g)
                qblocks = [(qb * GS, min(GS, nqt - qb * GS)) for qb in range((nqt + GS - 1) // GS)]
                for (qj, nq) in qblocks:
                    if g["dv"] == 128:
                        attend(gi % 2, gi % 2, ji % 4, qj * 128, nq, nkt, 128, vsel_d, (qtok0 // 128 + qj, q * 128))
                    else:
                        r = q - 4
                        attend(gi % 2, gi % 2, ji % 4, qj * 128, nq, nkt, 64, vsel_g,
                               (qtok0 // 128 + qj, 512 + r * 64, 512 + (4 + r) * 64))
            T.barrier()
            T.release(phase_bufs)

        with ExitStack() as es:
            WC_t = es.enter_context(nc.sbuf_tensor("WC%d" % l, [128, 8, 4608], BF16))
            WB_t = es.enter_context(nc.sbuf_tensor("WB%d" % l, [128, 12, D], BF16))
            WO_t = es.enter_context(nc.sbuf_tensor("WO%d" % l, [128, 8, D], BF16))
            PW_t = es.enter_context(nc.sbuf_tensor("PW%d" % l, [128, 4, 128], BF16))
            psT_t = es.enter_context(nc.sbuf_tensor("psT%d" % l, [128, 4], F32))
            Uh_t = es.enter_context(nc.sbuf_tensor("Uh%d" % l, [64, NB * 2 * 512], BF16))
            hTc_t = es.enter_context(nc.sbuf_tensor("hTc%d" % l, [128, 8, GC * 128], BF16))
            aog_t = es.enter_context(nc.sbuf_tensor("aog%d" % l, [128, GC, D], BF16))
            ug_t = es.enter_context(nc.sbuf_tensor("ug%d" % l, [128, GC + 2, 512], BF16))
            plT_t = es.enter_context(nc.sbuf_tensor("plT%d" % l, [128, 4, GC * 128], BF16))
            sig_t = es.enter_context(nc.sbuf_tensor("sig%d" % l, [128, 2, GC * 128], F32))
            sz_t = es.enter_context(nc.sbuf_tensor("sz%d" % l, [128, 2, GC * 128], BF16))
            bT_t = es.enter_context(nc.sbuf_tensor("bT%d" % l, [128, 12, GC * 128], BF16))
            sg_t = es.enter_context(nc.sbuf_tensor("sg%d" % l, [128, 3, GC * 128], BF16))
            acc_t = es.enter_context(nc.sbuf_tensor("acc%d" % l, [128, 2, GC * 128], F32))
            yT_t = es.enter_context(nc.sbuf_tensor("yT%d" % l, [128, 8, GC * 128], BF16))
            gbc_t = es.enter_context(nc.sbuf_tensor("gbc%d" % l, [128, D], F32))
            xc_t = es.enter_context(nc.sbuf_tensor("xc%d" % l, [128, 2, D], F32))
            xo_t = es.enter_context(nc.sbuf_tensor("xo%d" % l, [128, 2, D], F32))
            WC, WB, WO, PW, psT, Uh, hTc, aog, ug, plT = (WC_t.ap(), WB_t.ap(), WO_t.ap(), PW_t.ap(), psT_t.ap(), Uh_t.ap(),
                                                       hTc_t.ap(), aog_t.ap(), ug_t.ap(), plT_t.ap())
            sig, sz, bT, sg, acc, yT, gbc, xc, xo = (sig_t.ap(), sz_t.ap(), bT_t.ap(), sg_t.ap(), acc_t.ap(), yT_t.ap(),
                                                  gbc_t.ap(), xc_t.ap(), xo_t.ap())
            b_WC = [Buf("WC%d" % i) for i in range(9)]
            b_WB = [Buf("WB%d" % i) for i in range(3)]
            b_WO, b_PW, b_psT, b_Uh, b_hTc, b_aog, b_ug, b_plT = [Buf(n) for n in ("WO", "PW", "psT", "Uh", "hTc", "aog", "ug", "plT")]
            b_sig = [Buf("sig0"), Buf("sig1")]
            b_sz = [Buf("sz0"), Buf("sz1")]
            b_bT, b_yT, b_gbc = Buf("bT"), Buf("yT"), Buf("gbc")
            b_sg = [Buf("sg%d" % i) for i in range(3)]
            b_acc = [Buf("acc0"), Buf("acc1")]
            b_xc = [Buf("xc0"), Buf("xc1")]
            b_xo = [Buf("xo0"), Buf("xo1")]
            phase_bufs = b_WC + b_WB + [b_WO, b_PW, b_psT, b_Uh, b_hTc, b_aog, b_ug] + b_xc + b_xo
            wsrc = w_in.ap()[l].rearrange("(k p) n -> p k n", p=128)
            csegs = [(0, 512, 512), (512, 2560, 512), (1024, 3840, 512)] + [(1536 + i * 512, 4352 + i * 512, 512) for i in range(6)]
            for i, (do, so, n_) in enumerate(csegs):
                T.dma(pool, WC[:, :, do:do + n_], wsrc[:, :, so:so + n_], b_WC[i], writes=[b_WC[i]])
            for i in range(3):
                T.dma(pool, WB[:, i * 4:(i + 1) * 4, :], w_br.ap()[l, i].rearrange("(k p) n -> p k n", p=128), b_WB[i], writes=[b_WB[i]])
            T.dma(pool, WO, w_out.ap()[l].rearrange("(k p) n -> p k n", p=128), b_WO, writes=[b_WO])
            T.dma(pool, PW, pool_w.ap()[l].rearrange("g c d -> c g d"), b_PW, writes=[b_PW])
            T.dma(sp, psT, pool_sT.ap()[l], b_psT, writes=[b_psT])
            T.dma(sp, Uh[0:64, :], uall[l].ap(), b_Uh, reads=[d_uall[l]], writes=[b_Uh])
            bctr = [0]

            def nb():
                i = bctr[0] % 8
                bctr[0] += 1
                return i

            def WCall(k):
                return b_WC

            gcl = min(2, LT)
            groups = [(2 + b * LT + g * gcl, gcl, b, g) for b in range(NB) for g in range(LT // gcl)]
            if not last:
                groups = [(0, 2, 4, 0)] + groups
            xdst = yout.ap() if last else x1_d.ap()
            for (t0, gs, m, gi) in groups:
                n = gs * 128
                tk = slice(t0 * 128, t0 * 128 + n)
                T.dma(sp, hTc[:, :, 0:n], hT_d.ap().rearrange("k p t -> p k t")[:, :, tk], b_hTc, reads=[d_hT], writes=[b_hTc])
                T.dma(sp, aog[:, 0:gs, :], ao_d.ap().rearrange("(t p) c -> p t c", p=128)[:, t0:t0 + gs, :], b_aog,
                      reads=[d_ao], writes=[b_aog])
                seq0, seq1 = (0, 2) if m == 4 else (2 + m * LT, 2 + m * LT + LT)
                lo, hi = max(t0 - 1, seq0), min(t0 + gs + 1, seq1)
                T.dma(sp, ug[:, lo - (t0 - 1):hi - (t0 - 1), :], u_d.ap().rearrange("(t p) c -> p t c", p=128)[:, lo:hi, :], b_ug,
                      reads=[d_u], writes=[b_ug])
                for hf in range(2):
                    bi = nb()
                    T.op(pe, lambda: nc.tensor.matmul(PSF[:, bi, :], lhsT=sel[:, m, :], rhs=mr[:, hf * 512:(hf + 1) * 512],
                                                      start=True, stop=True), reads=[b_sel, b_mr], writes=[bank[bi]])
                    T.op(act, lambda: nc.scalar.copy(out=gbc[:, hf * 512:(hf + 1) * 512], in_=PSF[:, bi, :]), reads=[bank[bi]],
                         writes=[b_gbc])
                for g in range(4):
                    bi = nb()
                    for j in range(gs):
                        tile = t0 + j
                        srcs = []
                        if m == 4:
                            srcs.append((ug[:, j + 1, g * 128:(g + 1) * 128], poolA[:, g, 3 + j, :]))
                            if j == 0:
                                srcs.append((ug[:, j + 2, g * 128:(g + 1) * 128], poolA[:, g, 6, :]))
                            else:
                                srcs.append((ug[:, j, g * 128:(g + 1) * 128], poolA[:, g, 5, :]))
                        else:
                            tb = tile - (2 + m * LT)
                            slot = 0 if tb == 0 else (2 if tb == LT - 1 else 1)
                            srcs.append((ug[:, j + 1, g * 128:(g + 1) * 128], poolA[:, g, slot, :]))
                            if tb > 0:
                                srcs.append((ug[:, j, g * 128:(g + 1) * 128], poolA[:, g, 5, :]))
                            else:
                                srcs.append((Uh[0:64, (m * 2 + 1) * 512 + g * 128:(m * 2 + 1) * 512 + (g + 1) * 128],
                                             poolA[0:64, g, 7, :]))
                            if tb < LT - 1:
                                srcs.append((ug[:, j + 2, g * 128:(g + 1) * 128], poolA[:, g, 6, :]))
                            else:
                                srcs.append((Uh[0:64, (m * 2) * 512 + g * 128:(m * 2) * 512 + (g + 1) * 128],
                                             poolA[0:64, g, 8, :]))
                        for si, (lh, rh) in enumerate(srcs):
                            T.op(pe, lambda lh=lh, rh=rh, si=si: nc.tensor.matmul(PSF[:, bi, j * 128:(j + 1) * 128], lhsT=lh, rhs=rh,
                                                                                  start=(si == 0), stop=(si == len(srcs) - 1)),
                                 reads=[b_ug, b_Uh, b_poolA], writes=[bank[bi]], inc=(si == len(srcs) - 1))
                    T.op(act, lambda: nc.scalar.copy(out=plT[:, g, 0:n], in_=PSF[:, bi, 0:n]), reads=[bank[bi]], writes=[b_plT])

                def zgate(zc, slot):
                    bi = nb()
                    for k in range(8):
                        T.op(pe, lambda k=k: nc.tensor.matmul(PSF[:, bi, 0:n], lhsT=WC[:, k, zc * 128:(zc + 1) * 128], rhs=hTc[:, k, 0:n],
                                                              start=(k == 0), stop=(k == 7)),
                             reads=[b_hTc] + b_WC, writes=[bank[bi]], inc=(k == 7))
                    T.op(act, lambda: nc.scalar.activation(out=sig[:, slot, 0:n], in_=PSF[:, bi, 0:n], func=AF.Sigmoid),
                         reads=[bank[bi]], writes=[b_sig[slot]])
                    T.op(dve, lambda: nc.vector.tensor_tensor(out=sz[:, slot, 0:n], in0=PSF[:, bi, 0:n], in1=sig[:, slot, 0:n],
                                                              op=ALU.mult), reads=[bank[bi], b_sig[slot]], writes=[b_sz[slot]])

                zi = 0
                for g in range(4):
                    slot = zi % 2
                    zi += 1
                    zgate(g, slot)
                    bi = nb()
                    T.op(pe, lambda: nc.tensor.matmul(PSF[:, bi, 0:n], lhsT=PW[:, g, :], rhs=plT[:, g, 0:n], start=True, stop=True),
                         reads=[b_PW, b_plT], writes=[bank[bi]])
                    T.op(dve, lambda: nc.vector.scalar_tensor_tensor(out=bT[:, g, 0:n], in0=PSF[:, bi, 0:n], scalar=psT[:, g:g + 1],
                                                                     in1=sz[:, slot, 0:n], op0=ALU.mult, op1=ALU.mult),
                         reads=[bank[bi], b_psT, b_sz[slot]], writes=[b_bT])
                for ch in range(8):
                    slot = zi % 2
                    zi += 1
                    zgate(4 + ch, slot)
                    bi = nb()
                    for j in range(gs):
                        T.op(pe, lambda j=j: nc.tensor.transpose(out=PSB[:, bi, j * 128:(j + 1) * 128],
                                                                 in_=aog[:, j, ch * 128:(ch + 1) * 128], identity=ident),
                             reads=[b_aog, b_ident], writes=[bank[bi]], inc=(j == gs - 1))
                    T.op(dve, lambda: nc.vector.tensor_tensor(out=bT[:, 4 + ch, 0:n], in0=PSB[:, bi, 0:n], in1=sz[:, slot, 0:n],
                                                              op=ALU.mult), reads=[bank[bi], b_sz[slot]], writes=[b_bT])
                for oc in range(8):
                    a = oc % 2
                    for i in range(3):
                        bi = nb()
                        for k in range(8):
                            c0 = 1536 + i * 1024 + oc * 128
                            T.op(pe, lambda k=k: nc.tensor.matmul(PSF[:, bi, 0:n], lhsT=WC[:, k, c0:c0 + 128], rhs=hTc[:, k, 0:n],
                                                                  start=(k == 0), stop=(k == 7)),
                                 reads=[b_hTc] + b_WC, writes=[bank[bi]], inc=(k == 7))
                        T.op(act, lambda: nc.scalar.activation(out=sg[:, i, 0:n], in_=PSF[:, bi, 0:n], func=AF.Sigmoid),
                             reads=[bank[bi]], writes=[b_sg[i]])
                        bi2 = nb()
                        for kk in range(4):
                            T.op(pe, lambda kk=kk: nc.tensor.matmul(PSF[:, bi2, 0:n], lhsT=WB[:, i * 4 + kk, oc * 128:(oc + 1) * 128],
                                                                    rhs=bT[:, i * 4 + kk, 0:n], start=(kk == 0), stop=(kk == 3)),
                                 reads=[b_bT] + b_WB, writes=[bank[bi2]], inc=(kk == 3))
                        if i == 0:
                            T.op(dve, lambda: nc.vector.tensor_tensor(out=acc[:, a, 0:n], in0=PSF[:, bi2, 0:n], in1=sg[:, i, 0:n],
                                                                      op=ALU.mult), reads=[bank[bi2], b_sg[i]], writes=[b_acc[a]])
                        else:
                            T.op(dve, lambda: nc.vector.tensor_tensor(out=sig[:, 0, 0:n], in0=PSF[:, bi2, 0:n], in1=sg[:, i, 0:n],
                                                                      op=ALU.mult), reads=[bank[bi2], b_sg[i]], writes=[b_sig[0]])
                            if i == 1:
                                T.op(pool, lambda: nc.gpsimd.tensor_tensor(out=acc[:, a, 0:n], in0=acc[:, a, 0:n], in1=sig[:, 0, 0:n],
                                                                           op=ALU.add), reads=[b_acc[a], b_sig[0]], writes=[b_acc[a]])
                            else:
                                T.op(pool, lambda: nc.gpsimd.tensor_tensor(out=yT[:, oc, 0:n], in0=acc[:, a, 0:n], in1=sig[:, 0, 0:n],
                                                                           op=ALU.add), reads=[b_acc[a], b_sig[0]], writes=[b_yT])
                for j in range(gs):
                    tile = t0 + j
                    s = tile % 2
                    T.dma(sp, xc[:, s, :], xsrc[tile * 128:(tile + 1) * 128, :], b_xc[s], reads=[d_xsrc] if d_xsrc else [],
                          writes=[b_xc[s]])
                    for hf in range(2):
                        bi = nb()
                        for k in range(8):
                            T.op(pe, lambda k=k: nc.tensor.matmul(PSF[:, bi, :], lhsT=yT[:, k, j * 128:(j + 1) * 128],
                                                                  rhs=WO[:, k, hf * 512:(hf + 1) * 512], start=(k == 0), stop=(k == 7)),
                                 reads=[b_yT, b_WO], writes=[bank[bi]], inc=(k == 7))
                        T.op(dve, lambda: nc.vector.tensor_tensor(out=xo[:, s, hf * 512:(hf + 1) * 512], in0=PSF[:, bi, :],
                                                                  in1=gbc[:, hf * 512:(hf + 1) * 512], op=ALU.mult),
                             reads=[bank[bi], b_gbc], writes=[b_xo[s]])
                    T.op(pool, lambda: nc.gpsimd.tensor_tensor(out=xo[:, s, :], in0=xo[:, s, :], in1=xc[:, s, :], op=ALU.add),
                         reads=[b_xo[s], b_xc[s]], writes=[b_xo[s]])
                    if last:
                        orow = (tile - 2) * 128
                        T.dma(sp, xdst[orow:orow + 128, :], xo[:, s, :], b_xo[s], reads=[b_xo[s]], writes=[])
                    else:
                        T.dma(sp, xdst[tile * 128:(tile + 1) * 128, :], xo[:, s, :], b_xo[s], reads=[b_xo[s]], writes=[d_x1])
            T.barrier()
            T.release(phase_bufs)
    T.barrier()
    return nc


def _pool_mat(in_pos, out_pos, w, n):
    left = w // 2
    right = w - 1 - left
    A = np.zeros((len(in_pos), len(out_pos)), np.float32)
    for oi, t in enumerate(out_pos):
        if t < 0 or t >= n:
            continue
        lo, hi = max(t - left, 0), min(t + right + 1, n)
        for ii, s in enumerate(in_pos):
            if lo <= s < hi:
                A[ii, oi] += 1.0 / (hi - lo)
            if s == t:
                A[ii, oi] -= 1.0
    return A


def _host_consts(c, LT, seq):
    NT = 2 + NB * LT
    inv = np.power(np.float32(10000.0), -np.arange(0, 32, 2, dtype=np.float32) / np.float32(32)).astype(np.float32)
    rope = np.zeros((NT * 128, 128), np.float32)
    rope[:, 0:64] = 1.0
    pos = c * LT * 128 + np.arange(LT * 128)
    row = (pos // GRID_W).astype(np.float32)
    col = (pos % GRID_W).astype(np.float32)
    ar = (row[:, None] * inv[None, :]).astype(np.float32)
    ac = (col[:, None] * inv[None, :]).astype(np.float32)
    cc = np.zeros((LT * 128, 2, 2, 16), np.float32)
    ssn = np.zeros((LT * 128, 2, 2, 16), np.float32)
    for a, ang in enumerate((ar, ac)):
        cc[:, a, 0] = np.cos(ang)
        cc[:, a, 1] = np.cos(ang)
        ssn[:, a, 0] = -np.sin(ang)
        ssn[:, a, 1] = np.sin(ang)
    for b in range(NB):
        r0 = (2 + b * LT) * 128
        rope[r0:r0 + LT * 128, 0:64] = cc.reshape(-1, 64)
        rope[r0:r0 + LT * 128, 64:128] = ssn.reshape(-1, 64)
    pa = np.zeros((128, 4, 9, 128), np.float32)
    ar128 = np.arange(128)
    g0 = c * LT * 128
    for g, w in enumerate(WIN):
        pa[:, g, 0] = _pool_mat(g0 + ar128, g0 + ar128, w, seq)
        mid = g0 + 128 if LT > 2 else seq // 2 // 128 * 128
        pa[:, g, 1] = _pool_mat(mid + ar128, mid + ar128, w, seq)
        gl = g0 + (LT - 1) * 128
        pa[:, g, 2] = _pool_mat(gl + ar128, gl + ar128, w, seq)
        pa[:, g, 3] = _pool_mat(ar128, ar128, w, CTX)
        pa[:, g, 4] = _pool_mat(128 + ar128, 128 + ar128, w, CTX)
        pa[:, g, 5] = _pool_mat(ar128, 128 + ar128, w, 1 << 30)
        pa[:, g, 6] = _pool_mat(128 + ar128, ar128, w, 1 << 30)
        hp_in = np.concatenate([(r + 1) * LT * 128 - 8 + np.arange(8) for r in range(NCORE)])
        hp_in = np.where(np.repeat(np.arange(NCORE), 8) == c - 1, hp_in, -10 ** 6)
        pa[0:64, g, 7] = _pool_mat(hp_in, g0 + ar128, w, seq)
        hn_in = np.concatenate([r * LT * 128 + np.arange(8) for r in range(NCORE)])
        hn_in = np.where(np.repeat(np.arange(NCORE), 8) == c + 1, hn_in, -10 ** 6)
        pa[0:64, g, 8] = _pool_mat(hn_in, g0 + (LT - 1) * 128 + ar128, w, seq)
    return rope, pa


_NC_CACHE = {}


def kernel(x, c, ctx, c_ctx, ada_w, ada_b, norm_g, w_in, pool_w, pool_scale, diff_q_norm, diff_k_norm,
           diff_lambda, diff_subln, gqa_q_norm, gqa_k_norm, w_branch, w_out):
    f = lambda a: np.ascontiguousarray(np.asarray(a, dtype=np.float32))
    x, c, ctx, c_ctx = f(x), f(c), f(ctx), f(c_ctx)
    seq = x.shape[1]
    LT = seq // (NCORE * 128)
    if LT not in _NC_CACHE:
        _NC_CACHE[LT] = build(LT)
    nc = _NC_CACHE[LT]
    cmat = np.concatenate([c, c_ctx[None, :]], 0)
    shared = {
        "cT": f(cmat.reshape(5, 8, 128).transpose(2, 1, 0)),
        "ident": np.eye(128, dtype=np.float32),
        "sel": f(np.eye(5, dtype=np.float32)[:, :, None] * np.ones((1, 1, 128), np.float32)),
        "ada_w": f(ada_w),
        "ada_bT": f(np.asarray(ada_b).reshape(DEPTH, 24, 128).transpose(0, 2, 1)),
        "ada_b": f(ada_b),
        "norm_gT": f(np.asarray(norm_g).reshape(DEPTH, 8, 128).transpose(0, 2, 1)),
        "w_in": f(w_in),
        "pool_w": f(pool_w),
        "pool_sT": f(np.asarray(pool_scale).reshape(DEPTH, 4, 128).transpose(0, 2, 1)),
        "qkg": f(np.concatenate([np.asarray(diff_q_norm), np.asarray(diff_k_norm), np.asarray(gqa_q_norm),
                                 np.asarray(gqa_k_norm)], 1)),
        "dlam": f(np.asarray(diff_lambda).reshape(DEPTH, 256)),
        "subln": f(diff_subln),
        "w_branch": f(w_branch),
        "w_out": f(w_out),
    }
    in_maps = []
    for ci in range(NCORE):
        rope, pa = _host_consts(ci, LT, seq)
        xi = np.concatenate([ctx[ci // 2]] + [x[b, ci * LT * 128:(ci + 1) * LT * 128] for b in range(NB)], 0)
        d = dict(shared)
        d.update({"xin": f(xi), "rope": rope, "poolA": pa})
        in_maps.append(d)
    res = run_bass_kernel_spmd(nc, in_maps, core_ids=list(range(NCORE)))
    out = np.zeros((NB, seq, D), np.float32)
    for ci in range(NCORE):
        yy = np.asarray(res.results[ci]["y"]).reshape(NB, LT * 128, D)
        out[:, ci * LT * 128:(ci + 1) * LT * 128, :] = yy
    return out
```
